# Optimizing a Trainium2 kernel written in Bass

```python
import jax, jax.numpy as jnp
from jax import lax
import numpy as np

D_MODEL = 4096
BATCH = 2
SEQ = 4096
DEPTH = 1

D_MIX = D_MODEL
ATT_HEADS = 16
HEAD_DIM = 128
D_ATT = ATT_HEADS * HEAD_DIM
IDX_HEADS = 32
IDX_DIM = 64
TOPK_MAX = 256
GMLP_GROUPS = 16
GMLP_CH = 128
D_GMLP = GMLP_GROUPS * GMLP_CH
CHUNK = 128
D_FF = -(-8 * D_MODEL // (3 * 256)) * 256
QBLOCK = 64
EPS = 1e-6
NEG = -1e30

SPLIT_SIZES = (D_ATT, D_ATT, D_ATT, IDX_HEADS * IDX_DIM, IDX_DIM, IDX_HEADS, D_GMLP, D_GMLP)
D_IN = sum(SPLIT_SIZES)

kernel_name = "hymba_dsa_gmlp_hybrid_block"


def rms_norm(x, g):
    xf = x.astype(jnp.float32)
    y = xf * lax.rsqrt(jnp.mean(xf * xf, axis=-1, keepdims=True) + EPS)
    return (y * g.astype(jnp.float32)).astype(x.dtype)


def dsa_sparse_attention(q, k, v, q_idx, k_idx, w_idx):
    B, L = q.shape[0], q.shape[1]
    topk = min(TOPK_MAX, L // 4)
    nblk = L // QBLOCK
    key_pos = jnp.arange(L)
    q_pos = jnp.arange(L).reshape(nblk, QBLOCK)
    k_idx_f = k_idx.astype(jnp.float32)

    def to_blocks(a):
        return a.reshape((B, nblk, QBLOCK) + a.shape[2:]).swapaxes(0, 1)

    def one_block(args):
        qb, qib, wb, pos = args
        logits = jnp.einsum('bqhd,bsd->bqhs', qib.astype(jnp.float32), k_idx_f) * (IDX_DIM ** -0.5)
        score = jnp.einsum('bqh,bqhs->bqs', wb.astype(jnp.float32), jax.nn.relu(logits))
        causal = key_pos[None, :] <= pos[:, None]
        score = jnp.where(causal[None], score, NEG)
        _, sel = lax.top_k(score, topk)
        valid = sel <= pos[None, :, None]
        k_sel = jax.vmap(lambda kk, idx: kk[idx])(k, sel)
        v_sel = jax.vmap(lambda vv, idx: vv[idx])(v, sel)
        s = jnp.einsum('bqhd,bqkhd->bqhk', qb.astype(jnp.float32), k_sel.astype(jnp.float32)) * (HEAD_DIM ** -0.5)
        s = jnp.where(valid[:, :, None, :], s, NEG)
        p = jax.nn.softmax(s, axis=-1)
        o = jnp.einsum('bqhk,bqkhd->bqhd', p.astype(v.dtype), v_sel)
        return o.reshape(B, QBLOCK, D_ATT)

    out = lax.map(one_block, (to_blocks(q), to_blocks(q_idx), to_blocks(w_idx), q_pos))
    return out.swapaxes(0, 1).reshape(B, L, D_ATT)


def chunked_spatial_gating(u, v, v_gain, w_s, b_s):
    B, L = u.shape[0], u.shape[1]
    u = jax.nn.gelu(u)
    v = jax.nn.gelu(v).reshape(B, L, GMLP_GROUPS, GMLP_CH)
    vf = v.astype(jnp.float32)
    mu = jnp.mean(vf, axis=-1, keepdims=True)
    var = jnp.mean(jnp.square(vf - mu), axis=-1, keepdims=True)
    v = ((vf - mu) * lax.rsqrt(var + EPS) * v_gain.astype(jnp.float32)).astype(u.dtype)
    v = v.reshape(B, L // CHUNK, CHUNK, GMLP_GROUPS, GMLP_CH)
    causal = jnp.tril(jnp.ones((CHUNK, CHUNK), dtype=w_s.dtype))
    w = w_s * causal[None]
    z = jnp.einsum('gts,bnsgc->bntgc', w, v) + b_s.T[None, None, :, :, None]
    return u * z.reshape(B, L, D_GMLP)


def swiglu(h, w_gate, w_up, w_down):
    return (jax.nn.silu(h @ w_gate) * (h @ w_up)) @ w_down


def setup_inputs(seed: int = 0) -> dict:
    key = jax.random.key(seed)
    ks = jax.random.split(key, 14)
    f32 = jnp.float32
    n = lambda k, shape, scale: jax.random.normal(k, shape, f32) * scale
    return {
        "x": jax.random.normal(ks[0], (BATCH, SEQ, D_MODEL), f32),
        "norm_mix": 1.0 + n(ks[1], (DEPTH, D_MODEL), 0.02),
        "w_in": n(ks[2], (DEPTH, D_MODEL, D_IN), D_MODEL ** -0.5),
        "gmlp_v_gain": 1.0 + n(ks[3], (DEPTH, GMLP_GROUPS, GMLP_CH), 0.02),
        "w_spatial": n(ks[4], (DEPTH, GMLP_GROUPS, CHUNK, CHUNK), CHUNK ** -0.5),
        "b_spatial": 1.0 + n(ks[5], (DEPTH, GMLP_GROUPS, CHUNK), 0.02),
        "w_out": n(ks[6], (DEPTH, D_MIX, D_MODEL), D_MIX ** -0.5),
        "norm_ffn": 1.0 + n(ks[7], (DEPTH, D_MODEL), 0.02),
        "w_gate": n(ks[8], (DEPTH, D_MODEL, D_FF), D_MODEL ** -0.5),
        "w_up": n(ks[9], (DEPTH, D_MODEL, D_FF), D_MODEL ** -0.5),
        "w_down": n(ks[10], (DEPTH, D_FF, D_MODEL), D_FF ** -0.5),
        "norm_final": 1.0 + n(ks[11], (D_MODEL,), 0.02),
    }


def reference(x, norm_mix, w_in, gmlp_v_gain, w_spatial, b_spatial, w_out,
              norm_ffn, w_gate, w_up, w_down, norm_final):
    B, L = x.shape[0], x.shape[1]
    offsets = list(np.cumsum(SPLIT_SIZES)[:-1])
    for i in range(DEPTH):
        h = rms_norm(x, norm_mix[i])
        proj = h @ w_in[i]
        q, k, v, q_idx, k_idx, w_idx, u_g, v_g = jnp.split(proj, offsets, axis=-1)
        q = q.reshape(B, L, ATT_HEADS, HEAD_DIM)
        k = k.reshape(B, L, ATT_HEADS, HEAD_DIM)
        v = v.reshape(B, L, ATT_HEADS, HEAD_DIM)
        q_idx = q_idx.reshape(B, L, IDX_HEADS, IDX_DIM)
        w_idx = w_idx * (IDX_HEADS ** -0.5)
        att = dsa_sparse_attention(q, k, v, q_idx, k_idx, w_idx)
        gm = chunked_spatial_gating(u_g, v_g, gmlp_v_gain[i], w_spatial[i], b_spatial[i])
        mix = jnp.concatenate([att, gm], axis=-1)
        x = x + mix @ w_out[i]
        h = rms_norm(x, norm_ffn[i])
        x = x + swiglu(h, w_gate[i], w_up[i], w_down[i])
    return rms_norm(x, norm_final)
```

```python
import os
from contextlib import ExitStack
import numpy as np
import concourse.bass as bass
import concourse.mybir as mybir
from concourse.bass_utils import run_bass_kernel_spmd

F32 = mybir.dt.float32
BF16 = mybir.dt.bfloat16
AF = mybir.ActivationFunctionType
ALU = mybir.AluOpType
AX = mybir.AxisListType

D = 4096
L = 4096
DFF = 11008
NFC = 86
NT = 8
EPS = 1e-6
NEG = -1e30
STAGE = int(os.environ.get("K_STAGE", "99"))
DEBUG = int(os.environ.get("K_DEBUG", "0"))


class Eng:
    def __init__(self, name, e, sem):
        self.name, self.e, self.sem = name, e, sem
        self.cnt = 0
        self.waited = {}

    def wait(self, tok):
        if tok is None:
            return
        key, sem, val = tok
        if self.waited.get(key, 0) >= val:
            return
        self.e.wait_ge(sem, val)
        self.waited[key] = val

    def sig(self, ins):
        self.cnt += 1
        ins.then_inc(self.sem, 1)
        return (self.name, self.sem, self.cnt)


class T:
    _n = 0

    def __init__(self, ap, dsem=None):
        self.ap = ap
        self.w = None
        self.r = {}
        self.dsem = dsem
        self.dcnt = 0
        T._n += 1
        self.name = "t%d" % T._n

    def __getitem__(self, idx):
        return self.ap[idx]


class Tk:
    def __init__(self, nc, es):
        self.nc = nc
        self.es = es
        self.E = {}
        for name, e in (("pe", nc.tensor), ("act", nc.scalar), ("dve", nc.vector),
                        ("pool", nc.gpsimd), ("sp", nc.sync)):
            sem = es.enter_context(nc.semaphore("s_" + name))
            self.E[name] = Eng(name, e, sem)
        self.dma_latest = {}
        self.free_dsems = {"sp": [], "pool": []}
        self.nsem = 0

    def dsem(self, kind):
        if self.free_dsems[kind]:
            return self.free_dsems[kind].pop()
        self.nsem += 1
        return [self.es.enter_context(self.nc.semaphore("d%d" % self.nsem)), 0, "d%d" % self.nsem]

    def _pre(self, E, en, R, W):
        for t in R:
            E.wait(t.w)
        for t in W:
            E.wait(t.w)
            for k, tok in t.r.items():
                E.wait(tok)

    def op(self, en, fn, R=(), W=()):
        E = self.E[en]
        self._pre(E, en, R, W)
        ins = fn(E.e)
        tok = E.sig(ins)
        for t in R:
            t.r[en] = tok
        for t in W:
            t.w = tok
            t.r = {}
        return tok

    def mm(self, out_t, mms, cont=False):
        E = self.E["pe"]
        if not cont:
            self._pre(E, "pe", (), (out_t,))
        allR = []
        ins = None
        for (o, l, r, st, sp, R) in mms:
            for t in R:
                E.wait(t.w)
                allR.append(t)
            ins = E.e.matmul(o, lhsT=l, rhs=r, start=st, stop=sp)
        tok = E.sig(ins)
        for t in allR:
            t.r["pe"] = tok
        out_t.w = tok
        if not cont:
            out_t.r = {}
        return tok

    def tr(self, out_t, trs):
        E = self.E["pe"]
        self._pre(E, "pe", (), (out_t,))
        allR = []
        ins = None
        for (o, i, idn, R) in trs:
            for t in R:
                E.wait(t.w)
                allR.append(t)
            ins = E.e.transpose(o, i, idn)
        tok = E.sig(ins)
        for t in allR:
            t.r["pe"] = tok
        out_t.w = tok
        out_t.r = {}
        return tok

    def dma(self, qn, out_ap, in_ap, R=(), W=(), owner=None):
        Q = self.E[qn]
        self._pre(Q, "dma", R, W)
        ins = Q.e.dma_start(out=out_ap, in_=in_ap)
        if owner.dsem is None:
            owner.dsem = self.dsem(qn)
            owner.dkind = qn
            owner.pool.ds.append((qn, owner.dsem))
        assert owner.dkind == qn, "tile DMA semaphore is bound to one queue kind"
        ds = owner.dsem
        ds[1] += 16
        ins.then_inc(ds[0], 16)
        tok = (ds[2], ds[0], ds[1])
        for t in R:
            t.r["dma" + ds[2]] = tok
        for t in W:
            t.w = tok
            t.r = {}
        self.dma_latest[ds[2]] = tok
        return tok

    def barrier(self):
        sp = self.E["sp"]
        for tok in self.dma_latest.values():
            sp.wait(tok)
        toks = []
        for en, E in self.E.items():
            if en == "sp":
                continue
            if E.cnt > 0:
                E.e.wait_ge(E.sem, E.cnt)
            ins = E.e.sem_inc(E.sem, 1)
            E.cnt += 1
            toks.append((E.name, E.sem, E.cnt))
        for tok in toks:
            sp.wait(tok)
        if sp.cnt > 0:
            sp.e.wait_ge(sp.sem, sp.cnt)
        ins = sp.e.sem_inc(sp.sem, 1)
        sp.cnt += 1
        stok = (sp.name, sp.sem, sp.cnt)
        for en, E in self.E.items():
            if en != "sp":
                E.wait(stok)


class Pool_:
    cnt = 0

    def __init__(self, tk, nc):
        self.tk, self.nc = tk, nc
        self.es = ExitStack()
        self.ds = []
        self.n = 0

    def sb(self, shape, dt, dma=False, name=None):
        Pool_.cnt += 1
        h = self.es.enter_context(self.nc.sbuf_tensor("sb%d" % Pool_.cnt, shape, dt))
        t = T(h[:], None)
        t.pool = self
        return t

    def ps(self, shape, dt):
        Pool_.cnt += 1
        h = self.es.enter_context(self.nc.psum_tensor("ps%d" % Pool_.cnt, shape, dt))
        return T(h[:])

    def close(self):
        self.tk.barrier()
        for kind, ds in self.ds:
            self.tk.free_dsems[kind].append(ds)
        self.es.close()


class Ring:
    def __init__(self, tiles):
        self.tiles = tiles
        self.i = 0

    def next(self):
        t = self.tiles[self.i % len(self.tiles)]
        self.i += 1
        return t


def build_program():
    nc = bass.Bass("TRN2", target_bir_lowering=False)
    dk = "ExternalOutput" if DEBUG else "Internal"

    def din(name, shape, dt=F32):
        return nc.dram_tensor(name, shape, dt, kind="ExternalInput").ap()

    x_all = din("x_all", [L, D])
    x_own = din("x_own", [NT * 128, D])
    negmask_in = din("negmask", [NT, 128, 512])
    ident_in = din("ident", [128, 128])
    tril_in = din("tril_st", [128, 128])
    gmix_in = din("gmix_b", [128, D])
    gffn_in = din("gffn_b", [128, D])
    gfin_in = din("gfin_b", [128, D])
    vgain_in = din("vgain_b", [128, 2048])
    bsb_in = din("bsb", [128, 16 * 128])
    wsT_in = din("wsT", [128, 16 * 128])
    wfm_kv = din("wfm_kv", [17, 128, 32 * 128])
    wfm_own = din("wfm_own", [48, 128, 32 * 128])
    wtm_v = din("wtm_v", [4, 4, 128, 8 * 512])
    wtm_vg = din("wtm_vg", [4, 4, 128, 8 * 512])
    wtm_wi = din("wtm_wi", [128, 32 * 32])
    wtm_out = din("wtm_out", [8, 4, 128, 8 * 512])
    wfm_g = din("wfm_g", [NFC, 128, 32 * 128])
    wfm_u = din("wfm_u", [NFC, 128, 32 * 128])
    wd_in = din("wd", [32, 2, 128, 43 * 128])
    out = nc.dram_tensor("out", [NT * 128, D], F32, kind="ExternalOutput").ap()

    KTs = nc.dram_tensor("KTs", [16, 128, L], BF16, kind=dk).ap()
    kidxTs = nc.dram_tensor("kidxTs", [128, L], BF16, kind=dk).ap()
    Vs = nc.dram_tensor("Vs", [16, 128, 32, 128], BF16, kind=dk).ap()
    OWN = nc.dram_tensor("OWNs", [48, 128, NT * 128], BF16, kind=dk).ap()
    x1s = nc.dram_tensor("x1s", [NT * 128, D], F32, kind=dk).ap()
    if DEBUG:
        mixdbg = nc.dram_tensor("mixdbg", [128, 32, NT * 128], BF16, kind="ExternalOutput").ap()
        scdbg = nc.dram_tensor("scdbg", [NT, 128, L], F32, kind="ExternalOutput").ap()
        nmdbg = nc.dram_tensor("nmdbg", [NT, 128, L], BF16, kind="ExternalOutput").ap()

    with ExitStack() as ges:
        tk = Tk(nc, ges)
        G = Pool_(tk, nc)
        identf = G.sb([128, 128], F32, dma=True)
        identb = G.sb([128, 128], BF16, dma=True)
        onesb = G.sb([128, 128], BF16)
        kmax = G.sb([1, 1], F32)
        qmax = G.sb([1, 1], F32)
        negb = G.sb([128, 1], F32)
        tmp1 = G.sb([1, 1], F32)
        epsT = G.sb([128, 1], F32)

        tk.dma("sp", identf[:], ident_in, W=(identf,), owner=identf)
        tk.dma("pool", identb[:], ident_in, W=(identb,), owner=identb)
        tk.op("dve", lambda e: e.memset(onesb[:], 1.0), W=(onesb,))
        tk.op("dve", lambda e: e.memset(kmax[:], 0.0), W=(kmax,))
        tk.op("dve", lambda e: e.memset(qmax[:], 0.0), W=(qmax,))
        tk.op("dve", lambda e: e.memset(epsT[:], EPS), W=(epsT,))

        def rmsnorm_T(x_src_ap, gb, xs_ring, h1_ring, junk, ptr_ring, hT, tcol, ss_ring):
            xs = xs_ring.next()
            tk.dma("sp", xs[:], x_src_ap, W=(xs,), owner=xs)
            ss = ss_ring.next()
            tk.op("act", lambda e: e.activation(out=junk[:], in_=xs[:], func=AF.Square, scale=1.0 / 64.0,
                                                 accum_out=ss[:, 0:1]), R=(xs,), W=(junk, ss))
            tk.op("act", lambda e: e.activation(out=ss[:, 1:2], in_=ss[:, 0:1], func=AF.Sqrt, bias=epsT[:, 0:1]),
                  R=(ss, epsT), W=(ss,))
            tk.op("dve", lambda e: e.reciprocal(out=ss[:, 2:3], in_=ss[:, 1:2]), R=(ss,), W=(ss,))
            h1 = h1_ring.next()
            tk.op("dve", lambda e: e.scalar_tensor_tensor(out=h1[:], in0=xs[:], scalar=ss[:, 2:3], in1=gb[:],
                                                            op0=ALU.mult, op1=ALU.mult), R=(xs, ss, gb), W=(h1,))
            for q in range(4):
                pt = ptr_ring.next()
                tk.tr(pt, [(pt[:, j * 128:(j + 1) * 128], h1[:, (q * 8 + j) * 128:(q * 8 + j + 1) * 128], identb[:],
                            (h1, identb)) for j in range(8)])
                if q % 2 == 0:
                    tk.op("act", lambda e: e.copy(out=hT[:, q * 8:(q + 1) * 8, tcol:tcol + 128],
                                                   in_=pt[:].rearrange("p (j t) -> p j t", j=8)), R=(pt,), W=(hT,))
                else:
                    tk.op("dve", lambda e: e.tensor_copy(out=hT[:, q * 8:(q + 1) * 8, tcol:tcol + 128],
                                                          in_=pt[:].rearrange("p (j t) -> p j t", j=8)), R=(pt,), W=(hT,))

        def sqnorm_a(src_t, sq_ring):
            sq_t = sq_ring.next()
            tk.op("act", lambda e: e.activation(out=sq_t[:, 0:1024], in_=src_t[:, 0:1024], func=AF.Square),
                  R=(src_t,), W=(sq_t,))
            tk.op("dve", lambda e: e.tensor_tensor(out=sq_t[:, 0:512], in0=sq_t[:, 0:512], in1=sq_t[:, 512:1024],
                                                    op=ALU.add), R=(sq_t,), W=(sq_t,))
            return sq_t

        def sqnorm_b(sq_t, nps, run_max):
            tk.mm(nps, [(nps[0:1, 0:512], onesb[:, 0:1], sq_t[:, 0:512], True, True, (sq_t, onesb))])
            tk.op("dve", lambda e: e.tensor_reduce(out=tmp1[:], in_=nps[0:1, 0:512], axis=AX.X, op=ALU.max),
                  R=(nps,), W=(tmp1,))
            tk.op("dve", lambda e: e.tensor_tensor(out=run_max[:], in0=run_max[:], in1=tmp1[:], op=ALU.max),
                  R=(tmp1, run_max), W=(run_max,))

        def fm_chunk(ws, hT, ntok, acc_ring, evac):
            for hh in range(ntok // 512):
                acc = acc_ring.next()
                tk.mm(acc, [(acc[:], ws[:, kc * 128:(kc + 1) * 128], hT[:, kc, hh * 512:(hh + 1) * 512],
                             kc == 0, kc == 31, (ws, hT)) for kc in range(32)])
                evac(acc, hh)

        def tm_tile(pcs, hT, j, acc):
            tk.mm(acc, [(acc[:], hT[:, kc, j * 128:(j + 1) * 128],
                         pcs[kc // 8][:, (kc % 8) * 512:(kc % 8 + 1) * 512],
                         kc == 0, kc == 31, (pcs[kc // 8], hT)) for kc in range(32)])

        P = Pool_(tk, nc)
        if STAGE >= 1:
            gb = P.sb([128, D], F32, dma=True)
            tk.dma("sp", gb[:], gmix_in, W=(gb,), owner=gb)
            hT = P.sb([128, 32, 1024], BF16)
            xs_ring = Ring([P.sb([128, D], F32, dma=True) for _ in range(2)])
            h1_ring = Ring([P.sb([128, D], BF16) for _ in range(2)])
            junk = P.sb([128, D], BF16)
            ss_ring = Ring([P.sb([128, 4], F32) for _ in range(4)])
            wslots = Ring([P.sb([128, 4096], BF16, dma=True) for _ in range(6)])
            kst_ring = Ring([P.sb([128, 1024], BF16, dma=True) for _ in range(2)])
            vst_ring = Ring([P.sb([128, 512], BF16, dma=True) for _ in range(3)])
            sq_ring = Ring([P.sb([128, 1024], BF16) for _ in range(2)])
            ptr_ring = Ring([P.ps([128, 1024], BF16) for _ in range(2)])
            acc_ring = Ring([P.ps([128, 512], F32) for _ in range(5)])
            nps = P.ps([128, 512], F32)
            pend_sq = None

            MASK = int(os.environ.get("K_P1MASK", "15"))
            NTB = int(os.environ.get("K_NTB", "4"))
            for tb in range(NTB):
                for j in range(8):
                    t0 = tb * 1024 + j * 128
                    rmsnorm_T(x_all[t0:t0 + 128, :], gb, xs_ring, h1_ring, junk, ptr_ring, hT, j * 128, ss_ring)
                for c in range(17 if MASK & 2 else 0):
                    ws = wslots.next()
                    tk.dma("pool", ws[:], wfm_kv[c], W=(ws,), owner=ws)
                    kst = kst_ring.next()
                    fm_chunk(ws, hT, 1024, acc_ring,
                             lambda acc, hh: tk.op("act", lambda e: e.copy(out=kst[:, hh * 512:(hh + 1) * 512], in_=acc[:]),
                                                   R=(acc,), W=(kst,)))
                    if pend_sq is not None:
                        sqnorm_b(pend_sq, nps, kmax)
                        pend_sq = None
                    if c < 16:
                        tk.dma("sp", KTs[c, :, tb * 1024:(tb + 1) * 1024], kst[:], R=(kst,), owner=kst)
                        pend_sq = sqnorm_a(kst, sq_ring)
                    else:
                        tk.dma("sp", kidxTs[:, tb * 1024:(tb + 1) * 1024], kst[:], R=(kst,), owner=kst)
                for cc in range(4 if MASK & 8 else 0):
                    pcs = []
                    for q in range(4):
                        ws = wslots.next()
                        tk.dma("pool", ws[:], wtm_v[cc, q], W=(ws,), owner=ws)
                        pcs.append(ws)
                    for j in range(8):
                        acc = acc_ring.next()
                        tm_tile(pcs, hT, j, acc)
                        vst = vst_ring.next()
                        tk.op("dve", lambda e: e.tensor_copy(out=vst[:], in_=acc[:]), R=(acc,), W=(vst,))
                        blk = tb * 8 + j
                        tk.dma("sp", Vs[cc * 4:(cc + 1) * 4, :, blk, :].rearrange("h p d -> p h d"),
                               vst[:].rearrange("p (h d) -> p h d", h=4), R=(vst,), owner=vst)
        P.close()

        M1 = Pool_(tk, nc)
        mix_gm = M1.sb([128, 16, NT * 128], BF16)
        wi = M1.sb([128, NT, 32], F32)

        P = Pool_(tk, nc)
        if STAGE >= 2:
            hT = P.sb([128, 32, 1024], BF16)
            acc_ring = Ring([P.ps([128, 512], F32) for _ in range(5)])
            nps = P.ps([128, 512], F32)
            WgT = P.sb([128, 2048], BF16)
            P2a = Pool_(tk, nc)
            gb = P2a.sb([128, D], F32, dma=True)
            tk.dma("sp", gb[:], gmix_in, W=(gb,), owner=gb)
            xs_ring = Ring([P2a.sb([128, D], F32, dma=True) for _ in range(1)])
            h1_ring = Ring([P2a.sb([128, D], BF16) for _ in range(1)])
            junk = P2a.sb([128, D], BF16)
            ss_ring = Ring([P2a.sb([128, 4], F32) for _ in range(4)])
            ptr_ring = Ring([P2a.ps([128, 1024], BF16) for _ in range(2)])
            wsT = xs_ring.tiles[0]
            tril = P2a.sb([128, 128], F32, dma=True)
            tk.dma("sp", tril[:], tril_in, W=(tril,), owner=tril)
            tk.dma("sp", wsT[:, 0:2048], wsT_in, W=(wsT,), owner=wsT)
            for g in range(16):
                tk.op("dve", lambda e: e.tensor_tensor(out=WgT[:, g * 128:(g + 1) * 128],
                                                        in0=wsT[:, g * 128:(g + 1) * 128], in1=tril[:], op=ALU.mult),
                      R=(wsT, tril), W=(WgT,))
            for j in range(NT):
                rmsnorm_T(x_own[j * 128:(j + 1) * 128, :], gb, xs_ring, h1_ring, junk, ptr_ring, hT, j * 128, ss_ring)
            P2a.close()
            wslots = Ring([P.sb([128, 4096], BF16, dma=True) for _ in range(6)])
            vgain = P.sb([128, 2048], F32, dma=True)
            bsb = P.sb([128, 2048], F32, dma=True)
            tk.dma("sp", vgain[:], vgain_in, W=(vgain,), owner=vgain)
            tk.dma("sp", bsb[:], bsb_in, W=(bsb,), owner=bsb)
            kst_ring = Ring([P.sb([128, 1024], BF16, dma=True) for _ in range(2)])
            sq_ring = Ring([P.sb([128, 1024], BF16) for _ in range(2)])
            pend_sq = None
            for c in range(48):
                ws = wslots.next()
                tk.dma("pool", ws[:], wfm_own[c], W=(ws,), owner=ws)
                kst = kst_ring.next()
                if c < 16:
                    fm_chunk(ws, hT, 1024, acc_ring,
                             lambda acc, hh: tk.op("act", lambda e: e.activation(out=kst[:, hh * 512:(hh + 1) * 512], in_=acc[:],
                                                                                 func=AF.Gelu_apprx_tanh), R=(acc,), W=(kst,)))
                else:
                    fm_chunk(ws, hT, 1024, acc_ring,
                             lambda acc, hh: tk.op("act", lambda e: e.copy(out=kst[:, hh * 512:(hh + 1) * 512], in_=acc[:]),
                                                   R=(acc,), W=(kst,)))
                tk.dma("sp", OWN[c], kst[:], R=(kst,), owner=kst)
                if pend_sq is not None:
                    sqnorm_b(pend_sq, nps, qmax)
                    pend_sq = None
                if 16 <= c < 32:
                    pend_sq = sqnorm_a(kst, sq_ring)
            wsw = wslots.next()
            tk.dma("pool", wsw[:, 0:1024], wtm_wi, W=(wsw,), owner=wsw)
            for j in range(NT):
                acc = acc_ring.next()
                tk.mm(acc, [(acc[:, 0:32], hT[:, kc, j * 128:(j + 1) * 128], wsw[:, kc * 32:(kc + 1) * 32],
                             kc == 0, kc == 31, (wsw, hT)) for kc in range(32)])
                tk.op("act", lambda e: e.mul(out=wi[:, j, :], in_=acc[:, 0:32], mul=float(32 ** -0.5 * 64 ** -0.5)),
                      R=(acc,), W=(wi,))
            for kt_ in kst_ring.tiles:
                for tok in list(kt_.r.values()):
                    tk.E["sp"].wait(tok)
            gl_ring = Ring([P.sb([128, 512], F32) for _ in range(2)])
            sq2_ring = Ring([P.sb([128, 512], F32) for _ in range(2)])
            st_ring = Ring([P.sb([128, 16], F32) for _ in range(2)])
            vn_ring = Ring([P.sb([128, 512], BF16) for _ in range(2)])
            zt_ring = Ring([P.sb([128, 512], F32) for _ in range(2)])
            gu_ring = Ring([P.sb([128, 4, 128], BF16, dma=True) for _ in range(3)])
            for cc in range(4):
                pcs = []
                for q in range(4):
                    ws = wslots.next()
                    tk.dma("pool", ws[:], wtm_vg[cc, q], W=(ws,), owner=ws)
                    pcs.append(ws)
                for j in range(NT):
                    gu = gu_ring.next()
                    tk.dma("sp", gu[:], OWN[cc * 4:(cc + 1) * 4, :, j * 128:(j + 1) * 128].rearrange("c p t -> p c t"),
                           W=(gu,), owner=gu)
                    acc = acc_ring.next()
                    tm_tile(pcs, hT, j, acc)
                    gl = gl_ring.next()
                    sq2 = sq2_ring.next()
                    st = st_ring.next()
                    vn = vn_ring.next()
                    tk.op("act", lambda e: e.activation(out=gl[:], in_=acc[:], func=AF.Gelu_apprx_tanh), R=(acc,), W=(gl,))
                    tk.op("act", lambda e: e.activation(out=sq2[:], in_=gl[:], func=AF.Square), R=(gl,), W=(sq2,))
                    tk.op("dve", lambda e: e.tensor_reduce(out=st[:, 0:4], in_=gl[:].rearrange("p (g c) -> p g c", g=4),
                                                            axis=AX.X, op=ALU.add), R=(gl,), W=(st,))
                    tk.op("dve", lambda e: e.tensor_reduce(out=st[:, 4:8], in_=sq2[:].rearrange("p (g c) -> p g c", g=4),
                                                            axis=AX.X, op=ALU.add), R=(sq2, st), W=(st,))
                    tk.op("dve", lambda e: e.tensor_scalar(out=st[:, 0:4], in0=st[:, 0:4], scalar1=1.0 / 128, scalar2=None,
                                                            op0=ALU.mult), R=(st,), W=(st,))
                    tk.op("dve", lambda e: e.tensor_tensor(out=st[:, 8:12], in0=st[:, 0:4], in1=st[:, 0:4], op=ALU.mult),
                          R=(st,), W=(st,))
                    tk.op("dve", lambda e: e.scalar_tensor_tensor(out=st[:, 12:16], in0=st[:, 4:8], scalar=1.0 / 128,
                                                                   in1=st[:, 8:12], op0=ALU.mult, op1=ALU.subtract),
                          R=(st,), W=(st,))
                    tk.op("act", lambda e: e.activation(out=st[:, 8:12], in_=st[:, 12:16], func=AF.Sqrt, bias=epsT[:, 0:1]),
                          R=(st, epsT), W=(st,))
                    tk.op("dve", lambda e: e.reciprocal(out=st[:, 12:16], in_=st[:, 8:12]), R=(st,), W=(st,))
                    for gi in range(4):
                        tk.op("dve", lambda e: e.tensor_scalar(out=gl[:, gi * 128:(gi + 1) * 128],
                                                                in0=gl[:, gi * 128:(gi + 1) * 128],
                                                                scalar1=st[:, gi:gi + 1], scalar2=st[:, 12 + gi:13 + gi],
                                                                op0=ALU.subtract, op1=ALU.mult), R=(gl, st), W=(gl,))
                    tk.op("dve", lambda e: e.tensor_tensor(out=vn[:], in0=gl[:], in1=vgain[:, cc * 512:(cc + 1) * 512],
                                                            op=ALU.mult), R=(gl, vgain), W=(vn,))
                    acc2 = acc_ring.next()
                    for gi in range(4):
                        g = cc * 4 + gi
                        tk.mm(acc2, [(acc2[:, gi * 128:(gi + 1) * 128], vn[:, gi * 128:(gi + 1) * 128],
                                      WgT[:, g * 128:(g + 1) * 128], True, True, (vn, WgT))], cont=(gi > 0))
                    zt = zt_ring.next()
                    tk.op("dve", lambda e: e.tensor_tensor(out=zt[:], in0=acc2[:], in1=bsb[:, cc * 512:(cc + 1) * 512],
                                                            op=ALU.add), R=(acc2, bsb), W=(zt,))
                    tk.op("dve", lambda e: e.tensor_tensor(
                        out=mix_gm[:, cc * 4:(cc + 1) * 4, j * 128:(j + 1) * 128],
                        in0=zt[:].rearrange("p (g t) -> p g t", g=4),
                        in1=gu[:], op=ALU.mult),
                        R=(zt, gu), W=(mix_gm,))
            tk.op("dve", lambda e: e.tensor_tensor(out=tmp1[:], in0=qmax[:], in1=kmax[:], op=ALU.mult),
                  R=(qmax, kmax), W=(tmp1,))
            tk.op("act", lambda e: e.activation(out=tmp1[:], in_=tmp1[:], func=AF.Sqrt), R=(tmp1,), W=(tmp1,))
            tk.op("dve", lambda e: e.tensor_scalar(out=tmp1[:], in0=tmp1[:], scalar1=-1.05 * 128 ** -0.5, scalar2=None,
                                                    op0=ALU.mult), R=(tmp1,), W=(tmp1,))
            tmpb = P.sb([1, 2], BF16)
            tk.op("dve", lambda e: e.tensor_copy(out=tmpb[:, 0:1], in_=tmp1[:]), R=(tmp1,), W=(tmpb,))
            tk.mm(nps, [(nps[:, 0:1], onesb[0:1, :], tmpb[0:1, 0:1], True, True, (onesb, tmpb))])
            tk.op("act", lambda e: e.copy(out=negb[:], in_=nps[:, 0:1]), R=(nps,), W=(negb,))
        P.close()

        M2 = Pool_(tk, nc)
        mix_att = M2.sb([128, 16, NT * 128], BF16)

        P = Pool_(tk, nc)
        if STAGE >= 3:
            kidx = P.sb([128, L], BF16, dma=True)
            tk.dma("sp", kidx[:], kidxTs, W=(kidx,), owner=kidx)
            qi_ring = Ring([P.sb([128, 16, 128], BF16, dma=True) for _ in range(1)])
            qt_ring = Ring([P.sb([128, 16, 128], BF16, dma=True) for _ in range(1)])
            Dg_ring = Ring([P.sb([128, 32, 128], BF16) for _ in range(1)])
            sc_ring = Ring([P.sb([128, L], F32, dma=True) for _ in range(2)])
            work = P.sb([128, L], F32)
            nm_in_ring = Ring([P.sb([128, 512], BF16, dma=True) for _ in range(2)])
            oh_ring = Ring([P.sb([128, 128], F32) for _ in range(3)])
            dh_ring = Ring([P.sb([128, 128], F32) for _ in range(3)])
            nm = P.sb([128, L], BF16, dma=True)
            nmT = P.sb([128, 32, 128], BF16)
            m8_ring = Ring([P.sb([128, 8], F32) for _ in range(2)])
            thr_ring = Ring([P.sb([128, 1], F32) for _ in range(2)])
            R_ring = Ring([P.sb([128, 512], BF16) for _ in range(4)])
            kt_ring = Ring([P.sb([128, L], BF16, dma=True) for _ in range(2)])
            v_ring = Ring([P.sb([128, 32, 128], BF16, dma=True) for _ in range(2)])
            pt_ring = Ring([P.sb([128, 512], BF16) for _ in range(3)])
            Lps_ring = Ring([P.ps([128, 512], F32) for _ in range(4)])
            dps_ring = Ring([P.ps([128, 512], F32) for _ in range(1)])
            accs_ring = Ring([P.ps([128, 512], F32) for _ in range(1)])
            sps_ring = Lps_ring
            ops_ring = Ring([P.ps([128, 512], F32) for _ in range(1)])
            tps_ring = Ring([P.ps([128, 1024], BF16) for _ in range(1)])

            def scores(m):
                nb = 4 * (m + 1)
                qi = qi_ring.next()
                tk.dma("sp", qi[:], OWN[32:48, :, m * 128:(m + 1) * 128].rearrange("c p t -> p c t"), W=(qi,), owner=qi)
                Dg = Dg_ring.next()
                for h in range(32):
                    tk.op("pool", lambda e: e.tensor_scalar(out=Dg[:, h, :], in0=identb[:], scalar1=wi[:, m, h:h + 1],
                                                             scalar2=None, op0=ALU.mult), R=(identb, wi), W=(Dg,))
                sc = sc_ring.next()
                for sg in range(nb // 4):
                    acc = accs_ring.next()
                    pend = []
                    LOOK = 2
                    last = (sg == nb // 4 - 1)
                    for h in range(32 + LOOK):
                        if h < 32:
                            lp = Lps_ring.next()
                            pr = (h % 2) * 64
                            tk.mm(lp, [(lp[:], qi[pr:pr + 64, h // 2, :], kidx[pr:pr + 64, sg * 512:(sg + 1) * 512],
                                        True, True, (qi, kidx))])
                            Rt = R_ring.next()
                            tk.op("act", lambda e: e.activation(out=Rt[:], in_=lp[:], func=AF.Relu), R=(lp,), W=(Rt,))
                            pend.append((h, Rt))
                        if h >= LOOK:
                            hh, Rr = pend.pop(0)
                            tk.mm(acc, [(acc[:], Dg[:, hh, :], Rr[:], hh == 0, hh == 31 and not last, (Dg, Rr))],
                                  cont=(hh > 0))
                    if sg == nb // 4 - 1:
                        nmi = nm_in_ring.next()
                        tk.dma("pool", nmi[:], negmask_in[m], W=(nmi,), owner=nmi)
                        tk.mm(acc, [(acc[:], identb[:], nmi[:], False, True, (identb, nmi))], cont=True)
                    tk.op("act", lambda e: e.copy(out=sc[:, sg * 512:(sg + 1) * 512], in_=acc[:]), R=(acc,), W=(sc,))
                return sc

            def topk(m, sc):
                nb = 4 * (m + 1)
                S = nb * 128
                m8 = None
                for rnd in range(32):
                    m8 = m8_ring.next()
                    src = sc if rnd == 0 else work
                    tk.op("dve", lambda e: e.max(out=m8[:], in_=src[:, 0:S]), R=(src,), W=(m8,))
                    if rnd < 31:
                        if rnd == 0:
                            tk.op("dve", lambda e: e.match_replace(out=work[:, 0:S], in_to_replace=m8[:],
                                                                    in_values=src[:, 0:S], imm_value=NEG),
                                  R=(m8, src), W=(work,))
                        else:
                            tk.op("dve", lambda e: e.match_replace(out=work[:, 0:S], in_to_replace=m8[:],
                                                                    in_values=work[:, 0:S], imm_value=NEG),
                                  R=(m8, work), W=(work,))
                thr = thr_ring.next()
                tk.op("dve", lambda e: e.tensor_scalar(out=thr[:], in0=m8[:, 7:8], scalar1=-1e29, scalar2=None, op0=ALU.max),
                      R=(m8,), W=(thr,))
                tk.op("dve", lambda e: e.tensor_scalar(out=nm[:, 0:S], in0=sc[:, 0:S], scalar1=thr[:, 0:1], scalar2=-30000.0,
                                                        op0=ALU.is_lt, op1=ALU.mult), R=(sc, thr), W=(nm,))
                if DEBUG:
                    tk.dma("sp", scdbg[m, :, 0:S], sc[:, 0:S], R=(sc,), owner=sc)
                    tk.dma("sp", nmdbg[m, :, 0:S], nm[:, 0:S], R=(nm,), owner=nm)

            def attention_pre(m):
                nb = 4 * (m + 1)
                for b0 in range(0, nb, 8):
                    nbb = min(8, nb - b0)
                    tp = tps_ring.next()
                    tk.tr(tp, [(tp[:, j * 128:(j + 1) * 128], nm[:, (b0 + j) * 128:(b0 + j + 1) * 128], identb[:], (nm, identb))
                               for j in range(nbb)])
                    tk.op("act", lambda e: e.copy(out=nmT[:, b0:b0 + nbb, :],
                                                   in_=tp[:, 0:nbb * 128].rearrange("p (j t) -> p j t", j=nbb)),
                          R=(tp,), W=(nmT,))

            def attention(m):
                nb = 4 * (m + 1)
                S = nb * 128
                qt = qt_ring.next()
                tk.dma("sp", qt[:], OWN[16:32, :, m * 128:(m + 1) * 128].rearrange("h p t -> p h t"), W=(qt,), owner=qt)
                for h in range(16):
                    kt = kt_ring.next()
                    tk.dma("sp", kt[:, 0:S], KTs[h, :, 0:S], W=(kt,), owner=kt)
                    vt = v_ring.next()
                    tk.dma("sp", vt[:, 0:nb, :], Vs[h, :, 0:nb, :], W=(vt,), owner=vt)
                    ops = ops_ring.next()
                    dps = dps_ring.next()
                    prev = None
                    nsg = nb // 4
                    for sg in range(nsg + 1):
                        if sg < nsg:
                            sp = sps_ring.next()
                            mms = []
                            for j in range(4):
                                blk = sg * 4 + j
                                mms.append((sp[:, j * 128:(j + 1) * 128], kt[:, blk * 128:(blk + 1) * 128], qt[:, h, :],
                                            True, False, (kt, qt)))
                                mms.append((sp[:, j * 128:(j + 1) * 128], identb[:], nmT[:, blk, :], False, True,
                                            (identb, nmT)))
                            tk.mm(sp, mms)
                            pt = pt_ring.next()
                            tk.op("act", lambda e: e.activation(out=pt[:], in_=sp[:], func=AF.Exp, bias=negb[:, 0:1],
                                                                 scale=float(128 ** -0.5)), R=(sp, negb), W=(pt,))
                        if prev is not None:
                            sg2, pt2 = prev
                            mms = []
                            mmd = []
                            for j in range(4):
                                blk = sg2 * 4 + j
                                mms.append((ops[:, 0:128], vt[:, blk, :], pt2[:, j * 128:(j + 1) * 128],
                                            blk == 0, blk == nb - 1, (vt, pt2)))
                                mmd.append((dps[:, 0:128], onesb[:], pt2[:, j * 128:(j + 1) * 128],
                                            blk == 0, blk == nb - 1, (onesb, pt2)))
                            tk.mm(ops, mms, cont=(sg2 > 0))
                            tk.mm(dps, mmd, cont=(sg2 > 0))
                        prev = (sg, pt) if sg < nsg else None
                    oh = oh_ring.next()
                    dh = dh_ring.next()
                    tk.op("act", lambda e: e.copy(out=oh[:], in_=ops[:, 0:128]), R=(ops,), W=(oh,))
                    tk.op("act", lambda e: e.activation(out=dh[:], in_=dps[:, 0:128], func=AF.Ln), R=(dps,), W=(dh,))
                    tk.op("act", lambda e: e.activation(out=dh[:], in_=dh[:], func=AF.Exp, scale=-1.0), R=(dh,), W=(dh,))
                    tk.op("pool", lambda e: e.tensor_tensor(out=mix_att[:, h, m * 128:(m + 1) * 128], in0=oh[:],
                                                             in1=dh[:], op=ALU.mult), R=(oh, dh), W=(mix_att,))

            sc_list = {0: scores(0), 1: scores(1)}
            topk(0, sc_list[0])
            for m in range(NT):
                attention_pre(m)
                if m + 1 < NT:
                    topk(m + 1, sc_list[m + 1])
                attention(m)
                if m + 2 < NT:
                    sc_list[m + 2] = scores(m + 2)
        P.close()

        if DEBUG and STAGE >= 3:
            Dp = Pool_(tk, nc)
            own = Dp.sb([1, 1], F32, dma=True)
            tk.dma("sp", mixdbg[:, 0:16, :], mix_att[:], R=(mix_att,), owner=own)
            tk.dma("sp", mixdbg[:, 16:32, :], mix_gm[:], R=(mix_gm,), owner=own)
            Dp.close()

        P = Pool_(tk, nc)
        if STAGE >= 4:
            wslots = Ring([P.sb([128, 4096], BF16, dma=True) for _ in range(8)])
            xr_ring = Ring([P.sb([128, 512], F32, dma=True) for _ in range(3)])
            x1_ring = Ring([P.sb([128, 512], F32, dma=True) for _ in range(3)])
            acc_ring = Ring([P.ps([128, 512], F32) for _ in range(6)])
            for ncn in range(8):
                pcs = []
                for q in range(4):
                    ws = wslots.next()
                    tk.dma("pool", ws[:], wtm_out[ncn, q], W=(ws,), owner=ws)
                    pcs.append(ws)
                for j in range(NT):
                    xr = xr_ring.next()
                    tk.dma("sp", xr[:], x_own[j * 128:(j + 1) * 128, ncn * 512:(ncn + 1) * 512], W=(xr,), owner=xr)
                    acc = acc_ring.next()
                    mms = []
                    for kc in range(32):
                        src = mix_att if kc < 16 else mix_gm
                        mms.append((acc[:], src[:, kc % 16, j * 128:(j + 1) * 128],
                                    pcs[kc // 8][:, (kc % 8) * 512:(kc % 8 + 1) * 512], kc == 0, kc == 31, (pcs[kc // 8], src)))
                    tk.mm(acc, mms)
                    x1t = x1_ring.next()
                    tk.op("dve", lambda e: e.tensor_tensor(out=x1t[:], in0=acc[:], in1=xr[:], op=ALU.add),
                          R=(acc, xr), W=(x1t,))
                    tk.dma("sp", x1s[j * 128:(j + 1) * 128, ncn * 512:(ncn + 1) * 512], x1t[:], R=(x1t,), owner=x1t)
        P.close()
        M2.close()
        M1.close()

        for half in range(2 if STAGE >= 5 else 0):
            P = Pool_(tk, nc)
            AT = P.sb([128, NFC, 512], BF16)
            Pg = Pool_(tk, nc)
            h2T = Pg.sb([128, 32, 512], BF16)
            Pn = Pool_(tk, nc)
            gb = Pn.sb([128, D], F32, dma=True)
            tk.dma("sp", gb[:], gffn_in, W=(gb,), owner=gb)
            xs_ring = Ring([Pn.sb([128, D], F32, dma=True) for _ in range(1)])
            h1_ring = Ring([Pn.sb([128, D], BF16) for _ in range(1)])
            junk = Pn.sb([128, D], BF16)
            ss_ring = Ring([Pn.sb([128, 4], F32) for _ in range(4)])
            ptr_ring = Ring([Pn.ps([128, 1024], BF16) for _ in range(2)])
            for tt in range(4):
                r0 = (half * 4 + tt) * 128
                rmsnorm_T(x1s[r0:r0 + 128, :], gb, xs_ring, h1_ring, junk, ptr_ring, h2T, tt * 128, ss_ring)
            Pn.close()
            wslots = Ring([Pg.sb([128, 4096], BF16, dma=True) for _ in range(6)])
            sg_ring = Ring([Pg.sb([128, 512], F32) for _ in range(2)])
            acc_ring = Ring([Pg.ps([128, 512], F32) for _ in range(6)])
            for fc in range(NFC):
                wg = wslots.next()
                tk.dma("pool", wg[:], wfm_g[fc], W=(wg,), owner=wg)
                wu = wslots.next()
                tk.dma("pool", wu[:], wfm_u[fc], W=(wu,), owner=wu)
                ag = acc_ring.next()
                tk.mm(ag, [(ag[:], wg[:, kc * 128:(kc + 1) * 128], h2T[:, kc, :], kc == 0, kc == 31, (wg, h2T)) for kc in range(32)])
                au = acc_ring.next()
                tk.mm(au, [(au[:], wu[:, kc * 128:(kc + 1) * 128], h2T[:, kc, :], kc == 0, kc == 31, (wu, h2T)) for kc in range(32)])
                sgt = sg_ring.next()
                tk.op("act", lambda e: e.activation(out=sgt[:], in_=ag[:], func=AF.Silu), R=(ag,), W=(sgt,))
                tk.op("dve", lambda e: e.tensor_tensor(out=AT[:, fc, :], in0=au[:], in1=sgt[:], op=ALU.mult),
                      R=(au, sgt), W=(AT,))
            Pg.close()
            Pd = Pool_(tk, nc)
            x2 = Pd.sb([128, 4, D], F32, dma=True)
            r0 = half * 512
            for tt in range(4):
                tk.dma("sp", x2[:, tt, :], x1s[r0 + tt * 128:r0 + (tt + 1) * 128, :], W=(x2,), owner=x2)
            wdslots = Ring([Pd.sb([128, 43 * 128], BF16, dma=True) for _ in range(4)])
            ys_ring = Ring([Pd.sb([128, 512], F32) for _ in range(2)])
            gfq_ring = Ring([Pd.sb([128, 1024], F32, dma=True) for _ in range(1)])
            yacc_ring = Ring([Pd.ps([128, 512], F32) for _ in range(4)])
            ytr_ring = Ring([Pd.ps([128, 512], F32) for _ in range(3)])
            ss_ring = Ring([Pd.sb([128, 4], F32) for _ in range(4)])
            for ncn in range(32):
                w0 = wdslots.next()
                tk.dma("pool", w0[:], wd_in[ncn, 0], W=(w0,), owner=w0)
                w1 = wdslots.next()
                tk.dma("pool", w1[:], wd_in[ncn, 1], W=(w1,), owner=w1)
                ya = yacc_ring.next()
                mms = []
                for fc in range(NFC):
                    wsrc = w0 if fc < 43 else w1
                    mms.append((ya[:], wsrc[:, (fc % 43) * 128:(fc % 43 + 1) * 128], AT[:, fc, :], fc == 0, fc == NFC - 1,
                                (wsrc, AT)))
                tk.mm(ya, mms)
                ys = ys_ring.next()
                tk.op("act", lambda e: e.copy(out=ys[:], in_=ya[:]), R=(ya,), W=(ys,))
                yt = ytr_ring.next()
                tk.tr(yt, [(yt[:, tt * 128:(tt + 1) * 128], ys[:, tt * 128:(tt + 1) * 128], identf[:], (ys, identf))
                           for tt in range(4)])
                tk.op("dve", lambda e: e.tensor_tensor(out=x2[:, :, ncn * 128:(ncn + 1) * 128],
                                                        in0=yt[:].rearrange("p (t n) -> p t n", t=4),
                                                        in1=x2[:, :, ncn * 128:(ncn + 1) * 128], op=ALU.add),
                      R=(yt, x2), W=(x2,))
            for tt in range(4):
                ss = ss_ring.next()
                tk.op("act", lambda e: e.activation(out=AT[:, 0:8, :].rearrange("p a b -> p (a b)"), in_=x2[:, tt, :],
                                                     func=AF.Square, scale=1.0 / 64.0, accum_out=ss[:, 0:1]),
                      R=(x2,), W=(AT, ss))
                tk.op("act", lambda e: e.activation(out=ss[:, 1:2], in_=ss[:, 0:1], func=AF.Sqrt, bias=epsT[:, 0:1]),
                      R=(ss, epsT), W=(ss,))
                tk.op("dve", lambda e: e.reciprocal(out=ss[:, 2:3], in_=ss[:, 1:2]), R=(ss,), W=(ss,))
                for q in range(4):
                    gfq = gfq_ring.next()
                    tk.dma("sp", gfq[:], gfin_in[:, q * 1024:(q + 1) * 1024], W=(gfq,), owner=gfq)
                    tk.op("dve", lambda e: e.scalar_tensor_tensor(out=x2[:, tt, q * 1024:(q + 1) * 1024],
                                                                   in0=x2[:, tt, q * 1024:(q + 1) * 1024], scalar=ss[:, 2:3],
                                                                   in1=gfq[:], op0=ALU.mult, op1=ALU.mult),
                          R=(x2, ss, gfq), W=(x2,))
            for tt in range(4):
                tk.dma("sp", out[r0 + tt * 128:r0 + (tt + 1) * 128, :], x2[:, tt, :], R=(x2,), owner=x2)
            Pd.close()
            P.close()
        G.close()
        tk.barrier()
    return nc


def _fm(Wc):
    n = Wc.shape[1]
    a = Wc.reshape(32, 128, n // 128, 128).transpose(2, 1, 0, 3)
    return np.ascontiguousarray(a).reshape(n // 128, 128, 32 * 128)


def _tm(Wc):
    n = Wc.shape[1]
    a = Wc.reshape(4, 8, 128, n // 512, 512).transpose(3, 0, 2, 1, 4)
    return np.ascontiguousarray(a).reshape(n // 512, 4, 128, 8 * 512)


def own_tiles(r):
    return [8 * (m // 2) + (r if m % 2 == 0 else 7 - r) for m in range(NT)]


_CACHE = {}


def kernel(x, norm_mix, w_in, gmlp_v_gain, w_spatial, b_spatial, w_out, norm_ffn, w_gate, w_up, w_down, norm_final):
    x = np.asarray(x, dtype=np.float32)
    w_in = np.asarray(w_in, dtype=np.float32)[0]
    w_out = np.asarray(w_out, dtype=np.float32)[0]
    w_gate = np.asarray(w_gate, dtype=np.float32)[0]
    w_up = np.asarray(w_up, dtype=np.float32)[0]
    w_down = np.asarray(w_down, dtype=np.float32)[0]
    f32 = np.float32
    shared = {}
    kcols = w_in[:, 2048:4096]
    kidxc = w_in[:, 8192:8256]
    shared["wfm_kv"] = np.concatenate([_fm(kcols), _fm(np.concatenate([kidxc, kidxc], axis=1))], axis=0)
    shared["wfm_own"] = np.concatenate([_fm(w_in[:, 8288:10336]), _fm(w_in[:, 0:2048]), _fm(w_in[:, 6144:8192])], axis=0)
    shared["wtm_v"] = _tm(w_in[:, 4096:6144])
    shared["wtm_vg"] = _tm(w_in[:, 10336:12384])
    shared["wtm_wi"] = np.ascontiguousarray(w_in[:, 8256:8288].reshape(32, 128, 32).transpose(1, 0, 2)).reshape(128, 32 * 32)
    shared["wtm_out"] = _tm(w_out)
    shared["wfm_g"] = _fm(w_gate)
    shared["wfm_u"] = _fm(w_up)
    wd = w_down.reshape(2, 43, 128, 32, 128).transpose(3, 0, 2, 1, 4)
    shared["wd"] = np.ascontiguousarray(wd).reshape(32, 2, 128, 43 * 128)
    shared["ident"] = np.eye(128, dtype=f32)
    shared["tril_st"] = np.triu(np.ones((128, 128), dtype=f32))
    shared["gmix_b"] = np.ascontiguousarray(np.broadcast_to(np.asarray(norm_mix, f32)[0][None, :], (128, D)))
    shared["gffn_b"] = np.ascontiguousarray(np.broadcast_to(np.asarray(norm_ffn, f32)[0][None, :], (128, D)))
    shared["gfin_b"] = np.ascontiguousarray(np.broadcast_to(np.asarray(norm_final, f32)[None, :], (128, D)))
    shared["vgain_b"] = np.ascontiguousarray(np.broadcast_to(np.asarray(gmlp_v_gain, f32)[0].reshape(1, 2048), (128, 2048)))
    shared["bsb"] = np.ascontiguousarray(np.broadcast_to(np.asarray(b_spatial, f32)[0].reshape(1, 2048), (128, 2048)))
    shared["wsT"] = np.ascontiguousarray(np.asarray(w_spatial, f32)[0].transpose(2, 0, 1)).reshape(128, 2048)

    in_maps = []
    for c in range(8):
        b, r = c // 4, c % 4
        tiles = own_tiles(r)
        xb = x[b]
        m = dict(shared)
        m["x_all"] = xb
        m["x_own"] = np.ascontiguousarray(np.concatenate([xb[t * 128:(t + 1) * 128] for t in tiles], axis=0))
        nmk = np.zeros((NT, 128, 512), dtype=f32)
        for mi, t in enumerate(tiles):
            nb = 4 * (mi + 1)
            qpos = t * 128 + np.arange(128)[:, None]
            kpos = (nb - 4) * 128 + np.arange(512)[None, :]
            nmk[mi] = np.where(kpos <= qpos, 0.0, NEG).astype(f32)
        m["negmask"] = nmk
        in_maps.append(m)

    if "nc" not in _CACHE:
        _CACHE["nc"] = build_program()
    nc = _CACHE["nc"]
    ncores = int(os.environ.get("K_NCORES", "8"))
    if ncores < 8:
        res = run_bass_kernel_spmd(nc, in_maps[:ncores], core_ids=list(range(ncores)))
        _CACHE["res"] = res
        return None
    res = run_bass_kernel_spmd(nc, in_maps, core_ids=list(range(8)))
    outp = np.zeros((2, L, D), dtype=np.float32)
    for c in range(8):
        b, r = c // 4, c % 4
        o = res.results[c]["out"]
        for mi, t in enumerate(own_tiles(r)):
            outp[b, t * 128:(t + 1) * 128] = o[mi * 128:(mi + 1) * 128]
    if DEBUG:
        _CACHE["res"] = res
    return outp
```

```python
import os
from contextlib import ExitStack
import numpy as np
import concourse.bass as bass
import concourse.mybir as mybir
from concourse.bass_utils import run_bass_kernel_spmd

F32 = mybir.dt.float32
BF16 = mybir.dt.bfloat16
AF = mybir.ActivationFunctionType
ALU = mybir.AluOpType
AX = mybir.AxisListType

D = 4096
L = 4096
DFF = 11008
NFC = 86
NT = 8
EPS = 1e-6
NEG = -1e30
STAGE = int(os.environ.get("K_STAGE", "99"))
DEBUG = int(os.environ.get("K_DEBUG", "0"))


class Eng:
    def __init__(self, name, e, sem):
        self.name, self.e, self.sem = name, e, sem
        self.cnt = 0
        self.waited = {}

    def wait(self, tok):
        if tok is None:
            return
        key, sem, val = tok
        if self.waited.get(key, 0) >= val:
            return
        self.e.wait_ge(sem, val)
        self.waited[key] = val

    def sig(self, ins):
        self.cnt += 1
        ins.then_inc(self.sem, 1)
        return (self.name, self.sem, self.cnt)


class T:
    _n = 0

    def __init__(self, ap, dsem=None):
        self.ap = ap
        self.w = None
        self.r = {}
        self.dsem = dsem
        self.dcnt = 0
        T._n += 1
        self.name = "t%d" % T._n

    def __getitem__(self, idx):
        return self.ap[idx]


class Tk:
    def __init__(self, nc, es):
        self.nc = nc
        self.es = es
        self.E = {}
        for name, e in (("pe", nc.tensor), ("act", nc.scalar), ("dve", nc.vector),
                        ("pool", nc.gpsimd), ("sp", nc.sync)):
            sem = es.enter_context(nc.semaphore("s_" + name))
            self.E[name] = Eng(name, e, sem)
        self.dma_latest = {}
        self.free_dsems = {"sp": [], "pool": []}
        self.nsem = 0

    def dsem(self, kind):
        if self.free_dsems[kind]:
            return self.free_dsems[kind].pop()
        self.nsem += 1
        return [self.es.enter_context(self.nc.semaphore("d%d" % self.nsem)), 0, "d%d" % self.nsem]

    def _pre(self, E, en, R, W):
        for t in R:
            E.wait(t.w)
        for t in W:
            E.wait(t.w)
            for k, tok in t.r.items():
                E.wait(tok)

    def op(self, en, fn, R=(), W=()):
        E = self.E[en]
        self._pre(E, en, R, W)
        ins = fn(E.e)
        tok = E.sig(ins)
        for t in R:
            t.r[en] = tok
        for t in W:
            t.w = tok
            t.r = {}
        return tok

    def mm(self, out_t, mms, cont=False):
        E = self.E["pe"]
        if not cont:
            self._pre(E, "pe", (), (out_t,))
        allR = []
        ins = None
        for (o, l, r, st, sp, R) in mms:
            for t in R:
                E.wait(t.w)
                allR.append(t)
            ins = E.e.matmul(o, lhsT=l, rhs=r, start=st, stop=sp)
        tok = E.sig(ins)
        for t in allR:
            t.r["pe"] = tok
        out_t.w = tok
        if not cont:
            out_t.r = {}
        return tok

    def tr(self, out_t, trs):
        E = self.E["pe"]
        self._pre(E, "pe", (), (out_t,))
        allR = []
        ins = None
        for (o, i, idn, R) in trs:
            for t in R:
                E.wait(t.w)
                allR.append(t)
            ins = E.e.transpose(o, i, idn)
        tok = E.sig(ins)
        for t in allR:
            t.r["pe"] = tok
        out_t.w = tok
        out_t.r = {}
        return tok

    def dma(self, qn, out_ap, in_ap, R=(), W=(), owner=None):
        Q = self.E[qn]
        self._pre(Q, "dma", R, W)
        ins = Q.e.dma_start(out=out_ap, in_=in_ap)
        if owner.dsem is None:
            owner.dsem = self.dsem(qn)
            owner.dkind = qn
            owner.pool.ds.append((qn, owner.dsem))
        assert owner.dkind == qn, "tile DMA semaphore is bound to one queue kind"
        ds = owner.dsem
        ds[1] += 16
        ins.then_inc(ds[0], 16)
        tok = (ds[2], ds[0], ds[1])
        for t in R:
            t.r["dma" + ds[2]] = tok
        for t in W:
            t.w = tok
            t.r = {}
        self.dma_latest[ds[2]] = tok
        return tok

    def barrier(self):
        sp = self.E["sp"]
        for tok in self.dma_latest.values():
            sp.wait(tok)
        toks = []
        for en, E in self.E.items():
            if en == "sp":
                continue
            if E.cnt > 0:
                E.e.wait_ge(E.sem, E.cnt)
            ins = E.e.sem_inc(E.sem, 1)
            E.cnt += 1
            toks.append((E.name, E.sem, E.cnt))
        for tok in toks:
            sp.wait(tok)
        if sp.cnt > 0:
            sp.e.wait_ge(sp.sem, sp.cnt)
        ins = sp.e.sem_inc(sp.sem, 1)
        sp.cnt += 1
        stok = (sp.name, sp.sem, sp.cnt)
        for en, E in self.E.items():
            if en != "sp":
                E.wait(stok)


class Pool_:
    cnt = 0

    def __init__(self, tk, nc):
        self.tk, self.nc = tk, nc
        self.es = ExitStack()
        self.ds = []
        self.n = 0

    def sb(self, shape, dt, dma=False, name=None):
        Pool_.cnt += 1
        h = self.es.enter_context(self.nc.sbuf_tensor("sb%d" % Pool_.cnt, shape, dt))
        t = T(h[:], None)
        t.pool = self
        return t

    def ps(self, shape, dt):
        Pool_.cnt += 1
        h = self.es.enter_context(self.nc.psum_tensor("ps%d" % Pool_.cnt, shape, dt))
        return T(h[:])

    def close(self):
        self.tk.barrier()
        for kind, ds in self.ds:
            self.tk.free_dsems[kind].append(ds)
        self.es.close()


class Ring:
    def __init__(self, tiles):
        self.tiles = tiles
        self.i = 0

    def next(self):
        t = self.tiles[self.i % len(self.tiles)]
        self.i += 1
        return t


def build_program():
    nc = bass.Bass("TRN2", target_bir_lowering=False)
    dk = "ExternalOutput" if DEBUG else "Internal"

    def din(name, shape, dt=F32):
        return nc.dram_tensor(name, shape, dt, kind="ExternalInput").ap()

    x_all = din("x_all", [L, D])
    x_own = din("x_own", [NT * 128, D])
    negmask_in = din("negmask", [NT, 128, 512])
    ident_in = din("ident", [128, 128])
    tril_in = din("tril_st", [128, 128])
    gmix_in = din("gmix_b", [128, D])
    gffn_in = din("gffn_b", [128, D])
    gfin_in = din("gfin_b", [128, D])
    vgain_in = din("vgain_b", [128, 2048])
    bsb_in = din("bsb", [128, 16 * 128])
    wsT_in = din("wsT", [128, 16 * 128])
    wfm_kv = din("wfm_kv", [17, 128, 32 * 128])
    wfm_own = din("wfm_own", [48, 128, 32 * 128])
    wtm_v = din("wtm_v", [4, 4, 128, 8 * 512])
    wtm_vg = din("wtm_vg", [4, 4, 128, 8 * 512])
    wtm_wi = din("wtm_wi", [128, 32 * 32])
    wtm_out = din("wtm_out", [8, 4, 128, 8 * 512])
    wfm_g = din("wfm_g", [NFC, 128, 32 * 128])
    wfm_u = din("wfm_u", [NFC, 128, 32 * 128])
    wd_in = din("wd", [32, 2, 128, 43 * 128])
    out = nc.dram_tensor("out", [NT * 128, D], F32, kind="ExternalOutput").ap()

    KTs = nc.dram_tensor("KTs", [16, 128, L], BF16, kind=dk).ap()
    kidxTs = nc.dram_tensor("kidxTs", [128, L], BF16, kind=dk).ap()
    Vs = nc.dram_tensor("Vs", [16, 128, 32, 128], BF16, kind=dk).ap()
    OWN = nc.dram_tensor("OWNs", [48, 128, NT * 128], BF16, kind=dk).ap()
    x1s = nc.dram_tensor("x1s", [NT * 128, D], F32, kind=dk).ap()
    if DEBUG:
        mixdbg = nc.dram_tensor("mixdbg", [128, 32, NT * 128], BF16, kind="ExternalOutput").ap()
        scdbg = nc.dram_tensor("scdbg", [NT, 128, L], F32, kind="ExternalOutput").ap()
        nmdbg = nc.dram_tensor("nmdbg", [NT, 128, L], BF16, kind="ExternalOutput").ap()

    with ExitStack() as ges:
        tk = Tk(nc, ges)
        G = Pool_(tk, nc)
        identf = G.sb([128, 128], F32, dma=True)
        identb = G.sb([128, 128], BF16, dma=True)
        onesb = G.sb([128, 128], BF16)
        kmax = G.sb([1, 1], F32)
        qmax = G.sb([1, 1], F32)
        negb = G.sb([128, 1], F32)
        tmp1 = G.sb([1, 1], F32)
        epsT = G.sb([128, 1], F32)

        tk.dma("sp", identf[:], ident_in, W=(identf,), owner=identf)
        tk.dma("pool", identb[:], ident_in, W=(identb,), owner=identb)
        tk.op("dve", lambda e: e.memset(onesb[:], 1.0), W=(onesb,))
        tk.op("dve", lambda e: e.memset(kmax[:], 0.0), W=(kmax,))
        tk.op("dve", lambda e: e.memset(qmax[:], 0.0), W=(qmax,))
        tk.op("dve", lambda e: e.memset(epsT[:], EPS), W=(epsT,))

        def rmsnorm_T(x_src_ap, gb, xs_ring, h1_ring, junk, ptr_ring, hT, tcol, ss_ring):
            xs = xs_ring.next()
            tk.dma("sp", xs[:], x_src_ap, W=(xs,), owner=xs)
            ss = ss_ring.next()
            tk.op("act", lambda e: e.activation(out=junk[:], in_=xs[:], func=AF.Square, scale=1.0 / 64.0,
                                                 accum_out=ss[:, 0:1]), R=(xs,), W=(junk, ss))
            tk.op("act", lambda e: e.activation(out=ss[:, 1:2], in_=ss[:, 0:1], func=AF.Sqrt, bias=epsT[:, 0:1]),
                  R=(ss, epsT), W=(ss,))
            tk.op("dve", lambda e: e.reciprocal(out=ss[:, 2:3], in_=ss[:, 1:2]), R=(ss,), W=(ss,))
            h1 = h1_ring.next()
            tk.op("dve", lambda e: e.scalar_tensor_tensor(out=h1[:], in0=xs[:], scalar=ss[:, 2:3], in1=gb[:],
                                                            op0=ALU.mult, op1=ALU.mult), R=(xs, ss, gb), W=(h1,))
            for q in range(4):
                pt = ptr_ring.next()
                tk.tr(pt, [(pt[:, j * 128:(j + 1) * 128], h1[:, (q * 8 + j) * 128:(q * 8 + j + 1) * 128], identb[:],
                            (h1, identb)) for j in range(8)])
                if q % 2 == 0:
                    tk.op("act", lambda e: e.copy(out=hT[:, q * 8:(q + 1) * 8, tcol:tcol + 128],
                                                   in_=pt[:].rearrange("p (j t) -> p j t", j=8)), R=(pt,), W=(hT,))
                else:
                    tk.op("dve", lambda e: e.tensor_copy(out=hT[:, q * 8:(q + 1) * 8, tcol:tcol + 128],
                                                          in_=pt[:].rearrange("p (j t) -> p j t", j=8)), R=(pt,), W=(hT,))

        def sqnorm_a(src_t, sq_ring):
            sq_t = sq_ring.next()
            tk.op("act", lambda e: e.activation(out=sq_t[:, 0:1024], in_=src_t[:, 0:1024], func=AF.Square),
                  R=(src_t,), W=(sq_t,))
            tk.op("dve", lambda e: e.tensor_tensor(out=sq_t[:, 0:512], in0=sq_t[:, 0:512], in1=sq_t[:, 512:1024],
                                                    op=ALU.add), R=(sq_t,), W=(sq_t,))
            return sq_t

        def sqnorm_b(sq_t, nps, run_max):
            tk.mm(nps, [(nps[0:1, 0:512], onesb[:, 0:1], sq_t[:, 0:512], True, True, (sq_t, onesb))])
            tk.op("dve", lambda e: e.tensor_reduce(out=tmp1[:], in_=nps[0:1, 0:512], axis=AX.X, op=ALU.max),
                  R=(nps,), W=(tmp1,))
            tk.op("dve", lambda e: e.tensor_tensor(out=run_max[:], in0=run_max[:], in1=tmp1[:], op=ALU.max),
                  R=(tmp1, run_max), W=(run_max,))

        def fm_chunk(ws, hT, ntok, acc_ring, evac):
            for hh in range(ntok // 512):
                acc = acc_ring.next()
                tk.mm(acc, [(acc[:], ws[:, kc * 128:(kc + 1) * 128], hT[:, kc, hh * 512:(hh + 1) * 512],
                             kc == 0, kc == 31, (ws, hT)) for kc in range(32)])
                evac(acc, hh)

        def tm_tile(pcs, hT, j, acc):
            tk.mm(acc, [(acc[:], hT[:, kc, j * 128:(j + 1) * 128],
                         pcs[kc // 8][:, (kc % 8) * 512:(kc % 8 + 1) * 512],
                         kc == 0, kc == 31, (pcs[kc // 8], hT)) for kc in range(32)])

        P = Pool_(tk, nc)
        if STAGE >= 1:
            gb = P.sb([128, D], F32, dma=True)
            tk.dma("sp", gb[:], gmix_in, W=(gb,), owner=gb)
            hT = P.sb([128, 32, 1024], BF16)
            xs_ring = Ring([P.sb([128, D], F32, dma=True) for _ in range(2)])
            h1_ring = Ring([P.sb([128, D], BF16) for _ in range(2)])
            junk = P.sb([128, D], BF16)
            ss_ring = Ring([P.sb([128, 4], F32) for _ in range(4)])
            wslots = Ring([P.sb([128, 4096], BF16, dma=True) for _ in range(6)])
            kst_ring = Ring([P.sb([128, 1024], BF16, dma=True) for _ in range(2)])
            vst_ring = Ring([P.sb([128, 512], BF16, dma=True) for _ in range(3)])
            sq_ring = Ring([P.sb([128, 1024], BF16) for _ in range(2)])
            ptr_ring = Ring([P.ps([128, 1024], BF16) for _ in range(2)])
            acc_ring = Ring([P.ps([128, 512], F32) for _ in range(5)])
            nps = P.ps([128, 512], F32)
            pend_sq = None

            MASK = int(os.environ.get("K_P1MASK", "15"))
            NTB = int(os.environ.get("K_NTB", "4"))
            for tb in range(NTB):
                for j in range(8):
                    t0 = tb * 1024 + j * 128
                    rmsnorm_T(x_all[t0:t0 + 128, :], gb, xs_ring, h1_ring, junk, ptr_ring, hT, j * 128, ss_ring)
                for c in range(17 if MASK & 2 else 0):
                    ws = wslots.next()
                    tk.dma("pool", ws[:], wfm_kv[c], W=(ws,), owner=ws)
                    kst = kst_ring.next()
                    fm_chunk(ws, hT, 1024, acc_ring,
                             lambda acc, hh: tk.op("act", lambda e: e.copy(out=kst[:, hh * 512:(hh + 1) * 512], in_=acc[:]),
                                                   R=(acc,), W=(kst,)))
                    if pend_sq is not None:
                        sqnorm_b(pend_sq, nps, kmax)
                        pend_sq = None
                    if c < 16:
                        tk.dma("sp", KTs[c, :, tb * 1024:(tb + 1) * 1024], kst[:], R=(kst,), owner=kst)
                        pend_sq = sqnorm_a(kst, sq_ring)
                    else:
                        tk.dma("sp", kidxTs[:, tb * 1024:(tb + 1) * 1024], kst[:], R=(kst,), owner=kst)
                for cc in range(4 if MASK & 8 else 0):
                    pcs = []
                    for q in range(4):
                        ws = wslots.next()
                        tk.dma("pool", ws[:], wtm_v[cc, q], W=(ws,), owner=ws)
                        pcs.append(ws)
                    for j in range(8):
                        acc = acc_ring.next()
                        tm_tile(pcs, hT, j, acc)
                        vst = vst_ring.next()
                        tk.op("dve", lambda e: e.tensor_copy(out=vst[:], in_=acc[:]), R=(acc,), W=(vst,))
                        blk = tb * 8 + j
                        tk.dma("sp", Vs[cc * 4:(cc + 1) * 4, :, blk, :].rearrange("h p d -> p h d"),
                               vst[:].rearrange("p (h d) -> p h d", h=4), R=(vst,), owner=vst)
        P.close()

        M1 = Pool_(tk, nc)
        mix_gm = M1.sb([128, 16, NT * 128], BF16)
        wi = M1.sb([128, NT, 32], F32)

        P = Pool_(tk, nc)
        if STAGE >= 2:
            hT = P.sb([128, 32, 1024], BF16)
            acc_ring = Ring([P.ps([128, 512], F32) for _ in range(5)])
            nps = P.ps([128, 512], F32)
            WgT = P.sb([128, 2048], BF16)
            P2a = Pool_(tk, nc)
            gb = P2a.sb([128, D], F32, dma=True)
            tk.dma("sp", gb[:], gmix_in, W=(gb,), owner=gb)
            xs_ring = Ring([P2a.sb([128, D], F32, dma=True) for _ in range(1)])
            h1_ring = Ring([P2a.sb([128, D], BF16) for _ in range(1)])
            junk = P2a.sb([128, D], BF16)
            ss_ring = Ring([P2a.sb([128, 4], F32) for _ in range(4)])
            ptr_ring = Ring([P2a.ps([128, 1024], BF16) for _ in range(2)])
            wsT = xs_ring.tiles[0]
            tril = P2a.sb([128, 128], F32, dma=True)
            tk.dma("sp", tril[:], tril_in, W=(tril,), owner=tril)
            tk.dma("sp", wsT[:, 0:2048], wsT_in, W=(wsT,), owner=wsT)
            for g in range(16):
                tk.op("dve", lambda e: e.tensor_tensor(out=WgT[:, g * 128:(g + 1) * 128],
                                                        in0=wsT[:, g * 128:(g + 1) * 128], in1=tril[:], op=ALU.mult),
                      R=(wsT, tril), W=(WgT,))
            for j in range(NT):
                rmsnorm_T(x_own[j * 128:(j + 1) * 128, :], gb, xs_ring, h1_ring, junk, ptr_ring, hT, j * 128, ss_ring)
            P2a.close()
            wslots = Ring([P.sb([128, 4096], BF16, dma=True) for _ in range(6)])
            vgain = P.sb([128, 2048], F32, dma=True)
            bsb = P.sb([128, 2048], F32, dma=True)
            tk.dma("sp", vgain[:], vgain_in, W=(vgain,), owner=vgain)
            tk.dma("sp", bsb[:], bsb_in, W=(bsb,), owner=bsb)
            kst_ring = Ring([P.sb([128, 1024], BF16, dma=True) for _ in range(2)])
            sq_ring = Ring([P.sb([128, 1024], BF16) for _ in range(2)])
            pend_sq = None
            for c in range(48):
                ws = wslots.next()
                tk.dma("pool", ws[:], wfm_own[c], W=(ws,), owner=ws)
                kst = kst_ring.next()
                if c < 16:
                    fm_chunk(ws, hT, 1024, acc_ring,
                             lambda acc, hh: tk.op("act", lambda e: e.activation(out=kst[:, hh * 512:(hh + 1) * 512], in_=acc[:],
                                                                                 func=AF.Gelu_apprx_tanh), R=(acc,), W=(kst,)))
                else:
                    fm_chunk(ws, hT, 1024, acc_ring,
                             lambda acc, hh: tk.op("act", lambda e: e.copy(out=kst[:, hh * 512:(hh + 1) * 512], in_=acc[:]),
                                                   R=(acc,), W=(kst,)))
                tk.dma("sp", OWN[c], kst[:], R=(kst,), owner=kst)
                if pend_sq is not None:
                    sqnorm_b(pend_sq, nps, qmax)
                    pend_sq = None
                if 16 <= c < 32:
                    pend_sq = sqnorm_a(kst, sq_ring)
            wsw = wslots.next()
            tk.dma("pool", wsw[:, 0:1024], wtm_wi, W=(wsw,), owner=wsw)
            for j in range(NT):
                acc = acc_ring.next()
                tk.mm(acc, [(acc[:, 0:32], hT[:, kc, j * 128:(j + 1) * 128], wsw[:, kc * 32:(kc + 1) * 32],
                             kc == 0, kc == 31, (wsw, hT)) for kc in range(32)])
                tk.op("act", lambda e: e.mul(out=wi[:, j, :], in_=acc[:, 0:32], mul=float(32 ** -0.5 * 64 ** -0.5)),
                      R=(acc,), W=(wi,))
            for kt_ in kst_ring.tiles:
                for tok in list(kt_.r.values()):
                    tk.E["sp"].wait(tok)
            gl_ring = Ring([P.sb([128, 512], F32) for _ in range(2)])
            sq2_ring = Ring([P.sb([128, 512], F32) for _ in range(2)])
            st_ring = Ring([P.sb([128, 16], F32) for _ in range(2)])
            vn_ring = Ring([P.sb([128, 512], BF16) for _ in range(2)])
            zt_ring = Ring([P.sb([128, 512], F32) for _ in range(2)])
            gu_ring = Ring([P.sb([128, 4, 128], BF16, dma=True) for _ in range(3)])
            for cc in range(4):
                pcs = []
                for q in range(4):
                    ws = wslots.next()
                    tk.dma("pool", ws[:], wtm_vg[cc, q], W=(ws,), owner=ws)
                    pcs.append(ws)
                for j in range(NT):
                    gu = gu_ring.next()
                    tk.dma("sp", gu[:], OWN[cc * 4:(cc + 1) * 4, :, j * 128:(j + 1) * 128].rearrange("c p t -> p c t"),
                           W=(gu,), owner=gu)
                    acc = acc_ring.next()
                    tm_tile(pcs, hT, j, acc)
                    gl = gl_ring.next()
                    sq2 = sq2_ring.next()
                    st = st_ring.next()
                    vn = vn_ring.next()
                    tk.op("act", lambda e: e.activation(out=gl[:], in_=acc[:], func=AF.Gelu_apprx_tanh), R=(acc,), W=(gl,))
                    tk.op("act", lambda e: e.activation(out=sq2[:], in_=gl[:], func=AF.Square), R=(gl,), W=(sq2,))
                    tk.op("dve", lambda e: e.tensor_reduce(out=st[:, 0:4], in_=gl[:].rearrange("p (g c) -> p g c", g=4),
                                                            axis=AX.X, op=ALU.add), R=(gl,), W=(st,))
                    tk.op("dve", lambda e: e.tensor_reduce(out=st[:, 4:8], in_=sq2[:].rearrange("p (g c) -> p g c", g=4),
                                                            axis=AX.X, op=ALU.add), R=(sq2, st), W=(st,))
                    tk.op("dve", lambda e: e.tensor_scalar(out=st[:, 0:4], in0=st[:, 0:4], scalar1=1.0 / 128, scalar2=None,
                                                            op0=ALU.mult), R=(st,), W=(st,))
                    tk.op("dve", lambda e: e.tensor_tensor(out=st[:, 8:12], in0=st[:, 0:4], in1=st[:, 0:4], op=ALU.mult),
                          R=(st,), W=(st,))
                    tk.op("dve", lambda e: e.scalar_tensor_tensor(out=st[:, 12:16], in0=st[:, 4:8], scalar=1.0 / 128,
                                                                   in1=st[:, 8:12], op0=ALU.mult, op1=ALU.subtract),
                          R=(st,), W=(st,))
                    tk.op("act", lambda e: e.activation(out=st[:, 8:12], in_=st[:, 12:16], func=AF.Sqrt, bias=epsT[:, 0:1]),
                          R=(st, epsT), W=(st,))
                    tk.op("dve", lambda e: e.reciprocal(out=st[:, 12:16], in_=st[:, 8:12]), R=(st,), W=(st,))
                    for gi in range(4):
                        tk.op("dve", lambda e: e.tensor_scalar(out=gl[:, gi * 128:(gi + 1) * 128],
                                                                in0=gl[:, gi * 128:(gi + 1) * 128],
                                                                scalar1=st[:, gi:gi + 1], scalar2=st[:, 12 + gi:13 + gi],
                                                                op0=ALU.subtract, op1=ALU.mult), R=(gl, st), W=(gl,))
                    tk.op("dve", lambda e: e.tensor_tensor(out=vn[:], in0=gl[:], in1=vgain[:, cc * 512:(cc + 1) * 512],
                                                            op=ALU.mult), R=(gl, vgain), W=(vn,))
                    acc2 = acc_ring.next()
                    for gi in range(4):
                        g = cc * 4 + gi
                        tk.mm(acc2, [(acc2[:, gi * 128:(gi + 1) * 128], vn[:, gi * 128:(gi + 1) * 128],
                                      WgT[:, g * 128:(g + 1) * 128], True, True, (vn, WgT))], cont=(gi > 0))
                    zt = zt_ring.next()
                    tk.op("dve", lambda e: e.tensor_tensor(out=zt[:], in0=acc2[:], in1=bsb[:, cc * 512:(cc + 1) * 512],
                                                            op=ALU.add), R=(acc2, bsb), W=(zt,))
                    tk.op("dve", lambda e: e.tensor_tensor(
                        out=mix_gm[:, cc * 4:(cc + 1) * 4, j * 128:(j + 1) * 128],
                        in0=zt[:].rearrange("p (g t) -> p g t", g=4),
                        in1=gu[:], op=ALU.mult),
                        R=(zt, gu), W=(mix_gm,))
            tk.op("dve", lambda e: e.tensor_tensor(out=tmp1[:], in0=qmax[:], in1=kmax[:], op=ALU.mult),
                  R=(qmax, kmax), W=(tmp1,))
            tk.op("act", lambda e: e.activation(out=tmp1[:], in_=tmp1[:], func=AF.Sqrt), R=(tmp1,), W=(tmp1,))
            tk.op("dve", lambda e: e.tensor_scalar(out=tmp1[:], in0=tmp1[:], scalar1=-1.05 * 128 ** -0.5, scalar2=None,
                                                    op0=ALU.mult), R=(tmp1,), W=(tmp1,))
            tmpb = P.sb([1, 2], BF16)
            tk.op("dve", lambda e: e.tensor_copy(out=tmpb[:, 0:1], in_=tmp1[:]), R=(tmp1,), W=(tmpb,))
            tk.mm(nps, [(nps[:, 0:1], onesb[0:1, :], tmpb[0:1, 0:1], True, True, (onesb, tmpb))])
            tk.op("act", lambda e: e.copy(out=negb[:], in_=nps[:, 0:1]), R=(nps,), W=(negb,))
        P.close()

        M2 = Pool_(tk, nc)
        mix_att = M2.sb([128, 16, NT * 128], BF16)

        P = Pool_(tk, nc)
        if STAGE >= 3:
            kidxA = P.sb([128, L], BF16, dma=True)
            kidxB = P.sb([128, L], BF16, dma=True)
            tk.op("dve", lambda e: e.memset(kidxA[64:128, :], 0.0), W=(kidxA,))
            tk.op("dve", lambda e: e.memset(kidxB[0:64, :], 0.0), W=(kidxB,))
            tk.dma("sp", kidxA[0:64, :], kidxTs[0:64, :], W=(kidxA,), owner=kidxA)
            tk.dma("sp", kidxB[64:128, :], kidxTs[64:128, :], W=(kidxB,), owner=kidxB)
            qi_ring = Ring([P.sb([128, 16, 128], BF16, dma=True) for _ in range(1)])
            qt_ring = Ring([P.sb([128, 16, 128], BF16, dma=True) for _ in range(1)])
            Dg_ring = Ring([P.sb([128, 32, 128], BF16) for _ in range(1)])
            sc_ring = Ring([P.sb([128, L], F32, dma=True) for _ in range(2)])
            work = P.sb([128, L], F32)
            nm_in_ring = Ring([P.sb([128, 512], BF16, dma=True) for _ in range(2)])
            oh_ring = Ring([P.sb([128, 128], F32) for _ in range(3)])
            dh_ring = Ring([P.sb([128, 128], F32) for _ in range(3)])
            nm = P.sb([128, L], BF16, dma=True)
            nmT = P.sb([128, 32, 128], BF16)
            m8_ring = Ring([P.sb([128, 8], F32) for _ in range(2)])
            thr_ring = Ring([P.sb([128, 1], F32) for _ in range(2)])
            R_ring = Ring([P.sb([128, 512], BF16) for _ in range(4)])
            kt_ring = Ring([P.sb([128, L], BF16, dma=True) for _ in range(2)])
            v_ring = Ring([P.sb([128, 32, 128], BF16, dma=True) for _ in range(2)])
            pt_ring = Ring([P.sb([128, 512], BF16) for _ in range(3)])
            Lps_ring = Ring([P.ps([128, 512], F32) for _ in range(4)])
            dps_ring = Ring([P.ps([128, 512], F32) for _ in range(1)])
            accs_ring = Ring([P.ps([128, 512], F32) for _ in range(1)])
            sps_ring = Lps_ring
            ops_ring = Ring([P.ps([128, 512], F32) for _ in range(1)])
            tps_ring = Ring([P.ps([128, 1024], BF16) for _ in range(1)])

            def scores(m):
                nb = 4 * (m + 1)
                qi = qi_ring.next()
                tk.dma("sp", qi[:], OWN[32:48, :, m * 128:(m + 1) * 128].rearrange("c p t -> p c t"), W=(qi,), owner=qi)
                Dg = Dg_ring.next()
                for h in range(32):
                    tk.op("pool", lambda e: e.tensor_scalar(out=Dg[:, h, :], in0=identb[:], scalar1=wi[:, m, h:h + 1],
                                                             scalar2=None, op0=ALU.mult), R=(identb, wi), W=(Dg,))
                sc = sc_ring.next()
                for sg in range(nb // 4):
                    acc = accs_ring.next()
                    pend = []
                    LOOK = 2
                    last = (sg == nb // 4 - 1)
                    for h in range(32 + LOOK):
                        if h < 32:
                            lp = Lps_ring.next()
                            kx = kidxA if h % 2 == 0 else kidxB
                            tk.mm(lp, [(lp[:], qi[:, h // 2, :], kx[:, sg * 512:(sg + 1) * 512],
                                        True, True, (qi, kx))])
                            Rt = R_ring.next()
                            tk.op("act", lambda e: e.activation(out=Rt[:], in_=lp[:], func=AF.Relu), R=(lp,), W=(Rt,))
                            pend.append((h, Rt))
                        if h >= LOOK:
                            hh, Rr = pend.pop(0)
                            tk.mm(acc, [(acc[:], Dg[:, hh, :], Rr[:], hh == 0, hh == 31 and not last, (Dg, Rr))],
                                  cont=(hh > 0))
                    if sg == nb // 4 - 1:
                        nmi = nm_in_ring.next()
                        tk.dma("pool", nmi[:], negmask_in[m], W=(nmi,), owner=nmi)
                        tk.mm(acc, [(acc[:], identb[:], nmi[:], False, True, (identb, nmi))], cont=True)
                    tk.op("act", lambda e: e.copy(out=sc[:, sg * 512:(sg + 1) * 512], in_=acc[:]), R=(acc,), W=(sc,))
                return sc

            def topk(m, sc):
                nb = 4 * (m + 1)
                S = nb * 128
                m8 = None
                for rnd in range(32):
                    m8 = m8_ring.next()
                    src = sc if rnd == 0 else work
                    tk.op("dve", lambda e: e.max(out=m8[:], in_=src[:, 0:S]), R=(src,), W=(m8,))
                    if rnd < 31:
                        if rnd == 0:
                            tk.op("dve", lambda e: e.match_replace(out=work[:, 0:S], in_to_replace=m8[:],
                                                                    in_values=src[:, 0:S], imm_value=NEG),
                                  R=(m8, src), W=(work,))
                        else:
                            tk.op("dve", lambda e: e.match_replace(out=work[:, 0:S], in_to_replace=m8[:],
                                                                    in_values=work[:, 0:S], imm_value=NEG),
                                  R=(m8, work), W=(work,))
                thr = thr_ring.next()
                tk.op("dve", lambda e: e.tensor_scalar(out=thr[:], in0=m8[:, 7:8], scalar1=-1e29, scalar2=None, op0=ALU.max),
                      R=(m8,), W=(thr,))
                tk.op("dve", lambda e: e.tensor_scalar(out=nm[:, 0:S], in0=sc[:, 0:S], scalar1=thr[:, 0:1], scalar2=-30000.0,
                                                        op0=ALU.is_lt, op1=ALU.mult), R=(sc, thr), W=(nm,))
                if DEBUG:
                    tk.dma("sp", scdbg[m, :, 0:S], sc[:, 0:S], R=(sc,), owner=sc)
                    tk.dma("sp", nmdbg[m, :, 0:S], nm[:, 0:S], R=(nm,), owner=nm)

            def attention_pre(m):
                nb = 4 * (m + 1)
                for b0 in range(0, nb, 8):
                    nbb = min(8, nb - b0)
                    tp = tps_ring.next()
                    tk.tr(tp, [(tp[:, j * 128:(j + 1) * 128], nm[:, (b0 + j) * 128:(b0 + j + 1) * 128], identb[:], (nm, identb))
                               for j in range(nbb)])
                    tk.op("act", lambda e: e.copy(out=nmT[:, b0:b0 + nbb, :],
                                                   in_=tp[:, 0:nbb * 128].rearrange("p (j t) -> p j t", j=nbb)),
                          R=(tp,), W=(nmT,))

            def attention(m):
                nb = 4 * (m + 1)
                S = nb * 128
                qt = qt_ring.next()
                tk.dma("sp", qt[:], OWN[16:32, :, m * 128:(m + 1) * 128].rearrange("h p t -> p h t"), W=(qt,), owner=qt)
                for h in range(16):
                    kt = kt_ring.next()
                    tk.dma("sp", kt[:, 0:S], KTs[h, :, 0:S], W=(kt,), owner=kt)
                    vt = v_ring.next()
                    tk.dma("sp", vt[:, 0:nb, :], Vs[h, :, 0:nb, :], W=(vt,), owner=vt)
                    ops = ops_ring.next()
                    dps = dps_ring.next()
                    prev = None
                    nsg = nb // 4
                    for sg in range(nsg + 1):
                        if sg < nsg:
                            sp = sps_ring.next()
                            mms = []
                            for j in range(4):
                                blk = sg * 4 + j
                                mms.append((sp[:, j * 128:(j + 1) * 128], kt[:, blk * 128:(blk + 1) * 128], qt[:, h, :],
                                            True, False, (kt, qt)))
                                mms.append((sp[:, j * 128:(j + 1) * 128], identb[:], nmT[:, blk, :], False, True,
                                            (identb, nmT)))
                            tk.mm(sp, mms)
                            pt = pt_ring.next()
                            tk.op("act", lambda e: e.activation(out=pt[:], in_=sp[:], func=AF.Exp, bias=negb[:, 0:1],
                                                                 scale=float(128 ** -0.5)), R=(sp, negb), W=(pt,))
                        if prev is not None:
                            sg2, pt2 = prev
                            mms = []
                            mmd = []
                            for j in range(4):
                                blk = sg2 * 4 + j
                                mms.append((ops[:, 0:128], vt[:, blk, :], pt2[:, j * 128:(j + 1) * 128],
                                            blk == 0, blk == nb - 1, (vt, pt2)))
                                mmd.append((dps[:, 0:128], onesb[:], pt2[:, j * 128:(j + 1) * 128],
                                            blk == 0, blk == nb - 1, (onesb, pt2)))
                            tk.mm(ops, mms, cont=(sg2 > 0))
                            tk.mm(dps, mmd, cont=(sg2 > 0))
                        prev = (sg, pt) if sg < nsg else None
                    oh = oh_ring.next()
                    dh = dh_ring.next()
                    tk.op("act", lambda e: e.copy(out=oh[:], in_=ops[:, 0:128]), R=(ops,), W=(oh,))
                    tk.op("act", lambda e: e.activation(out=dh[:], in_=dps[:, 0:128], func=AF.Ln), R=(dps,), W=(dh,))
                    tk.op("act", lambda e: e.activation(out=dh[:], in_=dh[:], func=AF.Exp, scale=-1.0), R=(dh,), W=(dh,))
                    tk.op("pool", lambda e: e.tensor_tensor(out=mix_att[:, h, m * 128:(m + 1) * 128], in0=oh[:],
                                                             in1=dh[:], op=ALU.mult), R=(oh, dh), W=(mix_att,))

            sc_list = {0: scores(0), 1: scores(1)}
            topk(0, sc_list[0])
            for m in range(NT):
                attention_pre(m)
                if m + 1 < NT:
                    topk(m + 1, sc_list[m + 1])
                attention(m)
                if m + 2 < NT:
                    sc_list[m + 2] = scores(m + 2)
        P.close()

        if DEBUG and STAGE >= 3:
            Dp = Pool_(tk, nc)
            own = Dp.sb([1, 1], F32, dma=True)
            tk.dma("sp", mixdbg[:, 0:16, :], mix_att[:], R=(mix_att,), owner=own)
            tk.dma("sp", mixdbg[:, 16:32, :], mix_gm[:], R=(mix_gm,), owner=own)
            Dp.close()

        P = Pool_(tk, nc)
        if STAGE >= 4:
            wslots = Ring([P.sb([128, 4096], BF16, dma=True) for _ in range(8)])
            xr_ring = Ring([P.sb([128, 512], F32, dma=True) for _ in range(3)])
            x1_ring = Ring([P.sb([128, 512], F32, dma=True) for _ in range(3)])
            acc_ring = Ring([P.ps([128, 512], F32) for _ in range(6)])
            for ncn in range(8):
                pcs = []
                for q in range(4):
                    ws = wslots.next()
                    tk.dma("pool", ws[:], wtm_out[ncn, q], W=(ws,), owner=ws)
                    pcs.append(ws)
                for j in range(NT):
                    xr = xr_ring.next()
                    tk.dma("sp", xr[:], x_own[j * 128:(j + 1) * 128, ncn * 512:(ncn + 1) * 512], W=(xr,), owner=xr)
                    acc = acc_ring.next()
                    mms = []
                    for kc in range(32):
                        src = mix_att if kc < 16 else mix_gm
                        mms.append((acc[:], src[:, kc % 16, j * 128:(j + 1) * 128],
                                    pcs[kc // 8][:, (kc % 8) * 512:(kc % 8 + 1) * 512], kc == 0, kc == 31, (pcs[kc // 8], src)))
                    tk.mm(acc, mms)
                    x1t = x1_ring.next()
                    tk.op("dve", lambda e: e.tensor_tensor(out=x1t[:], in0=acc[:], in1=xr[:], op=ALU.add),
                          R=(acc, xr), W=(x1t,))
                    tk.dma("sp", x1s[j * 128:(j + 1) * 128, ncn * 512:(ncn + 1) * 512], x1t[:], R=(x1t,), owner=x1t)
        P.close()
        M2.close()
        M1.close()

        for half in range(2 if STAGE >= 5 else 0):
            P = Pool_(tk, nc)
            AT = P.sb([128, NFC, 512], BF16)
            Pg = Pool_(tk, nc)
            h2T = Pg.sb([128, 32, 512], BF16)
            Pn = Pool_(tk, nc)
            gb = Pn.sb([128, D], F32, dma=True)
            tk.dma("sp", gb[:], gffn_in, W=(gb,), owner=gb)
            xs_ring = Ring([Pn.sb([128, D], F32, dma=True) for _ in range(1)])
            h1_ring = Ring([Pn.sb([128, D], BF16) for _ in range(1)])
            junk = Pn.sb([128, D], BF16)
            ss_ring = Ring([Pn.sb([128, 4], F32) for _ in range(4)])
            ptr_ring = Ring([Pn.ps([128, 1024], BF16) for _ in range(2)])
            for tt in range(4):
                r0 = (half * 4 + tt) * 128
                rmsnorm_T(x1s[r0:r0 + 128, :], gb, xs_ring, h1_ring, junk, ptr_ring, h2T, tt * 128, ss_ring)
            Pn.close()
            wslots = Ring([Pg.sb([128, 4096], BF16, dma=True) for _ in range(6)])
            sg_ring = Ring([Pg.sb([128, 512], F32) for _ in range(2)])
            acc_ring = Ring([Pg.ps([128, 512], F32) for _ in range(6)])
            for fc in range(NFC):
                wg = wslots.next()
                tk.dma("pool", wg[:], wfm_g[fc], W=(wg,), owner=wg)
                wu = wslots.next()
                tk.dma("pool", wu[:], wfm_u[fc], W=(wu,), owner=wu)
                ag = acc_ring.next()
                tk.mm(ag, [(ag[:], wg[:, kc * 128:(kc + 1) * 128], h2T[:, kc, :], kc == 0, kc == 31, (wg, h2T)) for kc in range(32)])
                au = acc_ring.next()
                tk.mm(au, [(au[:], wu[:, kc * 128:(kc + 1) * 128], h2T[:, kc, :], kc == 0, kc == 31, (wu, h2T)) for kc in range(32)])
                sgt = sg_ring.next()
                tk.op("act", lambda e: e.activation(out=sgt[:], in_=ag[:], func=AF.Silu), R=(ag,), W=(sgt,))
                tk.op("dve", lambda e: e.tensor_tensor(out=AT[:, fc, :], in0=au[:], in1=sgt[:], op=ALU.mult),
                      R=(au, sgt), W=(AT,))
            Pg.close()
            Pd = Pool_(tk, nc)
            x2 = Pd.sb([128, 4, D], F32, dma=True)
            r0 = half * 512
            for tt in range(4):
                tk.dma("sp", x2[:, tt, :], x1s[r0 + tt * 128:r0 + (tt + 1) * 128, :], W=(x2,), owner=x2)
            wdslots = Ring([Pd.sb([128, 43 * 128], BF16, dma=True) for _ in range(4)])
            ys_ring = Ring([Pd.sb([128, 512], F32) for _ in range(2)])
            gfq_ring = Ring([Pd.sb([128, 1024], F32, dma=True) for _ in range(1)])
            yacc_ring = Ring([Pd.ps([128, 512], F32) for _ in range(4)])
            ytr_ring = Ring([Pd.ps([128, 512], F32) for _ in range(3)])
            ss_ring = Ring([Pd.sb([128, 4], F32) for _ in range(4)])
            for ncn in range(32):
                w0 = wdslots.next()
                tk.dma("pool", w0[:], wd_in[ncn, 0], W=(w0,), owner=w0)
                w1 = wdslots.next()
                tk.dma("pool", w1[:], wd_in[ncn, 1], W=(w1,), owner=w1)
                ya = yacc_ring.next()
                mms = []
                for fc in range(NFC):
                    wsrc = w0 if fc < 43 else w1
                    mms.append((ya[:], wsrc[:, (fc % 43) * 128:(fc % 43 + 1) * 128], AT[:, fc, :], fc == 0, fc == NFC - 1,
                                (wsrc, AT)))
                tk.mm(ya, mms)
                ys = ys_ring.next()
                tk.op("act", lambda e: e.copy(out=ys[:], in_=ya[:]), R=(ya,), W=(ys,))
                yt = ytr_ring.next()
                tk.tr(yt, [(yt[:, tt * 128:(tt + 1) * 128], ys[:, tt * 128:(tt + 1) * 128], identf[:], (ys, identf))
                           for tt in range(4)])
                tk.op("dve", lambda e: e.tensor_tensor(out=x2[:, :, ncn * 128:(ncn + 1) * 128],
                                                        in0=yt[:].rearrange("p (t n) -> p t n", t=4),
                                                        in1=x2[:, :, ncn * 128:(ncn + 1) * 128], op=ALU.add),
                      R=(yt, x2), W=(x2,))
            for tt in range(4):
                ss = ss_ring.next()
                tk.op("act", lambda e: e.activation(out=AT[:, 0:8, :].rearrange("p a b -> p (a b)"), in_=x2[:, tt, :],
                                                     func=AF.Square, scale=1.0 / 64.0, accum_out=ss[:, 0:1]),
                      R=(x2,), W=(AT, ss))
                tk.op("act", lambda e: e.activation(out=ss[:, 1:2], in_=ss[:, 0:1], func=AF.Sqrt, bias=epsT[:, 0:1]),
                      R=(ss, epsT), W=(ss,))
                tk.op("dve", lambda e: e.reciprocal(out=ss[:, 2:3], in_=ss[:, 1:2]), R=(ss,), W=(ss,))
                for q in range(4):
                    gfq = gfq_ring.next()
                    tk.dma("sp", gfq[:], gfin_in[:, q * 1024:(q + 1) * 1024], W=(gfq,), owner=gfq)
                    tk.op("dve", lambda e: e.scalar_tensor_tensor(out=x2[:, tt, q * 1024:(q + 1) * 1024],
                                                                   in0=x2[:, tt, q * 1024:(q + 1) * 1024], scalar=ss[:, 2:3],
                                                                   in1=gfq[:], op0=ALU.mult, op1=ALU.mult),
                          R=(x2, ss, gfq), W=(x2,))
            for tt in range(4):
                tk.dma("sp", out[r0 + tt * 128:r0 + (tt + 1) * 128, :], x2[:, tt, :], R=(x2,), owner=x2)
            Pd.close()
            P.close()
        G.close()
        tk.barrier()
    return nc


def _fm(Wc):
    n = Wc.shape[1]
    a = Wc.reshape(32, 128, n // 128, 128).transpose(2, 1, 0, 3)
    return np.ascontiguousarray(a).reshape(n // 128, 128, 32 * 128)


def _tm(Wc):
    n = Wc.shape[1]
    a = Wc.reshape(4, 8, 128, n // 512, 512).transpose(3, 0, 2, 1, 4)
    return np.ascontiguousarray(a).reshape(n // 512, 4, 128, 8 * 512)


def own_tiles(r):
    return [8 * (m // 2) + (r if m % 2 == 0 else 7 - r) for m in range(NT)]


_CACHE = {}


def kernel(x, norm_mix, w_in, gmlp_v_gain, w_spatial, b_spatial, w_out, norm_ffn, w_gate, w_up, w_down, norm_final):
    x = np.asarray(x, dtype=np.float32)
    w_in = np.asarray(w_in, dtype=np.float32)[0]
    w_out = np.asarray(w_out, dtype=np.float32)[0]
    w_gate = np.asarray(w_gate, dtype=np.float32)[0]
    w_up = np.asarray(w_up, dtype=np.float32)[0]
    w_down = np.asarray(w_down, dtype=np.float32)[0]
    f32 = np.float32
    shared = {}
    kcols = w_in[:, 2048:4096]
    kidxc = w_in[:, 8192:8256]
    shared["wfm_kv"] = np.concatenate([_fm(kcols), _fm(np.concatenate([kidxc, kidxc], axis=1))], axis=0)
    shared["wfm_own"] = np.concatenate([_fm(w_in[:, 8288:10336]), _fm(w_in[:, 0:2048]), _fm(w_in[:, 6144:8192])], axis=0)
    shared["wtm_v"] = _tm(w_in[:, 4096:6144])
    shared["wtm_vg"] = _tm(w_in[:, 10336:12384])
    shared["wtm_wi"] = np.ascontiguousarray(w_in[:, 8256:8288].reshape(32, 128, 32).transpose(1, 0, 2)).reshape(128, 32 * 32)
    shared["wtm_out"] = _tm(w_out)
    shared["wfm_g"] = _fm(w_gate)
    shared["wfm_u"] = _fm(w_up)
    wd = w_down.reshape(2, 43, 128, 32, 128).transpose(3, 0, 2, 1, 4)
    shared["wd"] = np.ascontiguousarray(wd).reshape(32, 2, 128, 43 * 128)
    shared["ident"] = np.eye(128, dtype=f32)
    shared["tril_st"] = np.triu(np.ones((128, 128), dtype=f32))
    shared["gmix_b"] = np.ascontiguousarray(np.broadcast_to(np.asarray(norm_mix, f32)[0][None, :], (128, D)))
    shared["gffn_b"] = np.ascontiguousarray(np.broadcast_to(np.asarray(norm_ffn, f32)[0][None, :], (128, D)))
    shared["gfin_b"] = np.ascontiguousarray(np.broadcast_to(np.asarray(norm_final, f32)[None, :], (128, D)))
    shared["vgain_b"] = np.ascontiguousarray(np.broadcast_to(np.asarray(gmlp_v_gain, f32)[0].reshape(1, 2048), (128, 2048)))
    shared["bsb"] = np.ascontiguousarray(np.broadcast_to(np.asarray(b_spatial, f32)[0].reshape(1, 2048), (128, 2048)))
    shared["wsT"] = np.ascontiguousarray(np.asarray(w_spatial, f32)[0].transpose(2, 0, 1)).reshape(128, 2048)

    in_maps = []
    for c in range(8):
        b, r = c // 4, c % 4
        tiles = own_tiles(r)
        xb = x[b]
        m = dict(shared)
        m["x_all"] = xb
        m["x_own"] = np.ascontiguousarray(np.concatenate([xb[t * 128:(t + 1) * 128] for t in tiles], axis=0))
        nmk = np.zeros((NT, 128, 512), dtype=f32)
        for mi, t in enumerate(tiles):
            nb = 4 * (mi + 1)
            qpos = t * 128 + np.arange(128)[:, None]
            kpos = (nb - 4) * 128 + np.arange(512)[None, :]
            nmk[mi] = np.where(kpos <= qpos, 0.0, NEG).astype(f32)
        m["negmask"] = nmk
        in_maps.append(m)

    if "nc" not in _CACHE:
        _CACHE["nc"] = build_program()
    nc = _CACHE["nc"]
    ncores = int(os.environ.get("K_NCORES", "8"))
    if ncores < 8:
        res = run_bass_kernel_spmd(nc, in_maps[:ncores], core_ids=list(range(ncores)))
        _CACHE["res"] = res
        return None
    res = run_bass_kernel_spmd(nc, in_maps, core_ids=list(range(8)))
    outp = np.zeros((2, L, D), dtype=np.float32)
    for c in range(8):
        b, r = c // 4, c % 4
        o = res.results[c]["out"]
        for mi, t in enumerate(own_tiles(r)):
            outp[b, t * 128:(t + 1) * 128] = o[mi * 128:(mi + 1) * 128]
    if DEBUG:
        _CACHE["res"] = res
    return outp
```

```python
import os
from contextlib import ExitStack
import numpy as np
import concourse.bass as bass
import concourse.mybir as mybir
from concourse.bass_utils import run_bass_kernel_spmd

F32 = mybir.dt.float32
BF16 = mybir.dt.bfloat16
AF = mybir.ActivationFunctionType
ALU = mybir.AluOpType
AX = mybir.AxisListType

D = 4096
L = 4096
DFF = 11008
NFC = 86
NT = 8
EPS = 1e-6
NEG = -1e30
STAGE = int(os.environ.get("K_STAGE", "99"))
DEBUG = int(os.environ.get("K_DEBUG", "0"))


class Eng:
    def __init__(self, name, e, sem):
        self.name, self.e, self.sem = name, e, sem
        self.cnt = 0
        self.waited = {}

    def wait(self, tok):
        if tok is None:
            return
        key, sem, val = tok
        if self.waited.get(key, 0) >= val:
            return
        self.e.wait_ge(sem, val)
        self.waited[key] = val

    def sig(self, ins):
        self.cnt += 1
        ins.then_inc(self.sem, 1)
        return (self.name, self.sem, self.cnt)


class T:
    _n = 0

    def __init__(self, ap, dsem=None):
        self.ap = ap
        self.w = None
        self.r = {}
        self.dsem = dsem
        self.dcnt = 0
        T._n += 1
        self.name = "t%d" % T._n

    def __getitem__(self, idx):
        return self.ap[idx]


class Tk:
    def __init__(self, nc, es):
        self.nc = nc
        self.es = es
        self.E = {}
        for name, e in (("pe", nc.tensor), ("act", nc.scalar), ("dve", nc.vector),
                        ("pool", nc.gpsimd), ("sp", nc.sync)):
            sem = es.enter_context(nc.semaphore("s_" + name))
            self.E[name] = Eng(name, e, sem)
        self.dma_latest = {}
        self.free_dsems = {"sp": [], "pool": []}
        self.nsem = 0

    def dsem(self, kind):
        if self.free_dsems[kind]:
            return self.free_dsems[kind].pop()
        self.nsem += 1
        return [self.es.enter_context(self.nc.semaphore("d%d" % self.nsem)), 0, "d%d" % self.nsem]

    def _pre(self, E, en, R, W):
        for t in R:
            E.wait(t.w)
        for t in W:
            E.wait(t.w)
            for k, tok in t.r.items():
                E.wait(tok)

    def op(self, en, fn, R=(), W=()):
        E = self.E[en]
        self._pre(E, en, R, W)
        ins = fn(E.e)
        tok = E.sig(ins)
        for t in R:
            t.r[en] = tok
        for t in W:
            t.w = tok
            t.r = {}
        return tok

    def mm(self, out_t, mms, cont=False):
        E = self.E["pe"]
        if not cont:
            self._pre(E, "pe", (), (out_t,))
        allR = []
        ins = None
        for (o, l, r, st, sp, R) in mms:
            for t in R:
                E.wait(t.w)
                allR.append(t)
            ins = E.e.matmul(o, lhsT=l, rhs=r, start=st, stop=sp)
        tok = E.sig(ins)
        for t in allR:
            t.r["pe"] = tok
        out_t.w = tok
        if not cont:
            out_t.r = {}
        return tok

    def tr(self, out_t, trs):
        E = self.E["pe"]
        self._pre(E, "pe", (), (out_t,))
        allR = []
        ins = None
        for (o, i, idn, R) in trs:
            for t in R:
                E.wait(t.w)
                allR.append(t)
            ins = E.e.transpose(o, i, idn)
        tok = E.sig(ins)
        for t in allR:
            t.r["pe"] = tok
        out_t.w = tok
        out_t.r = {}
        return tok

    def dma(self, qn, out_ap, in_ap, R=(), W=(), owner=None):
        Q = self.E[qn]
        self._pre(Q, "dma", R, W)
        ins = Q.e.dma_start(out=out_ap, in_=in_ap)
        if owner.dsem is None:
            owner.dsem = self.dsem(qn)
            owner.dkind = qn
            owner.pool.ds.append((qn, owner.dsem))
        assert owner.dkind == qn, "tile DMA semaphore is bound to one queue kind"
        ds = owner.dsem
        ds[1] += 16
        ins.then_inc(ds[0], 16)
        tok = (ds[2], ds[0], ds[1])
        for t in R:
            t.r["dma" + ds[2]] = tok
        for t in W:
            t.w = tok
            t.r = {}
        self.dma_latest[ds[2]] = tok
        return tok

    def barrier(self):
        sp = self.E["sp"]
        for tok in self.dma_latest.values():
            sp.wait(tok)
        toks = []
        for en, E in self.E.items():
            if en == "sp":
                continue
            if E.cnt > 0:
                E.e.wait_ge(E.sem, E.cnt)
            ins = E.e.sem_inc(E.sem, 1)
            E.cnt += 1
            toks.append((E.name, E.sem, E.cnt))
        for tok in toks:
            sp.wait(tok)
        if sp.cnt > 0:
            sp.e.wait_ge(sp.sem, sp.cnt)
        ins = sp.e.sem_inc(sp.sem, 1)
        sp.cnt += 1
        stok = (sp.name, sp.sem, sp.cnt)
        for en, E in self.E.items():
            if en != "sp":
                E.wait(stok)


class Pool_:
    cnt = 0

    def __init__(self, tk, nc):
        self.tk, self.nc = tk, nc
        self.es = ExitStack()
        self.ds = []
        self.n = 0

    def sb(self, shape, dt, dma=False, name=None):
        Pool_.cnt += 1
        h = self.es.enter_context(self.nc.sbuf_tensor("sb%d" % Pool_.cnt, shape, dt))
        t = T(h[:], None)
        t.pool = self
        return t

    def ps(self, shape, dt):
        Pool_.cnt += 1
        h = self.es.enter_context(self.nc.psum_tensor("ps%d" % Pool_.cnt, shape, dt))
        return T(h[:])

    def close(self):
        self.tk.barrier()
        for kind, ds in self.ds:
            self.tk.free_dsems[kind].append(ds)
        self.es.close()


class Ring:
    def __init__(self, tiles):
        self.tiles = tiles
        self.i = 0

    def next(self):
        t = self.tiles[self.i % len(self.tiles)]
        self.i += 1
        return t


def build_program():
    nc = bass.Bass("TRN2", target_bir_lowering=False)
    dk = "ExternalOutput" if DEBUG else "Internal"

    def din(name, shape, dt=F32):
        return nc.dram_tensor(name, shape, dt, kind="ExternalInput").ap()

    x_all = din("x_all", [L, D])
    x_own = din("x_own", [NT * 128, D])
    negmask_in = din("negmask", [NT, 128, 512])
    ident_in = din("ident", [128, 128])
    tril_in = din("tril_st", [128, 128])
    gmix_in = din("gmix_b", [128, D])
    gffn_in = din("gffn_b", [128, D])
    gfin_in = din("gfin_b", [128, D])
    vgain_in = din("vgain_b", [128, 2048])
    bsb_in = din("bsb", [128, 16 * 128])
    wsT_in = din("wsT", [128, 16 * 128])
    wfm_kv = din("wfm_kv", [17, 128, 32 * 128])
    wfm_own = din("wfm_own", [48, 128, 32 * 128])
    wtm_v = din("wtm_v", [4, 4, 128, 8 * 512])
    wtm_vg = din("wtm_vg", [4, 4, 128, 8 * 512])
    wtm_wi = din("wtm_wi", [128, 32 * 32])
    wtm_out = din("wtm_out", [8, 4, 128, 8 * 512])
    wfm_g = din("wfm_g", [NFC, 128, 32 * 128])
    wfm_u = din("wfm_u", [NFC, 128, 32 * 128])
    wd_in = din("wd", [32, 2, 128, 43 * 128])
    out = nc.dram_tensor("out", [NT * 128, D], F32, kind="ExternalOutput").ap()

    KTs = nc.dram_tensor("KTs", [16, 128, L], BF16, kind=dk).ap()
    kidxTs = nc.dram_tensor("kidxTs", [128, L], BF16, kind=dk).ap()
    Vs = nc.dram_tensor("Vs", [16, 128, 32, 128], BF16, kind=dk).ap()
    OWN = nc.dram_tensor("OWNs", [48, 128, NT * 128], BF16, kind=dk).ap()
    x1s = nc.dram_tensor("x1s", [NT * 128, D], F32, kind=dk).ap()
    if DEBUG:
        mixdbg = nc.dram_tensor("mixdbg", [128, 32, NT * 128], BF16, kind="ExternalOutput").ap()
        scdbg = nc.dram_tensor("scdbg", [NT, 128, L], F32, kind="ExternalOutput").ap()
        nmdbg = nc.dram_tensor("nmdbg", [NT, 128, L], BF16, kind="ExternalOutput").ap()

    with ExitStack() as ges:
        tk = Tk(nc, ges)
        G = Pool_(tk, nc)
        identf = G.sb([128, 128], F32, dma=True)
        identb = G.sb([128, 128], BF16, dma=True)
        onesb = G.sb([128, 128], BF16)
        kmax = G.sb([1, 1], F32)
        qmax = G.sb([1, 1], F32)
        negb = G.sb([128, 1], F32)
        tmp1 = G.sb([1, 1], F32)
        epsT = G.sb([128, 1], F32)

        tk.dma("sp", identf[:], ident_in, W=(identf,), owner=identf)
        tk.dma("pool", identb[:], ident_in, W=(identb,), owner=identb)
        tk.op("dve", lambda e: e.memset(onesb[:], 1.0), W=(onesb,))
        tk.op("dve", lambda e: e.memset(kmax[:], 0.0), W=(kmax,))
        tk.op("dve", lambda e: e.memset(qmax[:], 0.0), W=(qmax,))
        tk.op("dve", lambda e: e.memset(epsT[:], EPS), W=(epsT,))

        def norm_stats(x_src_ap, xs_ring, junk, ss_ring):
            xs = xs_ring.next()
            tk.dma("sp", xs[:], x_src_ap, W=(xs,), owner=xs)
            ss = ss_ring.next()
            tk.op("act", lambda e: e.activation(out=junk[:], in_=xs[:], func=AF.Square, scale=1.0 / 64.0,
                                                 accum_out=ss[:, 0:1]), R=(xs,), W=(junk, ss))
            tk.op("act", lambda e: e.activation(out=ss[:, 1:2], in_=ss[:, 0:1], func=AF.Sqrt, bias=epsT[:, 0:1]),
                  R=(ss, epsT), W=(ss,))
            tk.op("dve", lambda e: e.reciprocal(out=ss[:, 2:3], in_=ss[:, 1:2]), R=(ss,), W=(ss,))
            return xs, ss

        def norm_evac(xs, ss, gb, h1_ring, ptr_ring, hT, tcol):
            h1 = h1_ring.next()
            tk.op("dve", lambda e: e.scalar_tensor_tensor(out=h1[:], in0=xs[:], scalar=ss[:, 2:3], in1=gb[:],
                                                            op0=ALU.mult, op1=ALU.mult), R=(xs, ss, gb), W=(h1,))
            for q in range(4):
                pt = ptr_ring.next()
                tk.tr(pt, [(pt[:, j * 128:(j + 1) * 128], h1[:, (q * 8 + j) * 128:(q * 8 + j + 1) * 128], identb[:],
                            (h1, identb)) for j in range(8)])
                if q % 2 == 0:
                    tk.op("act", lambda e: e.copy(out=hT[:, q * 8:(q + 1) * 8, tcol:tcol + 128],
                                                   in_=pt[:].rearrange("p (j t) -> p j t", j=8)), R=(pt,), W=(hT,))
                else:
                    tk.op("dve", lambda e: e.tensor_copy(out=hT[:, q * 8:(q + 1) * 8, tcol:tcol + 128],
                                                          in_=pt[:].rearrange("p (j t) -> p j t", j=8)), R=(pt,), W=(hT,))

        def rmsnorm_tiles(srcs, gb, xs_ring, h1_ring, junk, ptr_ring, hT, ss_ring):
            assert len(xs_ring.tiles) >= 2
            pend = norm_stats(srcs[0], xs_ring, junk, ss_ring)
            for j in range(len(srcs)):
                nxt = norm_stats(srcs[j + 1], xs_ring, junk, ss_ring) if j + 1 < len(srcs) else None
                norm_evac(pend[0], pend[1], gb, h1_ring, ptr_ring, hT, j * 128)
                pend = nxt

        def sqnorm_a(src_t, sq_ring):
            sq_t = sq_ring.next()
            tk.op("act", lambda e: e.activation(out=sq_t[:, 0:1024], in_=src_t[:, 0:1024], func=AF.Square),
                  R=(src_t,), W=(sq_t,))
            tk.op("dve", lambda e: e.tensor_tensor(out=sq_t[:, 0:512], in0=sq_t[:, 0:512], in1=sq_t[:, 512:1024],
                                                    op=ALU.add), R=(sq_t,), W=(sq_t,))
            return sq_t

        def sqnorm_b(sq_t, nps, run_max):
            tk.mm(nps, [(nps[0:1, 0:512], onesb[:, 0:1], sq_t[:, 0:512], True, True, (sq_t, onesb))])
            tk.op("dve", lambda e: e.tensor_reduce(out=tmp1[:], in_=nps[0:1, 0:512], axis=AX.X, op=ALU.max),
                  R=(nps,), W=(tmp1,))
            tk.op("dve", lambda e: e.tensor_tensor(out=run_max[:], in0=run_max[:], in1=tmp1[:], op=ALU.max),
                  R=(tmp1, run_max), W=(run_max,))

        def fm_chunk(ws, hT, ntok, acc_ring, evac):
            for hh in range(ntok // 512):
                acc = acc_ring.next()
                tk.mm(acc, [(acc[:], ws[:, kc * 128:(kc + 1) * 128], hT[:, kc, hh * 512:(hh + 1) * 512],
                             kc == 0, kc == 31, (ws, hT)) for kc in range(32)])
                evac(acc, hh)

        def tm_tile(pcs, hT, j, acc):
            tk.mm(acc, [(acc[:], hT[:, kc, j * 128:(j + 1) * 128],
                         pcs[kc // 8][:, (kc % 8) * 512:(kc % 8 + 1) * 512],
                         kc == 0, kc == 31, (pcs[kc // 8], hT)) for kc in range(32)])

        P = Pool_(tk, nc)
        if STAGE >= 1:
            gb = P.sb([128, D], F32, dma=True)
            tk.dma("sp", gb[:], gmix_in, W=(gb,), owner=gb)
            hT = P.sb([128, 32, 1024], BF16)
            xs_ring = Ring([P.sb([128, D], F32, dma=True) for _ in range(2)])
            h1_ring = Ring([P.sb([128, D], BF16) for _ in range(2)])
            junk = P.sb([128, D], BF16)
            ss_ring = Ring([P.sb([128, 4], F32) for _ in range(4)])
            wslots = Ring([P.sb([128, 4096], BF16, dma=True) for _ in range(6)])
            kst_ring = Ring([P.sb([128, 1024], BF16, dma=True) for _ in range(2)])
            vst_ring = Ring([P.sb([128, 512], BF16, dma=True) for _ in range(3)])
            sq_ring = Ring([P.sb([128, 1024], BF16) for _ in range(2)])
            ptr_ring = Ring([P.ps([128, 1024], BF16) for _ in range(2)])
            acc_ring = Ring([P.ps([128, 512], F32) for _ in range(5)])
            nps = P.ps([128, 512], F32)
            pend_sq = None

            MASK = int(os.environ.get("K_P1MASK", "15"))
            NTB = int(os.environ.get("K_NTB", "4"))
            for tb in range(NTB):
                rmsnorm_tiles([x_all[tb * 1024 + j * 128:tb * 1024 + (j + 1) * 128, :] for j in range(8)],
                              gb, xs_ring, h1_ring, junk, ptr_ring, hT, ss_ring)
                for c in range(17 if MASK & 2 else 0):
                    ws = wslots.next()
                    tk.dma("pool", ws[:], wfm_kv[c], W=(ws,), owner=ws)
                    kst = kst_ring.next()
                    fm_chunk(ws, hT, 1024, acc_ring,
                             lambda acc, hh: tk.op("act", lambda e: e.copy(out=kst[:, hh * 512:(hh + 1) * 512], in_=acc[:]),
                                                   R=(acc,), W=(kst,)))
                    if pend_sq is not None:
                        sqnorm_b(pend_sq, nps, kmax)
                        pend_sq = None
                    if c < 16:
                        tk.dma("sp", KTs[c, :, tb * 1024:(tb + 1) * 1024], kst[:], R=(kst,), owner=kst)
                        pend_sq = sqnorm_a(kst, sq_ring)
                    else:
                        tk.dma("sp", kidxTs[:, tb * 1024:(tb + 1) * 1024], kst[:], R=(kst,), owner=kst)
                for cc in range(4 if MASK & 8 else 0):
                    pcs = []
                    for q in range(4):
                        ws = wslots.next()
                        tk.dma("pool", ws[:], wtm_v[cc, q], W=(ws,), owner=ws)
                        pcs.append(ws)
                    for j in range(8):
                        acc = acc_ring.next()
                        tm_tile(pcs, hT, j, acc)
                        vst = vst_ring.next()
                        tk.op("dve", lambda e: e.tensor_copy(out=vst[:], in_=acc[:]), R=(acc,), W=(vst,))
                        blk = tb * 8 + j
                        tk.dma("sp", Vs[cc * 4:(cc + 1) * 4, :, blk, :].rearrange("h p d -> p h d"),
                               vst[:].rearrange("p (h d) -> p h d", h=4), R=(vst,), owner=vst)
        P.close()

        M1 = Pool_(tk, nc)
        mix_gm = M1.sb([128, 16, NT * 128], BF16)
        wi = M1.sb([128, NT, 32], F32)

        P = Pool_(tk, nc)
        if STAGE >= 2:
            hT = P.sb([128, 32, 1024], BF16)
            acc_ring = Ring([P.ps([128, 512], F32) for _ in range(5)])
            nps = P.ps([128, 512], F32)
            WgT = P.sb([128, 2048], BF16)
            P2a = Pool_(tk, nc)
            gb = P2a.sb([128, D], F32, dma=True)
            tk.dma("sp", gb[:], gmix_in, W=(gb,), owner=gb)
            xs_ring = Ring([P2a.sb([128, D], F32, dma=True) for _ in range(2)])
            h1_ring = Ring([P2a.sb([128, D], BF16) for _ in range(1)])
            junk = P2a.sb([128, D], BF16)
            ss_ring = Ring([P2a.sb([128, 4], F32) for _ in range(4)])
            ptr_ring = Ring([P2a.ps([128, 1024], BF16) for _ in range(2)])
            wsT = xs_ring.tiles[0]
            tril = P2a.sb([128, 128], F32, dma=True)
            tk.dma("sp", tril[:], tril_in, W=(tril,), owner=tril)
            tk.dma("sp", wsT[:, 0:2048], wsT_in, W=(wsT,), owner=wsT)
            for g in range(16):
                tk.op("dve", lambda e: e.tensor_tensor(out=WgT[:, g * 128:(g + 1) * 128],
                                                        in0=wsT[:, g * 128:(g + 1) * 128], in1=tril[:], op=ALU.mult),
                      R=(wsT, tril), W=(WgT,))
            rmsnorm_tiles([x_own[j * 128:(j + 1) * 128, :] for j in range(NT)],
                          gb, xs_ring, h1_ring, junk, ptr_ring, hT, ss_ring)
            P2a.close()
            wslots = Ring([P.sb([128, 4096], BF16, dma=True) for _ in range(6)])
            vgain = P.sb([128, 2048], F32, dma=True)
            bsb = P.sb([128, 2048], F32, dma=True)
            tk.dma("sp", vgain[:], vgain_in, W=(vgain,), owner=vgain)
            tk.dma("sp", bsb[:], bsb_in, W=(bsb,), owner=bsb)
            kst_ring = Ring([P.sb([128, 1024], BF16, dma=True) for _ in range(2)])
            sq_ring = Ring([P.sb([128, 1024], BF16) for _ in range(2)])
            pend_sq = None
            for c in range(48):
                ws = wslots.next()
                tk.dma("pool", ws[:], wfm_own[c], W=(ws,), owner=ws)
                kst = kst_ring.next()
                if c < 16:
                    fm_chunk(ws, hT, 1024, acc_ring,
                             lambda acc, hh: tk.op("act", lambda e: e.activation(out=kst[:, hh * 512:(hh + 1) * 512], in_=acc[:],
                                                                                 func=AF.Gelu_apprx_tanh), R=(acc,), W=(kst,)))
                else:
                    fm_chunk(ws, hT, 1024, acc_ring,
                             lambda acc, hh: tk.op("act", lambda e: e.copy(out=kst[:, hh * 512:(hh + 1) * 512], in_=acc[:]),
                                                   R=(acc,), W=(kst,)))
                tk.dma("sp", OWN[c], kst[:], R=(kst,), owner=kst)
                if pend_sq is not None:
                    sqnorm_b(pend_sq, nps, qmax)
                    pend_sq = None
                if 16 <= c < 32:
                    pend_sq = sqnorm_a(kst, sq_ring)
            wsw = wslots.next()
            tk.dma("pool", wsw[:, 0:1024], wtm_wi, W=(wsw,), owner=wsw)
            for j in range(NT):
                acc = acc_ring.next()
                tk.mm(acc, [(acc[:, 0:32], hT[:, kc, j * 128:(j + 1) * 128], wsw[:, kc * 32:(kc + 1) * 32],
                             kc == 0, kc == 31, (wsw, hT)) for kc in range(32)])
                tk.op("act", lambda e: e.mul(out=wi[:, j, :], in_=acc[:, 0:32], mul=float(32 ** -0.5 * 64 ** -0.5)),
                      R=(acc,), W=(wi,))
            for kt_ in kst_ring.tiles:
                for tok in list(kt_.r.values()):
                    tk.E["sp"].wait(tok)
            gl_ring = Ring([P.sb([128, 512], F32) for _ in range(2)])
            sq2_ring = Ring([P.sb([128, 512], F32) for _ in range(2)])
            st_ring = Ring([P.sb([128, 16], F32) for _ in range(2)])
            vn_ring = Ring([P.sb([128, 512], BF16) for _ in range(2)])
            zt_ring = Ring([P.sb([128, 512], F32) for _ in range(2)])
            gu_ring = Ring([P.sb([128, 4, 128], BF16, dma=True) for _ in range(3)])
            for cc in range(4):
                pcs = []
                for q in range(4):
                    ws = wslots.next()
                    tk.dma("pool", ws[:], wtm_vg[cc, q], W=(ws,), owner=ws)
                    pcs.append(ws)
                for j in range(NT):
                    gu = gu_ring.next()
                    tk.dma("sp", gu[:], OWN[cc * 4:(cc + 1) * 4, :, j * 128:(j + 1) * 128].rearrange("c p t -> p c t"),
                           W=(gu,), owner=gu)
                    acc = acc_ring.next()
                    tm_tile(pcs, hT, j, acc)
                    gl = gl_ring.next()
                    sq2 = sq2_ring.next()
                    st = st_ring.next()
                    vn = vn_ring.next()
                    tk.op("act", lambda e: e.activation(out=gl[:], in_=acc[:], func=AF.Gelu_apprx_tanh), R=(acc,), W=(gl,))
                    tk.op("act", lambda e: e.activation(out=sq2[:], in_=gl[:], func=AF.Square), R=(gl,), W=(sq2,))
                    tk.op("dve", lambda e: e.tensor_reduce(out=st[:, 0:4], in_=gl[:].rearrange("p (g c) -> p g c", g=4),
                                                            axis=AX.X, op=ALU.add), R=(gl,), W=(st,))
                    tk.op("dve", lambda e: e.tensor_reduce(out=st[:, 4:8], in_=sq2[:].rearrange("p (g c) -> p g c", g=4),
                                                            axis=AX.X, op=ALU.add), R=(sq2, st), W=(st,))
                    tk.op("dve", lambda e: e.tensor_scalar(out=st[:, 0:4], in0=st[:, 0:4], scalar1=1.0 / 128, scalar2=None,
                                                            op0=ALU.mult), R=(st,), W=(st,))
                    tk.op("dve", lambda e: e.tensor_tensor(out=st[:, 8:12], in0=st[:, 0:4], in1=st[:, 0:4], op=ALU.mult),
                          R=(st,), W=(st,))
                    tk.op("dve", lambda e: e.scalar_tensor_tensor(out=st[:, 12:16], in0=st[:, 4:8], scalar=1.0 / 128,
                                                                   in1=st[:, 8:12], op0=ALU.mult, op1=ALU.subtract),
                          R=(st,), W=(st,))
                    tk.op("act", lambda e: e.activation(out=st[:, 8:12], in_=st[:, 12:16], func=AF.Sqrt, bias=epsT[:, 0:1]),
                          R=(st, epsT), W=(st,))
                    tk.op("dve", lambda e: e.reciprocal(out=st[:, 12:16], in_=st[:, 8:12]), R=(st,), W=(st,))
                    for gi in range(4):
                        tk.op("dve", lambda e: e.tensor_scalar(out=gl[:, gi * 128:(gi + 1) * 128],
                                                                in0=gl[:, gi * 128:(gi + 1) * 128],
                                                                scalar1=st[:, gi:gi + 1], scalar2=st[:, 12 + gi:13 + gi],
                                                                op0=ALU.subtract, op1=ALU.mult), R=(gl, st), W=(gl,))
                    tk.op("dve", lambda e: e.tensor_tensor(out=vn[:], in0=gl[:], in1=vgain[:, cc * 512:(cc + 1) * 512],
                                                            op=ALU.mult), R=(gl, vgain), W=(vn,))
                    acc2 = acc_ring.next()
                    for gi in range(4):
                        g = cc * 4 + gi
                        tk.mm(acc2, [(acc2[:, gi * 128:(gi + 1) * 128], vn[:, gi * 128:(gi + 1) * 128],
                                      WgT[:, g * 128:(g + 1) * 128], True, True, (vn, WgT))], cont=(gi > 0))
                    zt = zt_ring.next()
                    tk.op("dve", lambda e: e.tensor_tensor(out=zt[:], in0=acc2[:], in1=bsb[:, cc * 512:(cc + 1) * 512],
                                                            op=ALU.add), R=(acc2, bsb), W=(zt,))
                    tk.op("dve", lambda e: e.tensor_tensor(
                        out=mix_gm[:, cc * 4:(cc + 1) * 4, j * 128:(j + 1) * 128],
                        in0=zt[:].rearrange("p (g t) -> p g t", g=4),
                        in1=gu[:], op=ALU.mult),
                        R=(zt, gu), W=(mix_gm,))
            tk.op("dve", lambda e: e.tensor_tensor(out=tmp1[:], in0=qmax[:], in1=kmax[:], op=ALU.mult),
                  R=(qmax, kmax), W=(tmp1,))
            tk.op("act", lambda e: e.activation(out=tmp1[:], in_=tmp1[:], func=AF.Sqrt), R=(tmp1,), W=(tmp1,))
            tk.op("dve", lambda e: e.tensor_scalar(out=tmp1[:], in0=tmp1[:], scalar1=-1.05 * 128 ** -0.5, scalar2=None,
                                                    op0=ALU.mult), R=(tmp1,), W=(tmp1,))
            tmpb = P.sb([1, 2], BF16)
            tk.op("dve", lambda e: e.tensor_copy(out=tmpb[:, 0:1], in_=tmp1[:]), R=(tmp1,), W=(tmpb,))
            tk.mm(nps, [(nps[:, 0:1], onesb[0:1, :], tmpb[0:1, 0:1], True, True, (onesb, tmpb))])
            tk.op("act", lambda e: e.copy(out=negb[:], in_=nps[:, 0:1]), R=(nps,), W=(negb,))
        P.close()

        M2 = Pool_(tk, nc)
        mix_att = M2.sb([128, 16, NT * 128], BF16)

        P = Pool_(tk, nc)
        if STAGE >= 3:
            kidxA = P.sb([128, L], BF16, dma=True)
            kidxB = P.sb([128, L], BF16, dma=True)
            tk.op("dve", lambda e: e.memset(kidxA[64:128, :], 0.0), W=(kidxA,))
            tk.op("dve", lambda e: e.memset(kidxB[0:64, :], 0.0), W=(kidxB,))
            tk.dma("sp", kidxA[0:64, :], kidxTs[0:64, :], W=(kidxA,), owner=kidxA)
            tk.dma("sp", kidxB[64:128, :], kidxTs[64:128, :], W=(kidxB,), owner=kidxB)
            qi_ring = Ring([P.sb([128, 16, 128], BF16, dma=True) for _ in range(1)])
            qt_ring = Ring([P.sb([128, 16, 128], BF16, dma=True) for _ in range(1)])
            Dg_ring = Ring([P.sb([128, 32, 128], BF16) for _ in range(1)])
            sc_ring = Ring([P.sb([128, L], F32, dma=True) for _ in range(2)])
            work = P.sb([128, L], F32)
            nm_in_ring = Ring([P.sb([128, 512], BF16, dma=True) for _ in range(2)])
            oh_ring = Ring([P.sb([128, 128], F32) for _ in range(3)])
            dh_ring = Ring([P.sb([128, 128], F32) for _ in range(3)])
            nm = P.sb([128, L], BF16, dma=True)
            nmT = P.sb([128, 32, 128], BF16)
            m8_ring = Ring([P.sb([128, 8], F32) for _ in range(2)])
            thr_ring = Ring([P.sb([128, 1], F32) for _ in range(2)])
            R_ring = Ring([P.sb([128, 512], BF16) for _ in range(4)])
            kt_ring = Ring([P.sb([128, L], BF16, dma=True) for _ in range(2)])
            v_ring = Ring([P.sb([128, 32, 128], BF16, dma=True) for _ in range(2)])
            pt_ring = Ring([P.sb([128, 512], BF16) for _ in range(3)])
            Lps_ring = Ring([P.ps([128, 512], F32) for _ in range(4)])
            dps_ring = Ring([P.ps([128, 512], F32) for _ in range(1)])
            accs_ring = Ring([P.ps([128, 512], F32) for _ in range(1)])
            sps_ring = Lps_ring
            ops_ring = Ring([P.ps([128, 512], F32) for _ in range(1)])
            tps_ring = Ring([P.ps([128, 1024], BF16) for _ in range(1)])

            def scores(m):
                nb = 4 * (m + 1)
                qi = qi_ring.next()
                tk.dma("sp", qi[:], OWN[32:48, :, m * 128:(m + 1) * 128].rearrange("c p t -> p c t"), W=(qi,), owner=qi)
                Dg = Dg_ring.next()
                for h in range(32):
                    tk.op("pool", lambda e: e.tensor_scalar(out=Dg[:, h, :], in0=identb[:], scalar1=wi[:, m, h:h + 1],
                                                             scalar2=None, op0=ALU.mult), R=(identb, wi), W=(Dg,))
                sc = sc_ring.next()
                for sg in range(nb // 4):
                    acc = accs_ring.next()
                    pend = []
                    LOOK = 2
                    last = (sg == nb // 4 - 1)
                    for h in range(32 + LOOK):
                        if h < 32:
                            lp = Lps_ring.next()
                            kx = kidxA if h % 2 == 0 else kidxB
                            tk.mm(lp, [(lp[:], qi[:, h // 2, :], kx[:, sg * 512:(sg + 1) * 512],
                                        True, True, (qi, kx))])
                            Rt = R_ring.next()
                            tk.op("act", lambda e: e.activation(out=Rt[:], in_=lp[:], func=AF.Relu), R=(lp,), W=(Rt,))
                            pend.append((h, Rt))
                        if h >= LOOK:
                            hh, Rr = pend.pop(0)
                            tk.mm(acc, [(acc[:], Dg[:, hh, :], Rr[:], hh == 0, hh == 31 and not last, (Dg, Rr))],
                                  cont=(hh > 0))
                    if sg == nb // 4 - 1:
                        nmi = nm_in_ring.next()
                        tk.dma("pool", nmi[:], negmask_in[m], W=(nmi,), owner=nmi)
                        tk.mm(acc, [(acc[:], identb[:], nmi[:], False, True, (identb, nmi))], cont=True)
                    tk.op("act", lambda e: e.copy(out=sc[:, sg * 512:(sg + 1) * 512], in_=acc[:]), R=(acc,), W=(sc,))
                return sc

            def topk(m, sc):
                nb = 4 * (m + 1)
                S = nb * 128
                m8 = None
                for rnd in range(32):
                    m8 = m8_ring.next()
                    src = sc if rnd == 0 else work
                    tk.op("dve", lambda e: e.max(out=m8[:], in_=src[:, 0:S]), R=(src,), W=(m8,))
                    if rnd < 31:
                        if rnd == 0:
                            tk.op("dve", lambda e: e.match_replace(out=work[:, 0:S], in_to_replace=m8[:],
                                                                    in_values=src[:, 0:S], imm_value=NEG),
                                  R=(m8, src), W=(work,))
                        else:
                            tk.op("dve", lambda e: e.match_replace(out=work[:, 0:S], in_to_replace=m8[:],
                                                                    in_values=work[:, 0:S], imm_value=NEG),
                                  R=(m8, work), W=(work,))
                thr = thr_ring.next()
                tk.op("dve", lambda e: e.tensor_scalar(out=thr[:], in0=m8[:, 7:8], scalar1=-1e29, scalar2=None, op0=ALU.max),
                      R=(m8,), W=(thr,))
                tk.op("dve", lambda e: e.tensor_scalar(out=nm[:, 0:S], in0=sc[:, 0:S], scalar1=thr[:, 0:1], scalar2=-30000.0,
                                                        op0=ALU.is_lt, op1=ALU.mult), R=(sc, thr), W=(nm,))
                if DEBUG:
                    tk.dma("sp", scdbg[m, :, 0:S], sc[:, 0:S], R=(sc,), owner=sc)
                    tk.dma("sp", nmdbg[m, :, 0:S], nm[:, 0:S], R=(nm,), owner=nm)

            def attention_pre(m):
                nb = 4 * (m + 1)
                for b0 in range(0, nb, 8):
                    nbb = min(8, nb - b0)
                    tp = tps_ring.next()
                    tk.tr(tp, [(tp[:, j * 128:(j + 1) * 128], nm[:, (b0 + j) * 128:(b0 + j + 1) * 128], identb[:], (nm, identb))
                               for j in range(nbb)])
                    tk.op("act", lambda e: e.copy(out=nmT[:, b0:b0 + nbb, :],
                                                   in_=tp[:, 0:nbb * 128].rearrange("p (j t) -> p j t", j=nbb)),
                          R=(tp,), W=(nmT,))

            def attention(m):
                nb = 4 * (m + 1)
                S = nb * 128
                qt = qt_ring.next()
                tk.dma("sp", qt[:], OWN[16:32, :, m * 128:(m + 1) * 128].rearrange("h p t -> p h t"), W=(qt,), owner=qt)
                for h in range(16):
                    kt = kt_ring.next()
                    tk.dma("sp", kt[:, 0:S], KTs[h, :, 0:S], W=(kt,), owner=kt)
                    vt = v_ring.next()
                    tk.dma("sp", vt[:, 0:nb, :], Vs[h, :, 0:nb, :], W=(vt,), owner=vt)
                    ops = ops_ring.next()
                    dps = dps_ring.next()
                    prev = None
                    nsg = nb // 4
                    for sg in range(nsg + 1):
                        if sg < nsg:
                            sp = sps_ring.next()
                            mms = []
                            for j in range(4):
                                blk = sg * 4 + j
                                mms.append((sp[:, j * 128:(j + 1) * 128], kt[:, blk * 128:(blk + 1) * 128], qt[:, h, :],
                                            True, False, (kt, qt)))
                                mms.append((sp[:, j * 128:(j + 1) * 128], identb[:], nmT[:, blk, :], False, True,
                                            (identb, nmT)))
                            tk.mm(sp, mms)
                            pt = pt_ring.next()
                            tk.op("act", lambda e: e.activation(out=pt[:], in_=sp[:], func=AF.Exp, bias=negb[:, 0:1],
                                                                 scale=float(128 ** -0.5)), R=(sp, negb), W=(pt,))
                        if prev is not None:
                            sg2, pt2 = prev
                            mms = []
                            mmd = []
                            for j in range(4):
                                blk = sg2 * 4 + j
                                mms.append((ops[:, 0:128], vt[:, blk, :], pt2[:, j * 128:(j + 1) * 128],
                                            blk == 0, blk == nb - 1, (vt, pt2)))
                                mmd.append((dps[:, 0:128], onesb[:], pt2[:, j * 128:(j + 1) * 128],
                                            blk == 0, blk == nb - 1, (onesb, pt2)))
                            tk.mm(ops, mms, cont=(sg2 > 0))
                            tk.mm(dps, mmd, cont=(sg2 > 0))
                        prev = (sg, pt) if sg < nsg else None
                    oh = oh_ring.next()
                    dh = dh_ring.next()
                    tk.op("act", lambda e: e.copy(out=oh[:], in_=ops[:, 0:128]), R=(ops,), W=(oh,))
                    tk.op("act", lambda e: e.activation(out=dh[:], in_=dps[:, 0:128], func=AF.Ln), R=(dps,), W=(dh,))
                    tk.op("act", lambda e: e.activation(out=dh[:], in_=dh[:], func=AF.Exp, scale=-1.0), R=(dh,), W=(dh,))
                    tk.op("pool", lambda e: e.tensor_tensor(out=mix_att[:, h, m * 128:(m + 1) * 128], in0=oh[:],
                                                             in1=dh[:], op=ALU.mult), R=(oh, dh), W=(mix_att,))

            sc_list = {0: scores(0), 1: scores(1)}
            topk(0, sc_list[0])
            for m in range(NT):
                attention_pre(m)
                if m + 1 < NT:
                    topk(m + 1, sc_list[m + 1])
                attention(m)
                if m + 2 < NT:
                    sc_list[m + 2] = scores(m + 2)
        P.close()

        if DEBUG and STAGE >= 3:
            Dp = Pool_(tk, nc)
            own = Dp.sb([1, 1], F32, dma=True)
            tk.dma("sp", mixdbg[:, 0:16, :], mix_att[:], R=(mix_att,), owner=own)
            tk.dma("sp", mixdbg[:, 16:32, :], mix_gm[:], R=(mix_gm,), owner=own)
            Dp.close()

        P = Pool_(tk, nc)
        if STAGE >= 4:
            wslots = Ring([P.sb([128, 4096], BF16, dma=True) for _ in range(8)])
            xr_ring = Ring([P.sb([128, 512], F32, dma=True) for _ in range(3)])
            x1_ring = Ring([P.sb([128, 512], F32, dma=True) for _ in range(3)])
            acc_ring = Ring([P.ps([128, 512], F32) for _ in range(6)])
            for ncn in range(8):
                pcs = []
                for q in range(4):
                    ws = wslots.next()
                    tk.dma("pool", ws[:], wtm_out[ncn, q], W=(ws,), owner=ws)
                    pcs.append(ws)
                for j in range(NT):
                    xr = xr_ring.next()
                    tk.dma("sp", xr[:], x_own[j * 128:(j + 1) * 128, ncn * 512:(ncn + 1) * 512], W=(xr,), owner=xr)
                    acc = acc_ring.next()
                    mms = []
                    for kc in range(32):
                        src = mix_att if kc < 16 else mix_gm
                        mms.append((acc[:], src[:, kc % 16, j * 128:(j + 1) * 128],
                                    pcs[kc // 8][:, (kc % 8) * 512:(kc % 8 + 1) * 512], kc == 0, kc == 31, (pcs[kc // 8], src)))
                    tk.mm(acc, mms)
                    x1t = x1_ring.next()
                    tk.op("dve", lambda e: e.tensor_tensor(out=x1t[:], in0=acc[:], in1=xr[:], op=ALU.add),
                          R=(acc, xr), W=(x1t,))
                    tk.dma("sp", x1s[j * 128:(j + 1) * 128, ncn * 512:(ncn + 1) * 512], x1t[:], R=(x1t,), owner=x1t)
        P.close()
        M2.close()
        M1.close()

        for half in range(2 if STAGE >= 5 else 0):
            P = Pool_(tk, nc)
            AT = P.sb([128, NFC, 512], BF16)
            Pg = Pool_(tk, nc)
            h2T = Pg.sb([128, 32, 512], BF16)
            Pn = Pool_(tk, nc)
            gb = Pn.sb([128, D], F32, dma=True)
            tk.dma("sp", gb[:], gffn_in, W=(gb,), owner=gb)
            xs_ring = Ring([Pn.sb([128, D], F32, dma=True) for _ in range(2)])
            h1_ring = Ring([Pn.sb([128, D], BF16) for _ in range(1)])
            junk = Pn.sb([128, D], BF16)
            ss_ring = Ring([Pn.sb([128, 4], F32) for _ in range(4)])
            ptr_ring = Ring([Pn.ps([128, 1024], BF16) for _ in range(2)])
            rmsnorm_tiles([x1s[(half * 4 + tt) * 128:(half * 4 + tt + 1) * 128, :] for tt in range(4)],
                          gb, xs_ring, h1_ring, junk, ptr_ring, h2T, ss_ring)
            Pn.close()
            wslots = Ring([Pg.sb([128, 4096], BF16, dma=True) for _ in range(6)])
            sg_ring = Ring([Pg.sb([128, 512], F32) for _ in range(2)])
            acc_ring = Ring([Pg.ps([128, 512], F32) for _ in range(6)])
            for fc in range(NFC):
                wg = wslots.next()
                tk.dma("pool", wg[:], wfm_g[fc], W=(wg,), owner=wg)
                wu = wslots.next()
                tk.dma("pool", wu[:], wfm_u[fc], W=(wu,), owner=wu)
                ag = acc_ring.next()
                tk.mm(ag, [(ag[:], wg[:, kc * 128:(kc + 1) * 128], h2T[:, kc, :], kc == 0, kc == 31, (wg, h2T)) for kc in range(32)])
                au = acc_ring.next()
                tk.mm(au, [(au[:], wu[:, kc * 128:(kc + 1) * 128], h2T[:, kc, :], kc == 0, kc == 31, (wu, h2T)) for kc in range(32)])
                sgt = sg_ring.next()
                tk.op("act", lambda e: e.activation(out=sgt[:], in_=ag[:], func=AF.Silu), R=(ag,), W=(sgt,))
                tk.op("dve", lambda e: e.tensor_tensor(out=AT[:, fc, :], in0=au[:], in1=sgt[:], op=ALU.mult),
                      R=(au, sgt), W=(AT,))
            Pg.close()
            Pd = Pool_(tk, nc)
            x2 = Pd.sb([128, 4, D], F32, dma=True)
            r0 = half * 512
            for tt in range(4):
                tk.dma("sp", x2[:, tt, :], x1s[r0 + tt * 128:r0 + (tt + 1) * 128, :], W=(x2,), owner=x2)
            wdslots = Ring([Pd.sb([128, 43 * 128], BF16, dma=True) for _ in range(4)])
            ys_ring = Ring([Pd.sb([128, 512], F32) for _ in range(2)])
            gfq_ring = Ring([Pd.sb([128, 1024], F32, dma=True) for _ in range(1)])
            yacc_ring = Ring([Pd.ps([128, 512], F32) for _ in range(4)])
            ytr_ring = Ring([Pd.ps([128, 512], F32) for _ in range(3)])
            ss_ring = Ring([Pd.sb([128, 4], F32) for _ in range(4)])
            for ncn in range(32):
                w0 = wdslots.next()
                tk.dma("pool", w0[:], wd_in[ncn, 0], W=(w0,), owner=w0)
                w1 = wdslots.next()
                tk.dma("pool", w1[:], wd_in[ncn, 1], W=(w1,), owner=w1)
                ya = yacc_ring.next()
                mms = []
                for fc in range(NFC):
                    wsrc = w0 if fc < 43 else w1
                    mms.append((ya[:], wsrc[:, (fc % 43) * 128:(fc % 43 + 1) * 128], AT[:, fc, :], fc == 0, fc == NFC - 1,
                                (wsrc, AT)))
                tk.mm(ya, mms)
                ys = ys_ring.next()
                tk.op("act", lambda e: e.copy(out=ys[:], in_=ya[:]), R=(ya,), W=(ys,))
                yt = ytr_ring.next()
                tk.tr(yt, [(yt[:, tt * 128:(tt + 1) * 128], ys[:, tt * 128:(tt + 1) * 128], identf[:], (ys, identf))
                           for tt in range(4)])
                tk.op("dve", lambda e: e.tensor_tensor(out=x2[:, :, ncn * 128:(ncn + 1) * 128],
                                                        in0=yt[:].rearrange("p (t n) -> p t n", t=4),
                                                        in1=x2[:, :, ncn * 128:(ncn + 1) * 128], op=ALU.add),
                      R=(yt, x2), W=(x2,))
            for tt in range(4):
                ss = ss_ring.next()
                tk.op("act", lambda e: e.activation(out=AT[:, 0:8, :].rearrange("p a b -> p (a b)"), in_=x2[:, tt, :],
                                                     func=AF.Square, scale=1.0 / 64.0, accum_out=ss[:, 0:1]),
                      R=(x2,), W=(AT, ss))
                tk.op("act", lambda e: e.activation(out=ss[:, 1:2], in_=ss[:, 0:1], func=AF.Sqrt, bias=epsT[:, 0:1]),
                      R=(ss, epsT), W=(ss,))
                tk.op("dve", lambda e: e.reciprocal(out=ss[:, 2:3], in_=ss[:, 1:2]), R=(ss,), W=(ss,))
                for q in range(4):
                    gfq = gfq_ring.next()
                    tk.dma("sp", gfq[:], gfin_in[:, q * 1024:(q + 1) * 1024], W=(gfq,), owner=gfq)
                    tk.op("dve", lambda e: e.scalar_tensor_tensor(out=x2[:, tt, q * 1024:(q + 1) * 1024],
                                                                   in0=x2[:, tt, q * 1024:(q + 1) * 1024], scalar=ss[:, 2:3],
                                                                   in1=gfq[:], op0=ALU.mult, op1=ALU.mult),
                          R=(x2, ss, gfq), W=(x2,))
            for tt in range(4):
                tk.dma("sp", out[r0 + tt * 128:r0 + (tt + 1) * 128, :], x2[:, tt, :], R=(x2,), owner=x2)
            Pd.close()
            P.close()
        G.close()
        tk.barrier()
    return nc


def _fm(Wc):
    n = Wc.shape[1]
    a = Wc.reshape(32, 128, n // 128, 128).transpose(2, 1, 0, 3)
    return np.ascontiguousarray(a).reshape(n // 128, 128, 32 * 128)


def _tm(Wc):
    n = Wc.shape[1]
    a = Wc.reshape(4, 8, 128, n // 512, 512).transpose(3, 0, 2, 1, 4)
    return np.ascontiguousarray(a).reshape(n // 512, 4, 128, 8 * 512)


def own_tiles(r):
    return [8 * (m // 2) + (r if m % 2 == 0 else 7 - r) for m in range(NT)]


_CACHE = {}


def kernel(x, norm_mix, w_in, gmlp_v_gain, w_spatial, b_spatial, w_out, norm_ffn, w_gate, w_up, w_down, norm_final):
    x = np.asarray(x, dtype=np.float32)
    w_in = np.asarray(w_in, dtype=np.float32)[0]
    w_out = np.asarray(w_out, dtype=np.float32)[0]
    w_gate = np.asarray(w_gate, dtype=np.float32)[0]
    w_up = np.asarray(w_up, dtype=np.float32)[0]
    w_down = np.asarray(w_down, dtype=np.float32)[0]
    f32 = np.float32
    shared = {}
    kcols = w_in[:, 2048:4096]
    kidxc = w_in[:, 8192:8256]
    shared["wfm_kv"] = np.concatenate([_fm(kcols), _fm(np.concatenate([kidxc, kidxc], axis=1))], axis=0)
    shared["wfm_own"] = np.concatenate([_fm(w_in[:, 8288:10336]), _fm(w_in[:, 0:2048]), _fm(w_in[:, 6144:8192])], axis=0)
    shared["wtm_v"] = _tm(w_in[:, 4096:6144])
    shared["wtm_vg"] = _tm(w_in[:, 10336:12384])
    shared["wtm_wi"] = np.ascontiguousarray(w_in[:, 8256:8288].reshape(32, 128, 32).transpose(1, 0, 2)).reshape(128, 32 * 32)
    shared["wtm_out"] = _tm(w_out)
    shared["wfm_g"] = _fm(w_gate)
    shared["wfm_u"] = _fm(w_up)
    wd = w_down.reshape(2, 43, 128, 32, 128).transpose(3, 0, 2, 1, 4)
    shared["wd"] = np.ascontiguousarray(wd).reshape(32, 2, 128, 43 * 128)
    shared["ident"] = np.eye(128, dtype=f32)
    shared["tril_st"] = np.triu(np.ones((128, 128), dtype=f32))
    shared["gmix_b"] = np.ascontiguousarray(np.broadcast_to(np.asarray(norm_mix, f32)[0][None, :], (128, D)))
    shared["gffn_b"] = np.ascontiguousarray(np.broadcast_to(np.asarray(norm_ffn, f32)[0][None, :], (128, D)))
    shared["gfin_b"] = np.ascontiguousarray(np.broadcast_to(np.asarray(norm_final, f32)[None, :], (128, D)))
    shared["vgain_b"] = np.ascontiguousarray(np.broadcast_to(np.asarray(gmlp_v_gain, f32)[0].reshape(1, 2048), (128, 2048)))
    shared["bsb"] = np.ascontiguousarray(np.broadcast_to(np.asarray(b_spatial, f32)[0].reshape(1, 2048), (128, 2048)))
    shared["wsT"] = np.ascontiguousarray(np.asarray(w_spatial, f32)[0].transpose(2, 0, 1)).reshape(128, 2048)

    in_maps = []
    for c in range(8):
        b, r = c // 4, c % 4
        tiles = own_tiles(r)
        xb = x[b]
        m = dict(shared)
        m["x_all"] = xb
        m["x_own"] = np.ascontiguousarray(np.concatenate([xb[t * 128:(t + 1) * 128] for t in tiles], axis=0))
        nmk = np.zeros((NT, 128, 512), dtype=f32)
        for mi, t in enumerate(tiles):
            nb = 4 * (mi + 1)
            qpos = t * 128 + np.arange(128)[:, None]
            kpos = (nb - 4) * 128 + np.arange(512)[None, :]
            nmk[mi] = np.where(kpos <= qpos, 0.0, NEG).astype(f32)
        m["negmask"] = nmk
        in_maps.append(m)

    if "nc" not in _CACHE:
        _CACHE["nc"] = build_program()
    nc = _CACHE["nc"]
    ncores = int(os.environ.get("K_NCORES", "8"))
    if ncores < 8:
        res = run_bass_kernel_spmd(nc, in_maps[:ncores], core_ids=list(range(ncores)))
        _CACHE["res"] = res
        return None
    res = run_bass_kernel_spmd(nc, in_maps, core_ids=list(range(8)))
    outp = np.zeros((2, L, D), dtype=np.float32)
    for c in range(8):
        b, r = c // 4, c % 4
        o = res.results[c]["out"]
        for mi, t in enumerate(own_tiles(r)):
            outp[b, t * 128:(t + 1) * 128] = o[mi * 128:(mi + 1) * 128]
    if DEBUG:
        _CACHE["res"] = res
    return outp
```

```python
import os
from contextlib import ExitStack
import numpy as np
import concourse.bass as bass
import concourse.mybir as mybir
from concourse.bass_utils import run_bass_kernel_spmd

F32 = mybir.dt.float32
BF16 = mybir.dt.bfloat16
AF = mybir.ActivationFunctionType
ALU = mybir.AluOpType
AX = mybir.AxisListType

D = 4096
L = 4096
DFF = 11008
NFC = 86
NT = 8
EPS = 1e-6
NEG = -1e30
STAGE = int(os.environ.get("K_STAGE", "99"))
DEBUG = int(os.environ.get("K_DEBUG", "0"))


class Eng:
    def __init__(self, name, e, sem):
        self.name, self.e, self.sem = name, e, sem
        self.cnt = 0
        self.waited = {}

    def wait(self, tok):
        if tok is None:
            return
        key, sem, val = tok
        if self.waited.get(key, 0) >= val:
            return
        self.e.wait_ge(sem, val)
        self.waited[key] = val

    def sig(self, ins):
        self.cnt += 1
        ins.then_inc(self.sem, 1)
        return (self.name, self.sem, self.cnt)


class T:
    _n = 0

    def __init__(self, ap, dsem=None):
        self.ap = ap
        self.w = None
        self.r = {}
        self.dsem = dsem
        self.dcnt = 0
        T._n += 1
        self.name = "t%d" % T._n

    def __getitem__(self, idx):
        return self.ap[idx]


class Tk:
    def __init__(self, nc, es):
        self.nc = nc
        self.es = es
        self.E = {}
        for name, e in (("pe", nc.tensor), ("act", nc.scalar), ("dve", nc.vector),
                        ("pool", nc.gpsimd), ("sp", nc.sync)):
            sem = es.enter_context(nc.semaphore("s_" + name))
            self.E[name] = Eng(name, e, sem)
        self.dma_latest = {}
        self.free_dsems = {"sp": [], "pool": []}
        self.nsem = 0

    def dsem(self, kind):
        if self.free_dsems[kind]:
            return self.free_dsems[kind].pop()
        self.nsem += 1
        return [self.es.enter_context(self.nc.semaphore("d%d" % self.nsem)), 0, "d%d" % self.nsem]

    def _pre(self, E, en, R, W):
        for t in R:
            E.wait(t.w)
        for t in W:
            E.wait(t.w)
            for k, tok in t.r.items():
                E.wait(tok)

    def op(self, en, fn, R=(), W=()):
        E = self.E[en]
        self._pre(E, en, R, W)
        ins = fn(E.e)
        tok = E.sig(ins)
        for t in R:
            t.r[en] = tok
        for t in W:
            t.w = tok
            t.r = {}
        return tok

    def mm(self, out_t, mms, cont=False):
        E = self.E["pe"]
        if not cont:
            self._pre(E, "pe", (), (out_t,))
        allR = []
        ins = None
        for (o, l, r, st, sp, R) in mms:
            for t in R:
                E.wait(t.w)
                allR.append(t)
            ins = E.e.matmul(o, lhsT=l, rhs=r, start=st, stop=sp)
        tok = E.sig(ins)
        for t in allR:
            t.r["pe"] = tok
        out_t.w = tok
        if not cont:
            out_t.r = {}
        return tok

    def tr(self, out_t, trs):
        E = self.E["pe"]
        self._pre(E, "pe", (), (out_t,))
        allR = []
        ins = None
        for (o, i, idn, R) in trs:
            for t in R:
                E.wait(t.w)
                allR.append(t)
            ins = E.e.transpose(o, i, idn)
        tok = E.sig(ins)
        for t in allR:
            t.r["pe"] = tok
        out_t.w = tok
        out_t.r = {}
        return tok

    def dma(self, qn, out_ap, in_ap, R=(), W=(), owner=None):
        Q = self.E[qn]
        self._pre(Q, "dma", R, W)
        ins = Q.e.dma_start(out=out_ap, in_=in_ap)
        if owner.dsem is None:
            owner.dsem = self.dsem(qn)
            owner.dkind = qn
            owner.pool.ds.append((qn, owner.dsem))
        assert owner.dkind == qn, "tile DMA semaphore is bound to one queue kind"
        ds = owner.dsem
        ds[1] += 16
        ins.then_inc(ds[0], 16)
        tok = (ds[2], ds[0], ds[1])
        for t in R:
            t.r["dma" + ds[2]] = tok
        for t in W:
            t.w = tok
            t.r = {}
        self.dma_latest[ds[2]] = tok
        return tok

    def barrier(self):
        sp = self.E["sp"]
        for tok in self.dma_latest.values():
            sp.wait(tok)
        toks = []
        for en, E in self.E.items():
            if en == "sp":
                continue
            if E.cnt > 0:
                E.e.wait_ge(E.sem, E.cnt)
            ins = E.e.sem_inc(E.sem, 1)
            E.cnt += 1
            toks.append((E.name, E.sem, E.cnt))
        for tok in toks:
            sp.wait(tok)
        if sp.cnt > 0:
            sp.e.wait_ge(sp.sem, sp.cnt)
        ins = sp.e.sem_inc(sp.sem, 1)
        sp.cnt += 1
        stok = (sp.name, sp.sem, sp.cnt)
        for en, E in self.E.items():
            if en != "sp":
                E.wait(stok)


class Pool_:
    cnt = 0

    def __init__(self, tk, nc):
        self.tk, self.nc = tk, nc
        self.es = ExitStack()
        self.ds = []
        self.n = 0

    def sb(self, shape, dt, dma=False, name=None):
        Pool_.cnt += 1
        h = self.es.enter_context(self.nc.sbuf_tensor("sb%d" % Pool_.cnt, shape, dt))
        t = T(h[:], None)
        t.pool = self
        return t

    def ps(self, shape, dt):
        Pool_.cnt += 1
        h = self.es.enter_context(self.nc.psum_tensor("ps%d" % Pool_.cnt, shape, dt))
        return T(h[:])

    def close(self):
        self.tk.barrier()
        for kind, ds in self.ds:
            self.tk.free_dsems[kind].append(ds)
        self.es.close()


class Ring:
    def __init__(self, tiles):
        self.tiles = tiles
        self.i = 0

    def next(self):
        t = self.tiles[self.i % len(self.tiles)]
        self.i += 1
        return t


def build_program():
    nc = bass.Bass("TRN2", target_bir_lowering=False)
    dk = "ExternalOutput" if DEBUG else "Internal"

    def din(name, shape, dt=F32):
        return nc.dram_tensor(name, shape, dt, kind="ExternalInput").ap()

    x_all = din("x_all", [L, D])
    x_own = din("x_own", [NT * 128, D])
    negmask_in = din("negmask", [NT, 128, 512])
    ident_in = din("ident", [128, 128])
    tril_in = din("tril_st", [128, 128])
    gmix_in = din("gmix_b", [128, D])
    gffn_in = din("gffn_b", [128, D])
    gfin_in = din("gfin_b", [128, D])
    vgain_in = din("vgain_b", [128, 2048])
    bsb_in = din("bsb", [128, 16 * 128])
    wsT_in = din("wsT", [128, 16 * 128])
    wfm_kv = din("wfm_kv", [17, 128, 32 * 128])
    wfm_own = din("wfm_own", [48, 128, 32 * 128])
    wtm_v = din("wtm_v", [4, 4, 128, 8 * 512])
    wtm_vg = din("wtm_vg", [4, 4, 128, 8 * 512])
    wtm_wi = din("wtm_wi", [128, 32 * 32])
    wtm_out = din("wtm_out", [8, 4, 128, 8 * 512])
    wfm_g = din("wfm_g", [NFC, 128, 32 * 128])
    wfm_u = din("wfm_u", [NFC, 128, 32 * 128])
    wd_in = din("wd", [32, 2, 128, 43 * 128])
    out = nc.dram_tensor("out", [NT * 128, D], F32, kind="ExternalOutput").ap()

    KTs = nc.dram_tensor("KTs", [16, 128, L], BF16, kind=dk).ap()
    kidxTs = nc.dram_tensor("kidxTs", [128, L], BF16, kind=dk).ap()
    Vs = nc.dram_tensor("Vs", [16, 128, 32, 128], BF16, kind=dk).ap()
    OWN = nc.dram_tensor("OWNs", [48, 128, NT * 128], BF16, kind=dk).ap()
    x1s = nc.dram_tensor("x1s", [NT * 128, D], F32, kind=dk).ap()
    if DEBUG:
        mixdbg = nc.dram_tensor("mixdbg", [128, 32, NT * 128], BF16, kind="ExternalOutput").ap()
        scdbg = nc.dram_tensor("scdbg", [NT, 128, L], F32, kind="ExternalOutput").ap()
        nmdbg = nc.dram_tensor("nmdbg", [NT, 128, L], BF16, kind="ExternalOutput").ap()

    with ExitStack() as ges:
        tk = Tk(nc, ges)
        G = Pool_(tk, nc)
        identf = G.sb([128, 128], F32, dma=True)
        identb = G.sb([128, 128], BF16, dma=True)
        onesb = G.sb([128, 128], BF16)
        kmax = G.sb([1, 1], F32)
        qmax = G.sb([1, 1], F32)
        negb = G.sb([128, 1], F32)
        tmp1 = G.sb([1, 1], F32)
        epsT = G.sb([128, 1], F32)

        tk.dma("sp", identf[:], ident_in, W=(identf,), owner=identf)
        tk.dma("pool", identb[:], ident_in, W=(identb,), owner=identb)
        tk.op("dve", lambda e: e.memset(onesb[:], 1.0), W=(onesb,))
        tk.op("dve", lambda e: e.memset(kmax[:], 0.0), W=(kmax,))
        tk.op("dve", lambda e: e.memset(qmax[:], 0.0), W=(qmax,))
        tk.op("dve", lambda e: e.memset(epsT[:], EPS), W=(epsT,))

        def norm_stats(x_src_ap, xs_ring, junk, ss_ring):
            xs = xs_ring.next()
            tk.dma("sp", xs[:], x_src_ap, W=(xs,), owner=xs)
            ss = ss_ring.next()
            tk.op("act", lambda e: e.activation(out=junk[:], in_=xs[:], func=AF.Square, scale=1.0 / 64.0,
                                                 accum_out=ss[:, 0:1]), R=(xs,), W=(junk, ss))
            tk.op("act", lambda e: e.activation(out=ss[:, 1:2], in_=ss[:, 0:1], func=AF.Sqrt, bias=epsT[:, 0:1]),
                  R=(ss, epsT), W=(ss,))
            tk.op("dve", lambda e: e.reciprocal(out=ss[:, 2:3], in_=ss[:, 1:2]), R=(ss,), W=(ss,))
            return xs, ss

        def norm_evac(xs, ss, gb, h1_ring, ptr_ring, hT, tcol):
            h1 = h1_ring.next()
            tk.op("dve", lambda e: e.scalar_tensor_tensor(out=h1[:], in0=xs[:], scalar=ss[:, 2:3], in1=gb[:],
                                                            op0=ALU.mult, op1=ALU.mult), R=(xs, ss, gb), W=(h1,))
            for q in range(4):
                pt = ptr_ring.next()
                tk.tr(pt, [(pt[:, j * 128:(j + 1) * 128], h1[:, (q * 8 + j) * 128:(q * 8 + j + 1) * 128], identb[:],
                            (h1, identb)) for j in range(8)])
                if q % 2 == 0:
                    tk.op("act", lambda e: e.copy(out=hT[:, q * 8:(q + 1) * 8, tcol:tcol + 128],
                                                   in_=pt[:].rearrange("p (j t) -> p j t", j=8)), R=(pt,), W=(hT,))
                else:
                    tk.op("dve", lambda e: e.tensor_copy(out=hT[:, q * 8:(q + 1) * 8, tcol:tcol + 128],
                                                          in_=pt[:].rearrange("p (j t) -> p j t", j=8)), R=(pt,), W=(hT,))

        def rmsnorm_tiles(srcs, gb, xs_ring, h1_ring, junk, ptr_ring, hT, ss_ring):
            assert len(xs_ring.tiles) >= 2
            pend = norm_stats(srcs[0], xs_ring, junk, ss_ring)
            for j in range(len(srcs)):
                nxt = norm_stats(srcs[j + 1], xs_ring, junk, ss_ring) if j + 1 < len(srcs) else None
                norm_evac(pend[0], pend[1], gb, h1_ring, ptr_ring, hT, j * 128)
                pend = nxt

        def sqnorm_a(src_t, sq_ring):
            sq_t = sq_ring.next()
            tk.op("act", lambda e: e.activation(out=sq_t[:, 0:1024], in_=src_t[:, 0:1024], func=AF.Square),
                  R=(src_t,), W=(sq_t,))
            tk.op("dve", lambda e: e.tensor_tensor(out=sq_t[:, 0:512], in0=sq_t[:, 0:512], in1=sq_t[:, 512:1024],
                                                    op=ALU.add), R=(sq_t,), W=(sq_t,))
            return sq_t

        def sqnorm_b(sq_t, nps, run_max):
            tk.mm(nps, [(nps[0:1, 0:512], onesb[:, 0:1], sq_t[:, 0:512], True, True, (sq_t, onesb))])
            tk.op("dve", lambda e: e.tensor_reduce(out=tmp1[:], in_=nps[0:1, 0:512], axis=AX.X, op=ALU.max),
                  R=(nps,), W=(tmp1,))
            tk.op("dve", lambda e: e.tensor_tensor(out=run_max[:], in0=run_max[:], in1=tmp1[:], op=ALU.max),
                  R=(tmp1, run_max), W=(run_max,))

        def fm_chunk(ws, hT, ntok, acc_ring, evac):
            for hh in range(ntok // 512):
                acc = acc_ring.next()
                tk.mm(acc, [(acc[:], ws[:, kc * 128:(kc + 1) * 128], hT[:, kc, hh * 512:(hh + 1) * 512],
                             kc == 0, kc == 31, (ws, hT)) for kc in range(32)])
                evac(acc, hh)

        def tm_tile(pcs, hT, j, acc):
            tk.mm(acc, [(acc[:], hT[:, kc, j * 128:(j + 1) * 128],
                         pcs[kc // 8][:, (kc % 8) * 512:(kc % 8 + 1) * 512],
                         kc == 0, kc == 31, (pcs[kc // 8], hT)) for kc in range(32)])

        P = Pool_(tk, nc)
        if STAGE >= 1:
            gb = P.sb([128, D], F32, dma=True)
            tk.dma("sp", gb[:], gmix_in, W=(gb,), owner=gb)
            hT = P.sb([128, 32, 1024], BF16)
            xs_ring = Ring([P.sb([128, D], F32, dma=True) for _ in range(2)])
            h1_ring = Ring([P.sb([128, D], BF16) for _ in range(2)])
            junk = P.sb([128, D], BF16)
            ss_ring = Ring([P.sb([128, 4], F32) for _ in range(4)])
            wslots = Ring([P.sb([128, 4096], BF16, dma=True) for _ in range(6)])
            kst_ring = Ring([P.sb([128, 1024], BF16, dma=True) for _ in range(2)])
            vst_ring = Ring([P.sb([128, 512], BF16, dma=True) for _ in range(3)])
            sq_ring = Ring([P.sb([128, 1024], BF16) for _ in range(2)])
            ptr_ring = Ring([P.ps([128, 1024], BF16) for _ in range(2)])
            acc_ring = Ring([P.ps([128, 512], F32) for _ in range(5)])
            nps = P.ps([128, 512], F32)
            pend_sq = None

            MASK = int(os.environ.get("K_P1MASK", "15"))
            NTB = int(os.environ.get("K_NTB", "4"))
            for tb in range(NTB):
                rmsnorm_tiles([x_all[tb * 1024 + j * 128:tb * 1024 + (j + 1) * 128, :] for j in range(8)],
                              gb, xs_ring, h1_ring, junk, ptr_ring, hT, ss_ring)
                for c in range(17 if MASK & 2 else 0):
                    ws = wslots.next()
                    tk.dma("pool", ws[:], wfm_kv[c], W=(ws,), owner=ws)
                    kst = kst_ring.next()
                    fm_chunk(ws, hT, 1024, acc_ring,
                             lambda acc, hh: tk.op("act", lambda e: e.copy(out=kst[:, hh * 512:(hh + 1) * 512], in_=acc[:]),
                                                   R=(acc,), W=(kst,)))
                    if pend_sq is not None:
                        sqnorm_b(pend_sq, nps, kmax)
                        pend_sq = None
                    if c < 16:
                        tk.dma("sp", KTs[c, :, tb * 1024:(tb + 1) * 1024], kst[:], R=(kst,), owner=kst)
                        pend_sq = sqnorm_a(kst, sq_ring)
                    else:
                        tk.dma("sp", kidxTs[:, tb * 1024:(tb + 1) * 1024], kst[:], R=(kst,), owner=kst)
                for cc in range(4 if MASK & 8 else 0):
                    pcs = []
                    for q in range(4):
                        ws = wslots.next()
                        tk.dma("pool", ws[:], wtm_v[cc, q], W=(ws,), owner=ws)
                        pcs.append(ws)
                    for j in range(8):
                        acc = acc_ring.next()
                        tm_tile(pcs, hT, j, acc)
                        vst = vst_ring.next()
                        tk.op("dve", lambda e: e.tensor_copy(out=vst[:], in_=acc[:]), R=(acc,), W=(vst,))
                        blk = tb * 8 + j
                        tk.dma("sp", Vs[cc * 4:(cc + 1) * 4, :, blk, :].rearrange("h p d -> p h d"),
                               vst[:].rearrange("p (h d) -> p h d", h=4), R=(vst,), owner=vst)
        P.close()

        M1 = Pool_(tk, nc)
        mix_gm = M1.sb([128, 16, NT * 128], BF16)
        wi = M1.sb([128, NT, 32], F32)

        P = Pool_(tk, nc)
        if STAGE >= 2:
            hT = P.sb([128, 32, 1024], BF16)
            acc_ring = Ring([P.ps([128, 512], F32) for _ in range(5)])
            nps = P.ps([128, 512], F32)
            WgT = P.sb([128, 2048], BF16)
            P2a = Pool_(tk, nc)
            gb = P2a.sb([128, D], F32, dma=True)
            tk.dma("sp", gb[:], gmix_in, W=(gb,), owner=gb)
            xs_ring = Ring([P2a.sb([128, D], F32, dma=True) for _ in range(2)])
            h1_ring = Ring([P2a.sb([128, D], BF16) for _ in range(1)])
            junk = P2a.sb([128, D], BF16)
            ss_ring = Ring([P2a.sb([128, 4], F32) for _ in range(4)])
            ptr_ring = Ring([P2a.ps([128, 1024], BF16) for _ in range(2)])
            wsT = xs_ring.tiles[0]
            tril = P2a.sb([128, 128], F32, dma=True)
            tk.dma("sp", tril[:], tril_in, W=(tril,), owner=tril)
            tk.dma("sp", wsT[:, 0:2048], wsT_in, W=(wsT,), owner=wsT)
            for g in range(16):
                tk.op("dve", lambda e: e.tensor_tensor(out=WgT[:, g * 128:(g + 1) * 128],
                                                        in0=wsT[:, g * 128:(g + 1) * 128], in1=tril[:], op=ALU.mult),
                      R=(wsT, tril), W=(WgT,))
            rmsnorm_tiles([x_own[j * 128:(j + 1) * 128, :] for j in range(NT)],
                          gb, xs_ring, h1_ring, junk, ptr_ring, hT, ss_ring)
            P2a.close()
            wslots = Ring([P.sb([128, 4096], BF16, dma=True) for _ in range(6)])
            vgain = P.sb([128, 2048], F32, dma=True)
            bsb = P.sb([128, 2048], F32, dma=True)
            tk.dma("sp", vgain[:], vgain_in, W=(vgain,), owner=vgain)
            tk.dma("sp", bsb[:], bsb_in, W=(bsb,), owner=bsb)
            kst_ring = Ring([P.sb([128, 1024], BF16, dma=True) for _ in range(2)])
            sq_ring = Ring([P.sb([128, 1024], BF16) for _ in range(2)])
            pend_sq = None
            for c in range(48):
                ws = wslots.next()
                tk.dma("pool", ws[:], wfm_own[c], W=(ws,), owner=ws)
                kst = kst_ring.next()
                if c < 16:
                    fm_chunk(ws, hT, 1024, acc_ring,
                             lambda acc, hh: tk.op("act", lambda e: e.activation(out=kst[:, hh * 512:(hh + 1) * 512], in_=acc[:],
                                                                                 func=AF.Gelu_apprx_tanh), R=(acc,), W=(kst,)))
                else:
                    fm_chunk(ws, hT, 1024, acc_ring,
                             lambda acc, hh: tk.op("act", lambda e: e.copy(out=kst[:, hh * 512:(hh + 1) * 512], in_=acc[:]),
                                                   R=(acc,), W=(kst,)))
                tk.dma("sp", OWN[c], kst[:], R=(kst,), owner=kst)
                if pend_sq is not None:
                    sqnorm_b(pend_sq, nps, qmax)
                    pend_sq = None
                if 16 <= c < 32:
                    pend_sq = sqnorm_a(kst, sq_ring)
            wsw = wslots.next()
            tk.dma("pool", wsw[:, 0:1024], wtm_wi, W=(wsw,), owner=wsw)
            for j in range(NT):
                acc = acc_ring.next()
                tk.mm(acc, [(acc[:, 0:32], hT[:, kc, j * 128:(j + 1) * 128], wsw[:, kc * 32:(kc + 1) * 32],
                             kc == 0, kc == 31, (wsw, hT)) for kc in range(32)])
                tk.op("act", lambda e: e.mul(out=wi[:, j, :], in_=acc[:, 0:32], mul=float(32 ** -0.5 * 64 ** -0.5)),
                      R=(acc,), W=(wi,))
            for kt_ in kst_ring.tiles:
                for tok in list(kt_.r.values()):
                    tk.E["sp"].wait(tok)
            gl_ring = Ring([P.sb([128, 512], F32) for _ in range(2)])
            sq2_ring = Ring([P.sb([128, 512], F32) for _ in range(2)])
            st_ring = Ring([P.sb([128, 16], F32) for _ in range(2)])
            vn_ring = Ring([P.sb([128, 512], BF16) for _ in range(2)])
            zt_ring = Ring([P.sb([128, 512], F32) for _ in range(2)])
            gu_ring = Ring([P.sb([128, 4, 128], BF16, dma=True) for _ in range(3)])
            for cc in range(4):
                pcs = []
                for q in range(4):
                    ws = wslots.next()
                    tk.dma("pool", ws[:], wtm_vg[cc, q], W=(ws,), owner=ws)
                    pcs.append(ws)
                for j in range(NT):
                    gu = gu_ring.next()
                    tk.dma("sp", gu[:], OWN[cc * 4:(cc + 1) * 4, :, j * 128:(j + 1) * 128].rearrange("c p t -> p c t"),
                           W=(gu,), owner=gu)
                    acc = acc_ring.next()
                    tm_tile(pcs, hT, j, acc)
                    gl = gl_ring.next()
                    sq2 = sq2_ring.next()
                    st = st_ring.next()
                    vn = vn_ring.next()
                    tk.op("act", lambda e: e.activation(out=gl[:], in_=acc[:], func=AF.Gelu_apprx_tanh), R=(acc,), W=(gl,))
                    tk.op("act", lambda e: e.activation(out=sq2[:], in_=gl[:], func=AF.Square), R=(gl,), W=(sq2,))
                    tk.op("dve", lambda e: e.tensor_reduce(out=st[:, 0:4], in_=gl[:].rearrange("p (g c) -> p g c", g=4),
                                                            axis=AX.X, op=ALU.add), R=(gl,), W=(st,))
                    tk.op("dve", lambda e: e.tensor_reduce(out=st[:, 4:8], in_=sq2[:].rearrange("p (g c) -> p g c", g=4),
                                                            axis=AX.X, op=ALU.add), R=(sq2, st), W=(st,))
                    tk.op("dve", lambda e: e.tensor_scalar(out=st[:, 0:4], in0=st[:, 0:4], scalar1=1.0 / 128, scalar2=None,
                                                            op0=ALU.mult), R=(st,), W=(st,))
                    tk.op("dve", lambda e: e.tensor_tensor(out=st[:, 8:12], in0=st[:, 0:4], in1=st[:, 0:4], op=ALU.mult),
                          R=(st,), W=(st,))
                    tk.op("dve", lambda e: e.scalar_tensor_tensor(out=st[:, 12:16], in0=st[:, 4:8], scalar=1.0 / 128,
                                                                   in1=st[:, 8:12], op0=ALU.mult, op1=ALU.subtract),
                          R=(st,), W=(st,))
                    tk.op("act", lambda e: e.activation(out=st[:, 8:12], in_=st[:, 12:16], func=AF.Sqrt, bias=epsT[:, 0:1]),
                          R=(st, epsT), W=(st,))
                    tk.op("dve", lambda e: e.reciprocal(out=st[:, 12:16], in_=st[:, 8:12]), R=(st,), W=(st,))
                    for gi in range(4):
                        tk.op("dve", lambda e: e.tensor_scalar(out=gl[:, gi * 128:(gi + 1) * 128],
                                                                in0=gl[:, gi * 128:(gi + 1) * 128],
                                                                scalar1=st[:, gi:gi + 1], scalar2=st[:, 12 + gi:13 + gi],
                                                                op0=ALU.subtract, op1=ALU.mult), R=(gl, st), W=(gl,))
                    tk.op("dve", lambda e: e.tensor_tensor(out=vn[:], in0=gl[:], in1=vgain[:, cc * 512:(cc + 1) * 512],
                                                            op=ALU.mult), R=(gl, vgain), W=(vn,))
                    acc2 = acc_ring.next()
                    for gi in range(4):
                        g = cc * 4 + gi
                        tk.mm(acc2, [(acc2[:, gi * 128:(gi + 1) * 128], vn[:, gi * 128:(gi + 1) * 128],
                                      WgT[:, g * 128:(g + 1) * 128], True, True, (vn, WgT))], cont=(gi > 0))
                    zt = zt_ring.next()
                    tk.op("dve", lambda e: e.tensor_tensor(out=zt[:], in0=acc2[:], in1=bsb[:, cc * 512:(cc + 1) * 512],
                                                            op=ALU.add), R=(acc2, bsb), W=(zt,))
                    tk.op("dve", lambda e: e.tensor_tensor(
                        out=mix_gm[:, cc * 4:(cc + 1) * 4, j * 128:(j + 1) * 128],
                        in0=zt[:].rearrange("p (g t) -> p g t", g=4),
                        in1=gu[:], op=ALU.mult),
                        R=(zt, gu), W=(mix_gm,))
            tk.op("dve", lambda e: e.tensor_tensor(out=tmp1[:], in0=qmax[:], in1=kmax[:], op=ALU.mult),
                  R=(qmax, kmax), W=(tmp1,))
            tk.op("act", lambda e: e.activation(out=tmp1[:], in_=tmp1[:], func=AF.Sqrt), R=(tmp1,), W=(tmp1,))
            tk.op("dve", lambda e: e.tensor_scalar(out=tmp1[:], in0=tmp1[:], scalar1=-1.05 * 128 ** -0.5, scalar2=None,
                                                    op0=ALU.mult), R=(tmp1,), W=(tmp1,))
            tmpb = P.sb([1, 2], BF16)
            tk.op("dve", lambda e: e.tensor_copy(out=tmpb[:, 0:1], in_=tmp1[:]), R=(tmp1,), W=(tmpb,))
            tk.mm(nps, [(nps[:, 0:1], onesb[0:1, :], tmpb[0:1, 0:1], True, True, (onesb, tmpb))])
            tk.op("act", lambda e: e.copy(out=negb[:], in_=nps[:, 0:1]), R=(nps,), W=(negb,))
        P.close()

        M2 = Pool_(tk, nc)
        mix_att = M2.sb([128, 16, NT * 128], BF16)

        P = Pool_(tk, nc)
        if STAGE >= 3:
            kidxA = P.sb([128, L], BF16, dma=True)
            kidxB = P.sb([128, L], BF16, dma=True)
            tk.op("dve", lambda e: e.memset(kidxA[64:128, :], 0.0), W=(kidxA,))
            tk.op("dve", lambda e: e.memset(kidxB[0:64, :], 0.0), W=(kidxB,))
            tk.dma("sp", kidxA[0:64, :], kidxTs[0:64, :], W=(kidxA,), owner=kidxA)
            tk.dma("sp", kidxB[64:128, :], kidxTs[64:128, :], W=(kidxB,), owner=kidxB)
            qi_ring = Ring([P.sb([128, 16, 128], BF16, dma=True) for _ in range(1)])
            qt_ring = Ring([P.sb([128, 16, 128], BF16, dma=True) for _ in range(1)])
            Dg_ring = Ring([P.sb([128, 32, 128], BF16) for _ in range(1)])
            sc_ring = Ring([P.sb([128, L], F32, dma=True) for _ in range(2)])
            work = P.sb([128, L], F32)
            nm_in_ring = Ring([P.sb([128, 512], BF16, dma=True) for _ in range(2)])
            oh_ring = Ring([P.sb([128, 128], F32) for _ in range(3)])
            dh_ring = Ring([P.sb([128, 128], F32) for _ in range(3)])
            nm = P.sb([128, L], BF16, dma=True)
            nmT = P.sb([128, 32, 128], BF16)
            m8_ring = Ring([P.sb([128, 8], F32) for _ in range(2)])
            thr_ring = Ring([P.sb([128, 1], F32) for _ in range(2)])
            R_ring = Ring([P.sb([128, 512], BF16) for _ in range(4)])
            kt_ring = Ring([P.sb([128, L], BF16, dma=True) for _ in range(2)])
            v_ring = Ring([P.sb([128, 32, 128], BF16, dma=True) for _ in range(2)])
            pt_ring = Ring([P.sb([128, 512], BF16) for _ in range(3)])
            Lps_ring = Ring([P.ps([128, 512], F32) for _ in range(4)])
            dps_ring = Ring([P.ps([128, 512], F32) for _ in range(1)])
            accs_ring = Ring([P.ps([128, 512], F32) for _ in range(1)])
            sps_ring = Lps_ring
            ops_ring = Ring([P.ps([128, 512], F32) for _ in range(1)])
            tps_ring = Ring([P.ps([128, 1024], BF16) for _ in range(1)])

            def scores(m):
                nb = 4 * (m + 1)
                qi = qi_ring.next()
                tk.dma("sp", qi[:], OWN[32:48, :, m * 128:(m + 1) * 128].rearrange("c p t -> p c t"), W=(qi,), owner=qi)
                Dg = Dg_ring.next()
                for h in range(32):
                    tk.op("pool", lambda e: e.tensor_scalar(out=Dg[:, h, :], in0=identb[:], scalar1=wi[:, m, h:h + 1],
                                                             scalar2=None, op0=ALU.mult), R=(identb, wi), W=(Dg,))
                sc = sc_ring.next()
                for sg in range(nb // 4):
                    acc = accs_ring.next()
                    pend = []
                    LOOK = 2
                    last = (sg == nb // 4 - 1)
                    for h in range(32 + LOOK):
                        if h < 32:
                            lp = Lps_ring.next()
                            kx = kidxA if h % 2 == 0 else kidxB
                            tk.mm(lp, [(lp[:], qi[:, h // 2, :], kx[:, sg * 512:(sg + 1) * 512],
                                        True, True, (qi, kx))])
                            Rt = R_ring.next()
                            tk.op("act", lambda e: e.activation(out=Rt[:], in_=lp[:], func=AF.Relu), R=(lp,), W=(Rt,))
                            pend.append((h, Rt))
                        if h >= LOOK:
                            hh, Rr = pend.pop(0)
                            tk.mm(acc, [(acc[:], Dg[:, hh, :], Rr[:], hh == 0, hh == 31 and not last, (Dg, Rr))],
                                  cont=(hh > 0))
                    if sg == nb // 4 - 1:
                        nmi = nm_in_ring.next()
                        tk.dma("pool", nmi[:], negmask_in[m], W=(nmi,), owner=nmi)
                        tk.mm(acc, [(acc[:], identb[:], nmi[:], False, True, (identb, nmi))], cont=True)
                    tk.op("act", lambda e: e.copy(out=sc[:, sg * 512:(sg + 1) * 512], in_=acc[:]), R=(acc,), W=(sc,))
                return sc

            def topk(m, sc):
                nb = 4 * (m + 1)
                S = nb * 128
                m8 = None
                for rnd in range(32):
                    m8 = m8_ring.next()
                    src = sc if rnd == 0 else work
                    tk.op("dve", lambda e: e.max(out=m8[:], in_=src[:, 0:S]), R=(src,), W=(m8,))
                    if rnd < 31:
                        if rnd == 0:
                            tk.op("dve", lambda e: e.match_replace(out=work[:, 0:S], in_to_replace=m8[:],
                                                                    in_values=src[:, 0:S], imm_value=NEG),
                                  R=(m8, src), W=(work,))
                        else:
                            tk.op("dve", lambda e: e.match_replace(out=work[:, 0:S], in_to_replace=m8[:],
                                                                    in_values=work[:, 0:S], imm_value=NEG),
                                  R=(m8, work), W=(work,))
                thr = thr_ring.next()
                tk.op("dve", lambda e: e.tensor_scalar(out=thr[:], in0=m8[:, 7:8], scalar1=-1e29, scalar2=None, op0=ALU.max),
                      R=(m8,), W=(thr,))
                tk.op("dve", lambda e: e.tensor_scalar(out=nm[:, 0:S], in0=sc[:, 0:S], scalar1=thr[:, 0:1], scalar2=-30000.0,
                                                        op0=ALU.is_lt, op1=ALU.mult), R=(sc, thr), W=(nm,))
                if DEBUG:
                    tk.dma("sp", scdbg[m, :, 0:S], sc[:, 0:S], R=(sc,), owner=sc)
                    tk.dma("sp", nmdbg[m, :, 0:S], nm[:, 0:S], R=(nm,), owner=nm)

            def attention_pre(m):
                nb = 4 * (m + 1)
                for b0 in range(0, nb, 8):
                    nbb = min(8, nb - b0)
                    tp = tps_ring.next()
                    tk.tr(tp, [(tp[:, j * 128:(j + 1) * 128], nm[:, (b0 + j) * 128:(b0 + j + 1) * 128], identb[:], (nm, identb))
                               for j in range(nbb)])
                    tk.op("act", lambda e: e.copy(out=nmT[:, b0:b0 + nbb, :],
                                                   in_=tp[:, 0:nbb * 128].rearrange("p (j t) -> p j t", j=nbb)),
                          R=(tp,), W=(nmT,))

            def attention(m):
                nb = 4 * (m + 1)
                S = nb * 128
                qt = qt_ring.next()
                tk.dma("sp", qt[:], OWN[16:32, :, m * 128:(m + 1) * 128].rearrange("h p t -> p h t"), W=(qt,), owner=qt)
                for h in range(16):
                    kt = kt_ring.next()
                    tk.dma("sp", kt[:, 0:S], KTs[h, :, 0:S], W=(kt,), owner=kt)
                    vt = v_ring.next()
                    tk.dma("sp", vt[:, 0:nb, :], Vs[h, :, 0:nb, :], W=(vt,), owner=vt)
                    ops = ops_ring.next()
                    dps = dps_ring.next()
                    prev = None
                    nsg = nb // 4
                    for sg in range(nsg + 1):
                        if sg < nsg:
                            sp = sps_ring.next()
                            mms = []
                            for j in range(4):
                                blk = sg * 4 + j
                                mms.append((sp[:, j * 128:(j + 1) * 128], kt[:, blk * 128:(blk + 1) * 128], qt[:, h, :],
                                            True, False, (kt, qt)))
                                mms.append((sp[:, j * 128:(j + 1) * 128], identb[:], nmT[:, blk, :], False, True,
                                            (identb, nmT)))
                            tk.mm(sp, mms)
                            pt = pt_ring.next()
                            tk.op("act", lambda e: e.activation(out=pt[:], in_=sp[:], func=AF.Exp, bias=negb[:, 0:1],
                                                                 scale=float(128 ** -0.5)), R=(sp, negb), W=(pt,))
                        if prev is not None:
                            sg2, pt2 = prev
                            mms = []
                            mmd = []
                            for j in range(4):
                                blk = sg2 * 4 + j
                                mms.append((ops[:, 0:128], vt[:, blk, :], pt2[:, j * 128:(j + 1) * 128],
                                            blk == 0, blk == nb - 1, (vt, pt2)))
                                mmd.append((dps[:, 0:128], onesb[:], pt2[:, j * 128:(j + 1) * 128],
                                            blk == 0, blk == nb - 1, (onesb, pt2)))
                            tk.mm(ops, mms, cont=(sg2 > 0))
                            tk.mm(dps, mmd, cont=(sg2 > 0))
                        prev = (sg, pt) if sg < nsg else None
                    oh = oh_ring.next()
                    dh = dh_ring.next()
                    tk.op("act", lambda e: e.copy(out=oh[:], in_=ops[:, 0:128]), R=(ops,), W=(oh,))
                    tk.op("act", lambda e: e.activation(out=dh[:], in_=dps[:, 0:128], func=AF.Ln), R=(dps,), W=(dh,))
                    tk.op("act", lambda e: e.activation(out=dh[:], in_=dh[:], func=AF.Exp, scale=-1.0), R=(dh,), W=(dh,))
                    tk.op("pool", lambda e: e.tensor_tensor(out=mix_att[:, h, m * 128:(m + 1) * 128], in0=oh[:],
                                                             in1=dh[:], op=ALU.mult), R=(oh, dh), W=(mix_att,))

            sc_list = {0: scores(0), 1: scores(1)}
            topk(0, sc_list[0])
            for m in range(NT):
                attention_pre(m)
                if m + 1 < NT:
                    topk(m + 1, sc_list[m + 1])
                if m + 2 < NT:
                    sc_list[m + 2] = scores(m + 2)
                attention(m)
        P.close()

        if DEBUG and STAGE >= 3:
            Dp = Pool_(tk, nc)
            own = Dp.sb([1, 1], F32, dma=True)
            tk.dma("sp", mixdbg[:, 0:16, :], mix_att[:], R=(mix_att,), owner=own)
            tk.dma("sp", mixdbg[:, 16:32, :], mix_gm[:], R=(mix_gm,), owner=own)
            Dp.close()

        P = Pool_(tk, nc)
        if STAGE >= 4:
            wslots = Ring([P.sb([128, 4096], BF16, dma=True) for _ in range(8)])
            xr_ring = Ring([P.sb([128, 512], F32, dma=True) for _ in range(3)])
            x1_ring = Ring([P.sb([128, 512], F32, dma=True) for _ in range(3)])
            acc_ring = Ring([P.ps([128, 512], F32) for _ in range(6)])
            for ncn in range(8):
                pcs = []
                for q in range(4):
                    ws = wslots.next()
                    tk.dma("pool", ws[:], wtm_out[ncn, q], W=(ws,), owner=ws)
                    pcs.append(ws)
                for j in range(NT):
                    xr = xr_ring.next()
                    tk.dma("sp", xr[:], x_own[j * 128:(j + 1) * 128, ncn * 512:(ncn + 1) * 512], W=(xr,), owner=xr)
                    acc = acc_ring.next()
                    mms = []
                    for kc in range(32):
                        src = mix_att if kc < 16 else mix_gm
                        mms.append((acc[:], src[:, kc % 16, j * 128:(j + 1) * 128],
                                    pcs[kc // 8][:, (kc % 8) * 512:(kc % 8 + 1) * 512], kc == 0, kc == 31, (pcs[kc // 8], src)))
                    tk.mm(acc, mms)
                    x1t = x1_ring.next()
                    tk.op("dve", lambda e: e.tensor_tensor(out=x1t[:], in0=acc[:], in1=xr[:], op=ALU.add),
                          R=(acc, xr), W=(x1t,))
                    tk.dma("sp", x1s[j * 128:(j + 1) * 128, ncn * 512:(ncn + 1) * 512], x1t[:], R=(x1t,), owner=x1t)
        P.close()
        M2.close()
        M1.close()

        for half in range(2 if STAGE >= 5 else 0):
            P = Pool_(tk, nc)
            AT = P.sb([128, NFC, 512], BF16)
            Pg = Pool_(tk, nc)
            h2T = Pg.sb([128, 32, 512], BF16)
            Pn = Pool_(tk, nc)
            gb = Pn.sb([128, D], F32, dma=True)
            tk.dma("sp", gb[:], gffn_in, W=(gb,), owner=gb)
            xs_ring = Ring([Pn.sb([128, D], F32, dma=True) for _ in range(2)])
            h1_ring = Ring([Pn.sb([128, D], BF16) for _ in range(1)])
            junk = Pn.sb([128, D], BF16)
            ss_ring = Ring([Pn.sb([128, 4], F32) for _ in range(4)])
            ptr_ring = Ring([Pn.ps([128, 1024], BF16) for _ in range(2)])
            rmsnorm_tiles([x1s[(half * 4 + tt) * 128:(half * 4 + tt + 1) * 128, :] for tt in range(4)],
                          gb, xs_ring, h1_ring, junk, ptr_ring, h2T, ss_ring)
            Pn.close()
            wslots = Ring([Pg.sb([128, 4096], BF16, dma=True) for _ in range(6)])
            sg_ring = Ring([Pg.sb([128, 512], F32) for _ in range(2)])
            acc_ring = Ring([Pg.ps([128, 512], F32) for _ in range(6)])
            for fc in range(NFC):
                wg = wslots.next()
                tk.dma("pool", wg[:], wfm_g[fc], W=(wg,), owner=wg)
                wu = wslots.next()
                tk.dma("pool", wu[:], wfm_u[fc], W=(wu,), owner=wu)
                ag = acc_ring.next()
                tk.mm(ag, [(ag[:], wg[:, kc * 128:(kc + 1) * 128], h2T[:, kc, :], kc == 0, kc == 31, (wg, h2T)) for kc in range(32)])
                au = acc_ring.next()
                tk.mm(au, [(au[:], wu[:, kc * 128:(kc + 1) * 128], h2T[:, kc, :], kc == 0, kc == 31, (wu, h2T)) for kc in range(32)])
                sgt = sg_ring.next()
                tk.op("act", lambda e: e.activation(out=sgt[:], in_=ag[:], func=AF.Silu), R=(ag,), W=(sgt,))
                tk.op("dve", lambda e: e.tensor_tensor(out=AT[:, fc, :], in0=au[:], in1=sgt[:], op=ALU.mult),
                      R=(au, sgt), W=(AT,))
            Pg.close()
            Pd = Pool_(tk, nc)
            x2 = Pd.sb([128, 4, D], F32, dma=True)
            r0 = half * 512
            for tt in range(4):
                tk.dma("sp", x2[:, tt, :], x1s[r0 + tt * 128:r0 + (tt + 1) * 128, :], W=(x2,), owner=x2)
            wdslots = Ring([Pd.sb([128, 43 * 128], BF16, dma=True) for _ in range(4)])
            ys_ring = Ring([Pd.sb([128, 512], F32) for _ in range(2)])
            gfq_ring = Ring([Pd.sb([128, 1024], F32, dma=True) for _ in range(1)])
            yacc_ring = Ring([Pd.ps([128, 512], F32) for _ in range(4)])
            ytr_ring = Ring([Pd.ps([128, 512], F32) for _ in range(3)])
            ss_ring = Ring([Pd.sb([128, 4], F32) for _ in range(4)])
            for ncn in range(32):
                w0 = wdslots.next()
                tk.dma("pool", w0[:], wd_in[ncn, 0], W=(w0,), owner=w0)
                w1 = wdslots.next()
                tk.dma("pool", w1[:], wd_in[ncn, 1], W=(w1,), owner=w1)
                ya = yacc_ring.next()
                mms = []
                for fc in range(NFC):
                    wsrc = w0 if fc < 43 else w1
                    mms.append((ya[:], wsrc[:, (fc % 43) * 128:(fc % 43 + 1) * 128], AT[:, fc, :], fc == 0, fc == NFC - 1,
                                (wsrc, AT)))
                tk.mm(ya, mms)
                ys = ys_ring.next()
                tk.op("act", lambda e: e.copy(out=ys[:], in_=ya[:]), R=(ya,), W=(ys,))
                yt = ytr_ring.next()
                tk.tr(yt, [(yt[:, tt * 128:(tt + 1) * 128], ys[:, tt * 128:(tt + 1) * 128], identf[:], (ys, identf))
                           for tt in range(4)])
                tk.op("dve", lambda e: e.tensor_tensor(out=x2[:, :, ncn * 128:(ncn + 1) * 128],
                                                        in0=yt[:].rearrange("p (t n) -> p t n", t=4),
                                                        in1=x2[:, :, ncn * 128:(ncn + 1) * 128], op=ALU.add),
                      R=(yt, x2), W=(x2,))
            for tt in range(4):
                ss = ss_ring.next()
                tk.op("act", lambda e: e.activation(out=AT[:, 0:8, :].rearrange("p a b -> p (a b)"), in_=x2[:, tt, :],
                                                     func=AF.Square, scale=1.0 / 64.0, accum_out=ss[:, 0:1]),
                      R=(x2,), W=(AT, ss))
                tk.op("act", lambda e: e.activation(out=ss[:, 1:2], in_=ss[:, 0:1], func=AF.Sqrt, bias=epsT[:, 0:1]),
                      R=(ss, epsT), W=(ss,))
                tk.op("dve", lambda e: e.reciprocal(out=ss[:, 2:3], in_=ss[:, 1:2]), R=(ss,), W=(ss,))
                for q in range(4):
                    gfq = gfq_ring.next()
                    tk.dma("sp", gfq[:], gfin_in[:, q * 1024:(q + 1) * 1024], W=(gfq,), owner=gfq)
                    tk.op("dve", lambda e: e.scalar_tensor_tensor(out=x2[:, tt, q * 1024:(q + 1) * 1024],
                                                                   in0=x2[:, tt, q * 1024:(q + 1) * 1024], scalar=ss[:, 2:3],
                                                                   in1=gfq[:], op0=ALU.mult, op1=ALU.mult),
                          R=(x2, ss, gfq), W=(x2,))
            for tt in range(4):
                tk.dma("sp", out[r0 + tt * 128:r0 + (tt + 1) * 128, :], x2[:, tt, :], R=(x2,), owner=x2)
            Pd.close()
            P.close()
        G.close()
        tk.barrier()
    return nc


def _fm(Wc):
    n = Wc.shape[1]
    a = Wc.reshape(32, 128, n // 128, 128).transpose(2, 1, 0, 3)
    return np.ascontiguousarray(a).reshape(n // 128, 128, 32 * 128)


def _tm(Wc):
    n = Wc.shape[1]
    a = Wc.reshape(4, 8, 128, n // 512, 512).transpose(3, 0, 2, 1, 4)
    return np.ascontiguousarray(a).reshape(n // 512, 4, 128, 8 * 512)


def own_tiles(r):
    return [8 * (m // 2) + (r if m % 2 == 0 else 7 - r) for m in range(NT)]


_CACHE = {}


def kernel(x, norm_mix, w_in, gmlp_v_gain, w_spatial, b_spatial, w_out, norm_ffn, w_gate, w_up, w_down, norm_final):
    x = np.asarray(x, dtype=np.float32)
    w_in = np.asarray(w_in, dtype=np.float32)[0]
    w_out = np.asarray(w_out, dtype=np.float32)[0]
    w_gate = np.asarray(w_gate, dtype=np.float32)[0]
    w_up = np.asarray(w_up, dtype=np.float32)[0]
    w_down = np.asarray(w_down, dtype=np.float32)[0]
    f32 = np.float32
    shared = {}
    kcols = w_in[:, 2048:4096]
    kidxc = w_in[:, 8192:8256]
    shared["wfm_kv"] = np.concatenate([_fm(kcols), _fm(np.concatenate([kidxc, kidxc], axis=1))], axis=0)
    shared["wfm_own"] = np.concatenate([_fm(w_in[:, 8288:10336]), _fm(w_in[:, 0:2048]), _fm(w_in[:, 6144:8192])], axis=0)
    shared["wtm_v"] = _tm(w_in[:, 4096:6144])
    shared["wtm_vg"] = _tm(w_in[:, 10336:12384])
    shared["wtm_wi"] = np.ascontiguousarray(w_in[:, 8256:8288].reshape(32, 128, 32).transpose(1, 0, 2)).reshape(128, 32 * 32)
    shared["wtm_out"] = _tm(w_out)
    shared["wfm_g"] = _fm(w_gate)
    shared["wfm_u"] = _fm(w_up)
    wd = w_down.reshape(2, 43, 128, 32, 128).transpose(3, 0, 2, 1, 4)
    shared["wd"] = np.ascontiguousarray(wd).reshape(32, 2, 128, 43 * 128)
    shared["ident"] = np.eye(128, dtype=f32)
    shared["tril_st"] = np.triu(np.ones((128, 128), dtype=f32))
    shared["gmix_b"] = np.ascontiguousarray(np.broadcast_to(np.asarray(norm_mix, f32)[0][None, :], (128, D)))
    shared["gffn_b"] = np.ascontiguousarray(np.broadcast_to(np.asarray(norm_ffn, f32)[0][None, :], (128, D)))
    shared["gfin_b"] = np.ascontiguousarray(np.broadcast_to(np.asarray(norm_final, f32)[None, :], (128, D)))
    shared["vgain_b"] = np.ascontiguousarray(np.broadcast_to(np.asarray(gmlp_v_gain, f32)[0].reshape(1, 2048), (128, 2048)))
    shared["bsb"] = np.ascontiguousarray(np.broadcast_to(np.asarray(b_spatial, f32)[0].reshape(1, 2048), (128, 2048)))
    shared["wsT"] = np.ascontiguousarray(np.asarray(w_spatial, f32)[0].transpose(2, 0, 1)).reshape(128, 2048)

    in_maps = []
    for c in range(8):
        b, r = c // 4, c % 4
        tiles = own_tiles(r)
        xb = x[b]
        m = dict(shared)
        m["x_all"] = xb
        m["x_own"] = np.ascontiguousarray(np.concatenate([xb[t * 128:(t + 1) * 128] for t in tiles], axis=0))
        nmk = np.zeros((NT, 128, 512), dtype=f32)
        for mi, t in enumerate(tiles):
            nb = 4 * (mi + 1)
            qpos = t * 128 + np.arange(128)[:, None]
            kpos = (nb - 4) * 128 + np.arange(512)[None, :]
            nmk[mi] = np.where(kpos <= qpos, 0.0, NEG).astype(f32)
        m["negmask"] = nmk
        in_maps.append(m)

    if "nc" not in _CACHE:
        _CACHE["nc"] = build_program()
    nc = _CACHE["nc"]
    ncores = int(os.environ.get("K_NCORES", "8"))
    if ncores < 8:
        res = run_bass_kernel_spmd(nc, in_maps[:ncores], core_ids=list(range(ncores)))
        _CACHE["res"] = res
        return None
    res = run_bass_kernel_spmd(nc, in_maps, core_ids=list(range(8)))
    outp = np.zeros((2, L, D), dtype=np.float32)
    for c in range(8):
        b, r = c // 4, c % 4
        o = res.results[c]["out"]
        for mi, t in enumerate(own_tiles(r)):
            outp[b, t * 128:(t + 1) * 128] = o[mi * 128:(mi + 1) * 128]
    if DEBUG:
        _CACHE["res"] = res
    return outp
```

```python
import os
from contextlib import ExitStack
import numpy as np
import concourse.bass as bass
import concourse.mybir as mybir
from concourse.bass_utils import run_bass_kernel_spmd

F32 = mybir.dt.float32
BF16 = mybir.dt.bfloat16
AF = mybir.ActivationFunctionType
ALU = mybir.AluOpType
AX = mybir.AxisListType

D = 4096
L = 4096
DFF = 11008
NFC = 86
NT = 8
EPS = 1e-6
NEG = -1e30
STAGE = int(os.environ.get("K_STAGE", "99"))
DEBUG = int(os.environ.get("K_DEBUG", "0"))


class Eng:
    def __init__(self, name, e, sem):
        self.name, self.e, self.sem = name, e, sem
        self.cnt = 0
        self.waited = {}

    def wait(self, tok):
        if tok is None:
            return
        key, sem, val = tok
        if self.waited.get(key, 0) >= val:
            return
        self.e.wait_ge(sem, val)
        self.waited[key] = val

    def sig(self, ins):
        self.cnt += 1
        ins.then_inc(self.sem, 1)
        return (self.name, self.sem, self.cnt)


class T:
    _n = 0

    def __init__(self, ap, dsem=None):
        self.ap = ap
        self.w = None
        self.r = {}
        self.dsem = dsem
        self.dcnt = 0
        T._n += 1
        self.name = "t%d" % T._n

    def __getitem__(self, idx):
        return self.ap[idx]


class Tk:
    def __init__(self, nc, es):
        self.nc = nc
        self.es = es
        self.E = {}
        for name, e in (("pe", nc.tensor), ("act", nc.scalar), ("dve", nc.vector),
                        ("pool", nc.gpsimd), ("sp", nc.sync)):
            sem = es.enter_context(nc.semaphore("s_" + name))
            self.E[name] = Eng(name, e, sem)
        self.dma_latest = {}
        self.free_dsems = {"sp": [], "pool": []}
        self.nsem = 0

    def dsem(self, kind):
        if self.free_dsems[kind]:
            return self.free_dsems[kind].pop()
        self.nsem += 1
        return [self.es.enter_context(self.nc.semaphore("d%d" % self.nsem)), 0, "d%d" % self.nsem]

    def _pre(self, E, en, R, W):
        for t in R:
            E.wait(t.w)
        for t in W:
            E.wait(t.w)
            for k, tok in t.r.items():
                E.wait(tok)

    def op(self, en, fn, R=(), W=()):
        E = self.E[en]
        self._pre(E, en, R, W)
        ins = fn(E.e)
        tok = E.sig(ins)
        for t in R:
            t.r[en] = tok
        for t in W:
            t.w = tok
            t.r = {}
        return tok

    def mm(self, out_t, mms, cont=False):
        E = self.E["pe"]
        if not cont:
            self._pre(E, "pe", (), (out_t,))
        allR = []
        ins = None
        for (o, l, r, st, sp, R) in mms:
            for t in R:
                E.wait(t.w)
                allR.append(t)
            ins = E.e.matmul(o, lhsT=l, rhs=r, start=st, stop=sp)
        tok = E.sig(ins)
        for t in allR:
            t.r["pe"] = tok
        out_t.w = tok
        if not cont:
            out_t.r = {}
        return tok

    def tr(self, out_t, trs):
        E = self.E["pe"]
        self._pre(E, "pe", (), (out_t,))
        allR = []
        ins = None
        for (o, i, idn, R) in trs:
            for t in R:
                E.wait(t.w)
                allR.append(t)
            ins = E.e.transpose(o, i, idn)
        tok = E.sig(ins)
        for t in allR:
            t.r["pe"] = tok
        out_t.w = tok
        out_t.r = {}
        return tok

    def dma(self, qn, out_ap, in_ap, R=(), W=(), owner=None):
        Q = self.E[qn]
        self._pre(Q, "dma", R, W)
        ins = Q.e.dma_start(out=out_ap, in_=in_ap)
        if owner.dsem is None:
            owner.dsem = self.dsem(qn)
            owner.dkind = qn
            owner.pool.ds.append((qn, owner.dsem))
        assert owner.dkind == qn, "tile DMA semaphore is bound to one queue kind"
        ds = owner.dsem
        ds[1] += 16
        ins.then_inc(ds[0], 16)
        tok = (ds[2], ds[0], ds[1])
        for t in R:
            t.r["dma" + ds[2]] = tok
        for t in W:
            t.w = tok
            t.r = {}
        self.dma_latest[ds[2]] = tok
        return tok

    def barrier(self):
        sp = self.E["sp"]
        for tok in self.dma_latest.values():
            sp.wait(tok)
        toks = []
        for en, E in self.E.items():
            if en == "sp":
                continue
            if E.cnt > 0:
                E.e.wait_ge(E.sem, E.cnt)
            ins = E.e.sem_inc(E.sem, 1)
            E.cnt += 1
            toks.append((E.name, E.sem, E.cnt))
        for tok in toks:
            sp.wait(tok)
        if sp.cnt > 0:
            sp.e.wait_ge(sp.sem, sp.cnt)
        ins = sp.e.sem_inc(sp.sem, 1)
        sp.cnt += 1
        stok = (sp.name, sp.sem, sp.cnt)
        for en, E in self.E.items():
            if en != "sp":
                E.wait(stok)


class Pool_:
    cnt = 0

    def __init__(self, tk, nc):
        self.tk, self.nc = tk, nc
        self.es = ExitStack()
        self.ds = []
        self.n = 0

    def sb(self, shape, dt, dma=False, name=None):
        Pool_.cnt += 1
        h = self.es.enter_context(self.nc.sbuf_tensor("sb%d" % Pool_.cnt, shape, dt))
        t = T(h[:], None)
        t.pool = self
        return t

    def ps(self, shape, dt):
        Pool_.cnt += 1
        h = self.es.enter_context(self.nc.psum_tensor("ps%d" % Pool_.cnt, shape, dt))
        return T(h[:])

    def close(self):
        self.tk.barrier()
        for kind, ds in self.ds:
            self.tk.free_dsems[kind].append(ds)
        self.es.close()


class Ring:
    def __init__(self, tiles):
        self.tiles = tiles
        self.i = 0

    def next(self):
        t = self.tiles[self.i % len(self.tiles)]
        self.i += 1
        return t


def build_program():
    nc = bass.Bass("TRN2", target_bir_lowering=False)
    dk = "ExternalOutput" if DEBUG else "Internal"

    def din(name, shape, dt=F32):
        return nc.dram_tensor(name, shape, dt, kind="ExternalInput").ap()

    x_all = din("x_all", [L, D])
    x_own = din("x_own", [NT * 128, D])
    negmask_in = din("negmask", [NT, 128, 512])
    ident_in = din("ident", [128, 128])
    tril_in = din("tril_st", [128, 128])
    gmix_in = din("gmix_b", [128, D])
    gffn_in = din("gffn_b", [128, D])
    gfin_in = din("gfin_b", [128, D])
    vgain_in = din("vgain_b", [128, 2048])
    bsb_in = din("bsb", [128, 16 * 128])
    wsT_in = din("wsT", [128, 16 * 128])
    wfm_kv = din("wfm_kv", [17, 128, 32 * 128])
    wfm_own = din("wfm_own", [48, 128, 32 * 128])
    wtm_v = din("wtm_v", [4, 4, 128, 8 * 512])
    wtm_vg = din("wtm_vg", [4, 4, 128, 8 * 512])
    wtm_wi = din("wtm_wi", [128, 32 * 32])
    wtm_out = din("wtm_out", [8, 4, 128, 8 * 512])
    wfm_g = din("wfm_g", [NFC, 128, 32 * 128])
    wfm_u = din("wfm_u", [NFC, 128, 32 * 128])
    wd_in = din("wd", [32, 2, 128, 43 * 128])
    out = nc.dram_tensor("out", [NT * 128, D], F32, kind="ExternalOutput").ap()

    KTs = nc.dram_tensor("KTs", [16, 128, L], BF16, kind=dk).ap()
    kidxTs = nc.dram_tensor("kidxTs", [128, L], BF16, kind=dk).ap()
    Vs = nc.dram_tensor("Vs", [16, 128, 32, 128], BF16, kind=dk).ap()
    OWN = nc.dram_tensor("OWNs", [48, 128, NT * 128], BF16, kind=dk).ap()
    x1s = nc.dram_tensor("x1s", [NT * 128, D], F32, kind=dk).ap()
    if DEBUG:
        mixdbg = nc.dram_tensor("mixdbg", [128, 32, NT * 128], BF16, kind="ExternalOutput").ap()
        scdbg = nc.dram_tensor("scdbg", [NT, 128, L], F32, kind="ExternalOutput").ap()
        nmdbg = nc.dram_tensor("nmdbg", [NT, 128, L], BF16, kind="ExternalOutput").ap()

    with ExitStack() as ges:
        tk = Tk(nc, ges)
        G = Pool_(tk, nc)
        identf = G.sb([128, 128], F32, dma=True)
        identb = G.sb([128, 128], BF16, dma=True)
        onesb = G.sb([128, 128], BF16)
        kmax = G.sb([1, 1], F32)
        qmax = G.sb([1, 1], F32)
        negb = G.sb([128, 1], F32)
        tmp1 = G.sb([1, 1], F32)
        epsT = G.sb([128, 1], F32)

        tk.dma("sp", identf[:], ident_in, W=(identf,), owner=identf)
        tk.dma("pool", identb[:], ident_in, W=(identb,), owner=identb)
        tk.op("dve", lambda e: e.memset(onesb[:], 1.0), W=(onesb,))
        tk.op("dve", lambda e: e.memset(kmax[:], 0.0), W=(kmax,))
        tk.op("dve", lambda e: e.memset(qmax[:], 0.0), W=(qmax,))
        tk.op("dve", lambda e: e.memset(epsT[:], EPS), W=(epsT,))

        def norm_stats(x_src_ap, xs_ring, junk, ss_ring):
            xs = xs_ring.next()
            tk.dma("sp", xs[:], x_src_ap, W=(xs,), owner=xs)
            ss = ss_ring.next()
            tk.op("act", lambda e: e.activation(out=junk[:], in_=xs[:], func=AF.Square, scale=1.0 / 64.0,
                                                 accum_out=ss[:, 0:1]), R=(xs,), W=(junk, ss))
            tk.op("act", lambda e: e.activation(out=ss[:, 1:2], in_=ss[:, 0:1], func=AF.Sqrt, bias=epsT[:, 0:1]),
                  R=(ss, epsT), W=(ss,))
            tk.op("dve", lambda e: e.reciprocal(out=ss[:, 2:3], in_=ss[:, 1:2]), R=(ss,), W=(ss,))
            return xs, ss

        def norm_evac(xs, ss, gb, h1_ring, ptr_ring, hT, tcol):
            h1 = h1_ring.next()
            tk.op("dve", lambda e: e.scalar_tensor_tensor(out=h1[:], in0=xs[:], scalar=ss[:, 2:3], in1=gb[:],
                                                            op0=ALU.mult, op1=ALU.mult), R=(xs, ss, gb), W=(h1,))
            for q in range(4):
                pt = ptr_ring.next()
                tk.tr(pt, [(pt[:, j * 128:(j + 1) * 128], h1[:, (q * 8 + j) * 128:(q * 8 + j + 1) * 128], identb[:],
                            (h1, identb)) for j in range(8)])
                if q % 2 == 0:
                    tk.op("act", lambda e: e.copy(out=hT[:, q * 8:(q + 1) * 8, tcol:tcol + 128],
                                                   in_=pt[:].rearrange("p (j t) -> p j t", j=8)), R=(pt,), W=(hT,))
                else:
                    tk.op("dve", lambda e: e.tensor_copy(out=hT[:, q * 8:(q + 1) * 8, tcol:tcol + 128],
                                                          in_=pt[:].rearrange("p (j t) -> p j t", j=8)), R=(pt,), W=(hT,))

        def rmsnorm_tiles(srcs, gb, xs_ring, h1_ring, junk, ptr_ring, hT, ss_ring):
            assert len(xs_ring.tiles) >= 2
            pend = norm_stats(srcs[0], xs_ring, junk, ss_ring)
            for j in range(len(srcs)):
                nxt = norm_stats(srcs[j + 1], xs_ring, junk, ss_ring) if j + 1 < len(srcs) else None
                norm_evac(pend[0], pend[1], gb, h1_ring, ptr_ring, hT, j * 128)
                pend = nxt

        def sqnorm_a(src_t, sq_ring):
            sq_t = sq_ring.next()
            tk.op("act", lambda e: e.activation(out=sq_t[:, 0:1024], in_=src_t[:, 0:1024], func=AF.Square),
                  R=(src_t,), W=(sq_t,))
            tk.op("dve", lambda e: e.tensor_tensor(out=sq_t[:, 0:512], in0=sq_t[:, 0:512], in1=sq_t[:, 512:1024],
                                                    op=ALU.add), R=(sq_t,), W=(sq_t,))
            return sq_t

        def sqnorm_b(sq_t, nps, run_max):
            tk.mm(nps, [(nps[0:1, 0:512], onesb[:, 0:1], sq_t[:, 0:512], True, True, (sq_t, onesb))])
            tk.op("dve", lambda e: e.tensor_reduce(out=tmp1[:], in_=nps[0:1, 0:512], axis=AX.X, op=ALU.max),
                  R=(nps,), W=(tmp1,))
            tk.op("dve", lambda e: e.tensor_tensor(out=run_max[:], in0=run_max[:], in1=tmp1[:], op=ALU.max),
                  R=(tmp1, run_max), W=(run_max,))

        def fm_chunk(ws, hT, ntok, acc_ring, evac):
            for hh in range(ntok // 512):
                acc = acc_ring.next()
                tk.mm(acc, [(acc[:], ws[:, kc * 128:(kc + 1) * 128], hT[:, kc, hh * 512:(hh + 1) * 512],
                             kc == 0, kc == 31, (ws, hT)) for kc in range(32)])
                evac(acc, hh)

        def tm_tile(pcs, hT, j, acc):
            tk.mm(acc, [(acc[:], hT[:, kc, j * 128:(j + 1) * 128],
                         pcs[kc // 8][:, (kc % 8) * 512:(kc % 8 + 1) * 512],
                         kc == 0, kc == 31, (pcs[kc // 8], hT)) for kc in range(32)])

        P = Pool_(tk, nc)
        if STAGE >= 1:
            gb = P.sb([128, D], F32, dma=True)
            tk.dma("sp", gb[:], gmix_in, W=(gb,), owner=gb)
            hT = P.sb([128, 32, 1024], BF16)
            xs_ring = Ring([P.sb([128, D], F32, dma=True) for _ in range(2)])
            h1_ring = Ring([P.sb([128, D], BF16) for _ in range(2)])
            junk = P.sb([128, D], BF16)
            ss_ring = Ring([P.sb([128, 4], F32) for _ in range(4)])
            wslots = Ring([P.sb([128, 4096], BF16, dma=True) for _ in range(6)])
            kst_ring = Ring([P.sb([128, 1024], BF16, dma=True) for _ in range(2)])
            vst_ring = Ring([P.sb([128, 512], BF16, dma=True) for _ in range(3)])
            sq_ring = Ring([P.sb([128, 1024], BF16) for _ in range(2)])
            ptr_ring = Ring([P.ps([128, 1024], BF16) for _ in range(2)])
            acc_ring = Ring([P.ps([128, 512], F32) for _ in range(5)])
            nps = P.ps([128, 512], F32)
            pend_sq = None

            MASK = int(os.environ.get("K_P1MASK", "15"))
            NTB = int(os.environ.get("K_NTB", "4"))
            for tb in range(NTB):
                rmsnorm_tiles([x_all[tb * 1024 + j * 128:tb * 1024 + (j + 1) * 128, :] for j in range(8)],
                              gb, xs_ring, h1_ring, junk, ptr_ring, hT, ss_ring)
                for c in range(17 if MASK & 2 else 0):
                    ws = wslots.next()
                    tk.dma("pool", ws[:], wfm_kv[c], W=(ws,), owner=ws)
                    kst = kst_ring.next()
                    fm_chunk(ws, hT, 1024, acc_ring,
                             lambda acc, hh: tk.op("act", lambda e: e.copy(out=kst[:, hh * 512:(hh + 1) * 512], in_=acc[:]),
                                                   R=(acc,), W=(kst,)))
                    if pend_sq is not None:
                        sqnorm_b(pend_sq, nps, kmax)
                        pend_sq = None
                    if c < 16:
                        tk.dma("sp", KTs[c, :, tb * 1024:(tb + 1) * 1024], kst[:], R=(kst,), owner=kst)
                        pend_sq = sqnorm_a(kst, sq_ring)
                    else:
                        tk.dma("sp", kidxTs[:, tb * 1024:(tb + 1) * 1024], kst[:], R=(kst,), owner=kst)
                for cc in range(4 if MASK & 8 else 0):
                    pcs = []
                    for q in range(4):
                        ws = wslots.next()
                        tk.dma("pool", ws[:], wtm_v[cc, q], W=(ws,), owner=ws)
                        pcs.append(ws)
                    for j in range(8):
                        acc = acc_ring.next()
                        tm_tile(pcs, hT, j, acc)
                        vst = vst_ring.next()
                        tk.op("dve", lambda e: e.tensor_copy(out=vst[:], in_=acc[:]), R=(acc,), W=(vst,))
                        blk = tb * 8 + j
                        tk.dma("sp", Vs[cc * 4:(cc + 1) * 4, :, blk, :].rearrange("h p d -> p h d"),
                               vst[:].rearrange("p (h d) -> p h d", h=4), R=(vst,), owner=vst)
        P.close()

        M1 = Pool_(tk, nc)
        mix_gm = M1.sb([128, 16, NT * 128], BF16)
        wi = M1.sb([128, NT, 32], F32)

        P = Pool_(tk, nc)
        if STAGE >= 2:
            hT = P.sb([128, 32, 1024], BF16)
            acc_ring = Ring([P.ps([128, 512], F32) for _ in range(5)])
            nps = P.ps([128, 512], F32)
            WgT = P.sb([128, 2048], BF16)
            P2a = Pool_(tk, nc)
            gb = P2a.sb([128, D], F32, dma=True)
            tk.dma("sp", gb[:], gmix_in, W=(gb,), owner=gb)
            xs_ring = Ring([P2a.sb([128, D], F32, dma=True) for _ in range(2)])
            h1_ring = Ring([P2a.sb([128, D], BF16) for _ in range(1)])
            junk = P2a.sb([128, D], BF16)
            ss_ring = Ring([P2a.sb([128, 4], F32) for _ in range(4)])
            ptr_ring = Ring([P2a.ps([128, 1024], BF16) for _ in range(2)])
            wsT = xs_ring.tiles[0]
            tril = P2a.sb([128, 128], F32, dma=True)
            tk.dma("sp", tril[:], tril_in, W=(tril,), owner=tril)
            tk.dma("sp", wsT[:, 0:2048], wsT_in, W=(wsT,), owner=wsT)
            for g in range(16):
                tk.op("dve", lambda e: e.tensor_tensor(out=WgT[:, g * 128:(g + 1) * 128],
                                                        in0=wsT[:, g * 128:(g + 1) * 128], in1=tril[:], op=ALU.mult),
                      R=(wsT, tril), W=(WgT,))
            rmsnorm_tiles([x_own[j * 128:(j + 1) * 128, :] for j in range(NT)],
                          gb, xs_ring, h1_ring, junk, ptr_ring, hT, ss_ring)
            P2a.close()
            wslots = Ring([P.sb([128, 4096], BF16, dma=True) for _ in range(6)])
            vgain = P.sb([128, 2048], F32, dma=True)
            bsb = P.sb([128, 2048], F32, dma=True)
            tk.dma("sp", vgain[:], vgain_in, W=(vgain,), owner=vgain)
            tk.dma("sp", bsb[:], bsb_in, W=(bsb,), owner=bsb)
            kst_ring = Ring([P.sb([128, 1024], BF16, dma=True) for _ in range(2)])
            sq_ring = Ring([P.sb([128, 1024], BF16) for _ in range(2)])
            pend_sq = None
            for c in range(48):
                ws = wslots.next()
                tk.dma("pool", ws[:], wfm_own[c], W=(ws,), owner=ws)
                kst = kst_ring.next()
                if c < 16:
                    fm_chunk(ws, hT, 1024, acc_ring,
                             lambda acc, hh: tk.op("act", lambda e: e.activation(out=kst[:, hh * 512:(hh + 1) * 512], in_=acc[:],
                                                                                 func=AF.Gelu_apprx_tanh), R=(acc,), W=(kst,)))
                else:
                    fm_chunk(ws, hT, 1024, acc_ring,
                             lambda acc, hh: tk.op("act", lambda e: e.copy(out=kst[:, hh * 512:(hh + 1) * 512], in_=acc[:]),
                                                   R=(acc,), W=(kst,)))
                tk.dma("sp", OWN[c], kst[:], R=(kst,), owner=kst)
                if pend_sq is not None:
                    sqnorm_b(pend_sq, nps, qmax)
                    pend_sq = None
                if 16 <= c < 32:
                    pend_sq = sqnorm_a(kst, sq_ring)
            wsw = wslots.next()
            tk.dma("pool", wsw[:, 0:1024], wtm_wi, W=(wsw,), owner=wsw)
            for j in range(NT):
                acc = acc_ring.next()
                tk.mm(acc, [(acc[:, 0:32], hT[:, kc, j * 128:(j + 1) * 128], wsw[:, kc * 32:(kc + 1) * 32],
                             kc == 0, kc == 31, (wsw, hT)) for kc in range(32)])
                tk.op("act", lambda e: e.mul(out=wi[:, j, :], in_=acc[:, 0:32], mul=float(32 ** -0.5 * 64 ** -0.5)),
                      R=(acc,), W=(wi,))
            for kt_ in kst_ring.tiles:
                for tok in list(kt_.r.values()):
                    tk.E["sp"].wait(tok)
            gl_ring = Ring([P.sb([128, 512], F32) for _ in range(2)])
            sq2_ring = Ring([P.sb([128, 512], F32) for _ in range(2)])
            st_ring = Ring([P.sb([128, 16], F32) for _ in range(2)])
            vn_ring = Ring([P.sb([128, 512], BF16) for _ in range(2)])
            zt_ring = Ring([P.sb([128, 512], F32) for _ in range(2)])
            gu_ring = Ring([P.sb([128, 4, 128], BF16, dma=True) for _ in range(3)])
            def gm_stage_a(cc, j, pcs):
                gu = gu_ring.next()
                tk.dma("sp", gu[:], OWN[cc * 4:(cc + 1) * 4, :, j * 128:(j + 1) * 128].rearrange("c p t -> p c t"),
                       W=(gu,), owner=gu)
                acc = acc_ring.next()
                tm_tile(pcs, hT, j, acc)
                gl = gl_ring.next()
                sq2 = sq2_ring.next()
                st = st_ring.next()
                vn = vn_ring.next()
                tk.op("act", lambda e: e.activation(out=gl[:], in_=acc[:], func=AF.Gelu_apprx_tanh), R=(acc,), W=(gl,))
                tk.op("act", lambda e: e.activation(out=sq2[:], in_=gl[:], func=AF.Square), R=(gl,), W=(sq2,))
                tk.op("dve", lambda e: e.tensor_reduce(out=st[:, 0:4], in_=gl[:].rearrange("p (g c) -> p g c", g=4),
                                                        axis=AX.X, op=ALU.add), R=(gl,), W=(st,))
                tk.op("dve", lambda e: e.tensor_reduce(out=st[:, 4:8], in_=sq2[:].rearrange("p (g c) -> p g c", g=4),
                                                        axis=AX.X, op=ALU.add), R=(sq2, st), W=(st,))
                tk.op("dve", lambda e: e.tensor_scalar(out=st[:, 0:4], in0=st[:, 0:4], scalar1=1.0 / 128, scalar2=None,
                                                        op0=ALU.mult), R=(st,), W=(st,))
                tk.op("dve", lambda e: e.tensor_tensor(out=st[:, 8:12], in0=st[:, 0:4], in1=st[:, 0:4], op=ALU.mult),
                      R=(st,), W=(st,))
                tk.op("dve", lambda e: e.scalar_tensor_tensor(out=st[:, 12:16], in0=st[:, 4:8], scalar=1.0 / 128,
                                                               in1=st[:, 8:12], op0=ALU.mult, op1=ALU.subtract),
                      R=(st,), W=(st,))
                tk.op("act", lambda e: e.activation(out=st[:, 8:12], in_=st[:, 12:16], func=AF.Sqrt, bias=epsT[:, 0:1]),
                      R=(st, epsT), W=(st,))
                tk.op("dve", lambda e: e.reciprocal(out=st[:, 12:16], in_=st[:, 8:12]), R=(st,), W=(st,))
                for gi in range(4):
                    tk.op("dve", lambda e: e.tensor_scalar(out=gl[:, gi * 128:(gi + 1) * 128],
                                                            in0=gl[:, gi * 128:(gi + 1) * 128],
                                                            scalar1=st[:, gi:gi + 1], scalar2=st[:, 12 + gi:13 + gi],
                                                            op0=ALU.subtract, op1=ALU.mult), R=(gl, st), W=(gl,))
                tk.op("dve", lambda e: e.tensor_tensor(out=vn[:], in0=gl[:], in1=vgain[:, cc * 512:(cc + 1) * 512],
                                                        op=ALU.mult), R=(gl, vgain), W=(vn,))
                return vn, gu

            def gm_stage_b(cc, j, vn, gu):
                acc2 = acc_ring.next()
                for gi in range(4):
                    g = cc * 4 + gi
                    tk.mm(acc2, [(acc2[:, gi * 128:(gi + 1) * 128], vn[:, gi * 128:(gi + 1) * 128],
                                  WgT[:, g * 128:(g + 1) * 128], True, True, (vn, WgT))], cont=(gi > 0))
                zt = zt_ring.next()
                tk.op("dve", lambda e: e.tensor_tensor(out=zt[:], in0=acc2[:], in1=bsb[:, cc * 512:(cc + 1) * 512],
                                                        op=ALU.add), R=(acc2, bsb), W=(zt,))
                tk.op("dve", lambda e: e.tensor_tensor(
                    out=mix_gm[:, cc * 4:(cc + 1) * 4, j * 128:(j + 1) * 128],
                    in0=zt[:].rearrange("p (g t) -> p g t", g=4),
                    in1=gu[:], op=ALU.mult),
                    R=(zt, gu), W=(mix_gm,))

            for cc in range(4):
                pcs = []
                for q in range(4):
                    ws = wslots.next()
                    tk.dma("pool", ws[:], wtm_vg[cc, q], W=(ws,), owner=ws)
                    pcs.append(ws)
                prev = None
                for j in range(NT):
                    cur = gm_stage_a(cc, j, pcs)
                    if prev is not None:
                        gm_stage_b(cc, j - 1, prev[0], prev[1])
                    prev = cur
                gm_stage_b(cc, NT - 1, prev[0], prev[1])
            tk.op("dve", lambda e: e.tensor_tensor(out=tmp1[:], in0=qmax[:], in1=kmax[:], op=ALU.mult),
                  R=(qmax, kmax), W=(tmp1,))
            tk.op("act", lambda e: e.activation(out=tmp1[:], in_=tmp1[:], func=AF.Sqrt), R=(tmp1,), W=(tmp1,))
            tk.op("dve", lambda e: e.tensor_scalar(out=tmp1[:], in0=tmp1[:], scalar1=-1.05 * 128 ** -0.5, scalar2=None,
                                                    op0=ALU.mult), R=(tmp1,), W=(tmp1,))
            tmpb = P.sb([1, 2], BF16)
            tk.op("dve", lambda e: e.tensor_copy(out=tmpb[:, 0:1], in_=tmp1[:]), R=(tmp1,), W=(tmpb,))
            tk.mm(nps, [(nps[:, 0:1], onesb[0:1, :], tmpb[0:1, 0:1], True, True, (onesb, tmpb))])
            tk.op("act", lambda e: e.copy(out=negb[:], in_=nps[:, 0:1]), R=(nps,), W=(negb,))
        P.close()

        M2 = Pool_(tk, nc)
        mix_att = M2.sb([128, 16, NT * 128], BF16)

        P = Pool_(tk, nc)
        if STAGE >= 3:
            kidxA = P.sb([128, L], BF16, dma=True)
            kidxB = P.sb([128, L], BF16, dma=True)
            tk.op("dve", lambda e: e.memset(kidxA[64:128, :], 0.0), W=(kidxA,))
            tk.op("dve", lambda e: e.memset(kidxB[0:64, :], 0.0), W=(kidxB,))
            tk.dma("sp", kidxA[0:64, :], kidxTs[0:64, :], W=(kidxA,), owner=kidxA)
            tk.dma("sp", kidxB[64:128, :], kidxTs[64:128, :], W=(kidxB,), owner=kidxB)
            qi_ring = Ring([P.sb([128, 16, 128], BF16, dma=True) for _ in range(1)])
            qt_ring = Ring([P.sb([128, 16, 128], BF16, dma=True) for _ in range(1)])
            Dg_ring = Ring([P.sb([128, 32, 128], BF16) for _ in range(1)])
            sc_ring = Ring([P.sb([128, L], F32, dma=True) for _ in range(2)])
            work = P.sb([128, L], F32)
            nm_in_ring = Ring([P.sb([128, 512], BF16, dma=True) for _ in range(2)])
            oh_ring = Ring([P.sb([128, 128], F32) for _ in range(3)])
            dh_ring = Ring([P.sb([128, 128], F32) for _ in range(3)])
            nm = P.sb([128, L], BF16, dma=True)
            nmT = P.sb([128, 32, 128], BF16)
            m8_ring = Ring([P.sb([128, 8], F32) for _ in range(2)])
            thr_ring = Ring([P.sb([128, 1], F32) for _ in range(2)])
            R_ring = Ring([P.sb([128, 512], BF16) for _ in range(4)])
            kt_ring = Ring([P.sb([128, L], BF16, dma=True) for _ in range(2)])
            v_ring = Ring([P.sb([128, 32, 128], BF16, dma=True) for _ in range(2)])
            pt_ring = Ring([P.sb([128, 512], BF16) for _ in range(3)])
            Lps_ring = Ring([P.ps([128, 512], F32) for _ in range(4)])
            dps_ring = Ring([P.ps([128, 512], F32) for _ in range(1)])
            accs_ring = Ring([P.ps([128, 512], F32) for _ in range(1)])
            sps_ring = Lps_ring
            ops_ring = Ring([P.ps([128, 512], F32) for _ in range(1)])
            tps_ring = Ring([P.ps([128, 1024], BF16) for _ in range(1)])

            def scores(m):
                nb = 4 * (m + 1)
                qi = qi_ring.next()
                tk.dma("sp", qi[:], OWN[32:48, :, m * 128:(m + 1) * 128].rearrange("c p t -> p c t"), W=(qi,), owner=qi)
                Dg = Dg_ring.next()
                for h in range(32):
                    tk.op("pool", lambda e: e.tensor_scalar(out=Dg[:, h, :], in0=identb[:], scalar1=wi[:, m, h:h + 1],
                                                             scalar2=None, op0=ALU.mult), R=(identb, wi), W=(Dg,))
                sc = sc_ring.next()
                for sg in range(nb // 4):
                    acc = accs_ring.next()
                    pend = []
                    LOOK = 2
                    last = (sg == nb // 4 - 1)
                    for h in range(32 + LOOK):
                        if h < 32:
                            lp = Lps_ring.next()
                            kx = kidxA if h % 2 == 0 else kidxB
                            tk.mm(lp, [(lp[:], qi[:, h // 2, :], kx[:, sg * 512:(sg + 1) * 512],
                                        True, True, (qi, kx))])
                            Rt = R_ring.next()
                            tk.op("act", lambda e: e.activation(out=Rt[:], in_=lp[:], func=AF.Relu), R=(lp,), W=(Rt,))
                            pend.append((h, Rt))
                        if h >= LOOK:
                            hh, Rr = pend.pop(0)
                            tk.mm(acc, [(acc[:], Dg[:, hh, :], Rr[:], hh == 0, hh == 31 and not last, (Dg, Rr))],
                                  cont=(hh > 0))
                    if sg == nb // 4 - 1:
                        nmi = nm_in_ring.next()
                        tk.dma("pool", nmi[:], negmask_in[m], W=(nmi,), owner=nmi)
                        tk.mm(acc, [(acc[:], identb[:], nmi[:], False, True, (identb, nmi))], cont=True)
                    tk.op("act", lambda e: e.copy(out=sc[:, sg * 512:(sg + 1) * 512], in_=acc[:]), R=(acc,), W=(sc,))
                return sc

            def topk(m, sc):
                nb = 4 * (m + 1)
                S = nb * 128
                m8 = None
                for rnd in range(32):
                    m8 = m8_ring.next()
                    src = sc if rnd == 0 else work
                    tk.op("dve", lambda e: e.max(out=m8[:], in_=src[:, 0:S]), R=(src,), W=(m8,))
                    if rnd < 31:
                        if rnd == 0:
                            tk.op("dve", lambda e: e.match_replace(out=work[:, 0:S], in_to_replace=m8[:],
                                                                    in_values=src[:, 0:S], imm_value=NEG),
                                  R=(m8, src), W=(work,))
                        else:
                            tk.op("dve", lambda e: e.match_replace(out=work[:, 0:S], in_to_replace=m8[:],
                                                                    in_values=work[:, 0:S], imm_value=NEG),
                                  R=(m8, work), W=(work,))
                thr = thr_ring.next()
                tk.op("dve", lambda e: e.tensor_scalar(out=thr[:], in0=m8[:, 7:8], scalar1=-1e29, scalar2=None, op0=ALU.max),
                      R=(m8,), W=(thr,))
                tk.op("dve", lambda e: e.tensor_scalar(out=nm[:, 0:S], in0=sc[:, 0:S], scalar1=thr[:, 0:1], scalar2=-30000.0,
                                                        op0=ALU.is_lt, op1=ALU.mult), R=(sc, thr), W=(nm,))
                if DEBUG:
                    tk.dma("sp", scdbg[m, :, 0:S], sc[:, 0:S], R=(sc,), owner=sc)
                    tk.dma("sp", nmdbg[m, :, 0:S], nm[:, 0:S], R=(nm,), owner=nm)

            def attention_pre(m):
                nb = 4 * (m + 1)
                for b0 in range(0, nb, 8):
                    nbb = min(8, nb - b0)
                    tp = tps_ring.next()
                    tk.tr(tp, [(tp[:, j * 128:(j + 1) * 128], nm[:, (b0 + j) * 128:(b0 + j + 1) * 128], identb[:], (nm, identb))
                               for j in range(nbb)])
                    tk.op("act", lambda e: e.copy(out=nmT[:, b0:b0 + nbb, :],
                                                   in_=tp[:, 0:nbb * 128].rearrange("p (j t) -> p j t", j=nbb)),
                          R=(tp,), W=(nmT,))

            def attention(m):
                nb = 4 * (m + 1)
                S = nb * 128
                qt = qt_ring.next()
                tk.dma("sp", qt[:], OWN[16:32, :, m * 128:(m + 1) * 128].rearrange("h p t -> p h t"), W=(qt,), owner=qt)
                for h in range(16):
                    kt = kt_ring.next()
                    tk.dma("sp", kt[:, 0:S], KTs[h, :, 0:S], W=(kt,), owner=kt)
                    vt = v_ring.next()
                    tk.dma("sp", vt[:, 0:nb, :], Vs[h, :, 0:nb, :], W=(vt,), owner=vt)
                    ops = ops_ring.next()
                    dps = dps_ring.next()
                    prev = None
                    nsg = nb // 4
                    for sg in range(nsg + 1):
                        if sg < nsg:
                            sp = sps_ring.next()
                            mms = []
                            for j in range(4):
                                blk = sg * 4 + j
                                mms.append((sp[:, j * 128:(j + 1) * 128], kt[:, blk * 128:(blk + 1) * 128], qt[:, h, :],
                                            True, False, (kt, qt)))
                                mms.append((sp[:, j * 128:(j + 1) * 128], identb[:], nmT[:, blk, :], False, True,
                                            (identb, nmT)))
                            tk.mm(sp, mms)
                            pt = pt_ring.next()
                            tk.op("act", lambda e: e.activation(out=pt[:], in_=sp[:], func=AF.Exp, bias=negb[:, 0:1],
                                                                 scale=float(128 ** -0.5)), R=(sp, negb), W=(pt,))
                        if prev is not None:
                            sg2, pt2 = prev
                            mms = []
                            mmd = []
                            for j in range(4):
                                blk = sg2 * 4 + j
                                mms.append((ops[:, 0:128], vt[:, blk, :], pt2[:, j * 128:(j + 1) * 128],
                                            blk == 0, blk == nb - 1, (vt, pt2)))
                                mmd.append((dps[:, 0:128], onesb[:], pt2[:, j * 128:(j + 1) * 128],
                                            blk == 0, blk == nb - 1, (onesb, pt2)))
                            tk.mm(ops, mms, cont=(sg2 > 0))
                            tk.mm(dps, mmd, cont=(sg2 > 0))
                        prev = (sg, pt) if sg < nsg else None
                    oh = oh_ring.next()
                    dh = dh_ring.next()
                    tk.op("act", lambda e: e.copy(out=oh[:], in_=ops[:, 0:128]), R=(ops,), W=(oh,))
                    tk.op("act", lambda e: e.activation(out=dh[:], in_=dps[:, 0:128], func=AF.Ln), R=(dps,), W=(dh,))
                    tk.op("act", lambda e: e.activation(out=dh[:], in_=dh[:], func=AF.Exp, scale=-1.0), R=(dh,), W=(dh,))
                    tk.op("pool", lambda e: e.tensor_tensor(out=mix_att[:, h, m * 128:(m + 1) * 128], in0=oh[:],
                                                             in1=dh[:], op=ALU.mult), R=(oh, dh), W=(mix_att,))

            sc_list = {0: scores(0), 1: scores(1)}
            topk(0, sc_list[0])
            for m in range(NT):
                attention_pre(m)
                if m + 1 < NT:
                    topk(m + 1, sc_list[m + 1])
                if m + 2 < NT:
                    sc_list[m + 2] = scores(m + 2)
                attention(m)
        P.close()

        if DEBUG and STAGE >= 3:
            Dp = Pool_(tk, nc)
            own = Dp.sb([1, 1], F32, dma=True)
            tk.dma("sp", mixdbg[:, 0:16, :], mix_att[:], R=(mix_att,), owner=own)
            tk.dma("sp", mixdbg[:, 16:32, :], mix_gm[:], R=(mix_gm,), owner=own)
            Dp.close()

        P = Pool_(tk, nc)
        if STAGE >= 4:
            wslots = Ring([P.sb([128, 4096], BF16, dma=True) for _ in range(8)])
            xr_ring = Ring([P.sb([128, 512], F32, dma=True) for _ in range(3)])
            x1_ring = Ring([P.sb([128, 512], F32, dma=True) for _ in range(3)])
            acc_ring = Ring([P.ps([128, 512], F32) for _ in range(6)])
            for ncn in range(8):
                pcs = []
                for q in range(4):
                    ws = wslots.next()
                    tk.dma("pool", ws[:], wtm_out[ncn, q], W=(ws,), owner=ws)
                    pcs.append(ws)
                for j in range(NT):
                    xr = xr_ring.next()
                    tk.dma("sp", xr[:], x_own[j * 128:(j + 1) * 128, ncn * 512:(ncn + 1) * 512], W=(xr,), owner=xr)
                    acc = acc_ring.next()
                    mms = []
                    for kc in range(32):
                        src = mix_att if kc < 16 else mix_gm
                        mms.append((acc[:], src[:, kc % 16, j * 128:(j + 1) * 128],
                                    pcs[kc // 8][:, (kc % 8) * 512:(kc % 8 + 1) * 512], kc == 0, kc == 31, (pcs[kc // 8], src)))
                    tk.mm(acc, mms)
                    x1t = x1_ring.next()
                    tk.op("dve", lambda e: e.tensor_tensor(out=x1t[:], in0=acc[:], in1=xr[:], op=ALU.add),
                          R=(acc, xr), W=(x1t,))
                    tk.dma("sp", x1s[j * 128:(j + 1) * 128, ncn * 512:(ncn + 1) * 512], x1t[:], R=(x1t,), owner=x1t)
        P.close()
        M2.close()
        M1.close()

        for half in range(2 if STAGE >= 5 else 0):
            P = Pool_(tk, nc)
            AT = P.sb([128, NFC, 512], BF16)
            Pg = Pool_(tk, nc)
            h2T = Pg.sb([128, 32, 512], BF16)
            Pn = Pool_(tk, nc)
            gb = Pn.sb([128, D], F32, dma=True)
            tk.dma("sp", gb[:], gffn_in, W=(gb,), owner=gb)
            xs_ring = Ring([Pn.sb([128, D], F32, dma=True) for _ in range(2)])
            h1_ring = Ring([Pn.sb([128, D], BF16) for _ in range(1)])
            junk = Pn.sb([128, D], BF16)
            ss_ring = Ring([Pn.sb([128, 4], F32) for _ in range(4)])
            ptr_ring = Ring([Pn.ps([128, 1024], BF16) for _ in range(2)])
            rmsnorm_tiles([x1s[(half * 4 + tt) * 128:(half * 4 + tt + 1) * 128, :] for tt in range(4)],
                          gb, xs_ring, h1_ring, junk, ptr_ring, h2T, ss_ring)
            Pn.close()
            wslots = Ring([Pg.sb([128, 4096], BF16, dma=True) for _ in range(6)])
            sg_ring = Ring([Pg.sb([128, 512], F32) for _ in range(2)])
            acc_ring = Ring([Pg.ps([128, 512], F32) for _ in range(6)])
            for fc in range(NFC):
                wg = wslots.next()
                tk.dma("pool", wg[:], wfm_g[fc], W=(wg,), owner=wg)
                wu = wslots.next()
                tk.dma("pool", wu[:], wfm_u[fc], W=(wu,), owner=wu)
                ag = acc_ring.next()
                tk.mm(ag, [(ag[:], wg[:, kc * 128:(kc + 1) * 128], h2T[:, kc, :], kc == 0, kc == 31, (wg, h2T)) for kc in range(32)])
                au = acc_ring.next()
                tk.mm(au, [(au[:], wu[:, kc * 128:(kc + 1) * 128], h2T[:, kc, :], kc == 0, kc == 31, (wu, h2T)) for kc in range(32)])
                sgt = sg_ring.next()
                tk.op("act", lambda e: e.activation(out=sgt[:], in_=ag[:], func=AF.Silu), R=(ag,), W=(sgt,))
                tk.op("dve", lambda e: e.tensor_tensor(out=AT[:, fc, :], in0=au[:], in1=sgt[:], op=ALU.mult),
                      R=(au, sgt), W=(AT,))
            Pg.close()
            Pd = Pool_(tk, nc)
            x2 = Pd.sb([128, 4, D], F32, dma=True)
            r0 = half * 512
            for tt in range(4):
                tk.dma("sp", x2[:, tt, :], x1s[r0 + tt * 128:r0 + (tt + 1) * 128, :], W=(x2,), owner=x2)
            wdslots = Ring([Pd.sb([128, 43 * 128], BF16, dma=True) for _ in range(4)])
            ys_ring = Ring([Pd.sb([128, 512], F32) for _ in range(2)])
            gfq_ring = Ring([Pd.sb([128, 1024], F32, dma=True) for _ in range(1)])
            yacc_ring = Ring([Pd.ps([128, 512], F32) for _ in range(4)])
            ytr_ring = Ring([Pd.ps([128, 512], F32) for _ in range(3)])
            ss_ring = Ring([Pd.sb([128, 4], F32) for _ in range(4)])
            for ncn in range(32):
                w0 = wdslots.next()
                tk.dma("pool", w0[:], wd_in[ncn, 0], W=(w0,), owner=w0)
                w1 = wdslots.next()
                tk.dma("pool", w1[:], wd_in[ncn, 1], W=(w1,), owner=w1)
                ya = yacc_ring.next()
                mms = []
                for fc in range(NFC):
                    wsrc = w0 if fc < 43 else w1
                    mms.append((ya[:], wsrc[:, (fc % 43) * 128:(fc % 43 + 1) * 128], AT[:, fc, :], fc == 0, fc == NFC - 1,
                                (wsrc, AT)))
                tk.mm(ya, mms)
                ys = ys_ring.next()
                tk.op("act", lambda e: e.copy(out=ys[:], in_=ya[:]), R=(ya,), W=(ys,))
                yt = ytr_ring.next()
                tk.tr(yt, [(yt[:, tt * 128:(tt + 1) * 128], ys[:, tt * 128:(tt + 1) * 128], identf[:], (ys, identf))
                           for tt in range(4)])
                tk.op("dve", lambda e: e.tensor_tensor(out=x2[:, :, ncn * 128:(ncn + 1) * 128],
                                                        in0=yt[:].rearrange("p (t n) -> p t n", t=4),
                                                        in1=x2[:, :, ncn * 128:(ncn + 1) * 128], op=ALU.add),
                      R=(yt, x2), W=(x2,))
            for tt in range(4):
                ss = ss_ring.next()
                tk.op("act", lambda e: e.activation(out=AT[:, 0:8, :].rearrange("p a b -> p (a b)"), in_=x2[:, tt, :],
                                                     func=AF.Square, scale=1.0 / 64.0, accum_out=ss[:, 0:1]),
                      R=(x2,), W=(AT, ss))
                tk.op("act", lambda e: e.activation(out=ss[:, 1:2], in_=ss[:, 0:1], func=AF.Sqrt, bias=epsT[:, 0:1]),
                      R=(ss, epsT), W=(ss,))
                tk.op("dve", lambda e: e.reciprocal(out=ss[:, 2:3], in_=ss[:, 1:2]), R=(ss,), W=(ss,))
                for q in range(4):
                    gfq = gfq_ring.next()
                    tk.dma("sp", gfq[:], gfin_in[:, q * 1024:(q + 1) * 1024], W=(gfq,), owner=gfq)
                    tk.op("dve", lambda e: e.scalar_tensor_tensor(out=x2[:, tt, q * 1024:(q + 1) * 1024],
                                                                   in0=x2[:, tt, q * 1024:(q + 1) * 1024], scalar=ss[:, 2:3],
                                                                   in1=gfq[:], op0=ALU.mult, op1=ALU.mult),
                          R=(x2, ss, gfq), W=(x2,))
            for tt in range(4):
                tk.dma("sp", out[r0 + tt * 128:r0 + (tt + 1) * 128, :], x2[:, tt, :], R=(x2,), owner=x2)
            Pd.close()
            P.close()
        G.close()
        tk.barrier()
    return nc


def _fm(Wc):
    n = Wc.shape[1]
    a = Wc.reshape(32, 128, n // 128, 128).transpose(2, 1, 0, 3)
    return np.ascontiguousarray(a).reshape(n // 128, 128, 32 * 128)


def _tm(Wc):
    n = Wc.shape[1]
    a = Wc.reshape(4, 8, 128, n // 512, 512).transpose(3, 0, 2, 1, 4)
    return np.ascontiguousarray(a).reshape(n // 512, 4, 128, 8 * 512)


def own_tiles(r):
    return [8 * (m // 2) + (r if m % 2 == 0 else 7 - r) for m in range(NT)]


_CACHE = {}


def kernel(x, norm_mix, w_in, gmlp_v_gain, w_spatial, b_spatial, w_out, norm_ffn, w_gate, w_up, w_down, norm_final):
    x = np.asarray(x, dtype=np.float32)
    w_in = np.asarray(w_in, dtype=np.float32)[0]
    w_out = np.asarray(w_out, dtype=np.float32)[0]
    w_gate = np.asarray(w_gate, dtype=np.float32)[0]
    w_up = np.asarray(w_up, dtype=np.float32)[0]
    w_down = np.asarray(w_down, dtype=np.float32)[0]
    f32 = np.float32
    shared = {}
    kcols = w_in[:, 2048:4096]
    kidxc = w_in[:, 8192:8256]
    shared["wfm_kv"] = np.concatenate([_fm(kcols), _fm(np.concatenate([kidxc, kidxc], axis=1))], axis=0)
    shared["wfm_own"] = np.concatenate([_fm(w_in[:, 8288:10336]), _fm(w_in[:, 0:2048]), _fm(w_in[:, 6144:8192])], axis=0)
    shared["wtm_v"] = _tm(w_in[:, 4096:6144])
    shared["wtm_vg"] = _tm(w_in[:, 10336:12384])
    shared["wtm_wi"] = np.ascontiguousarray(w_in[:, 8256:8288].reshape(32, 128, 32).transpose(1, 0, 2)).reshape(128, 32 * 32)
    shared["wtm_out"] = _tm(w_out)
    shared["wfm_g"] = _fm(w_gate)
    shared["wfm_u"] = _fm(w_up)
    wd = w_down.reshape(2, 43, 128, 32, 128).transpose(3, 0, 2, 1, 4)
    shared["wd"] = np.ascontiguousarray(wd).reshape(32, 2, 128, 43 * 128)
    shared["ident"] = np.eye(128, dtype=f32)
    shared["tril_st"] = np.triu(np.ones((128, 128), dtype=f32))
    shared["gmix_b"] = np.ascontiguousarray(np.broadcast_to(np.asarray(norm_mix, f32)[0][None, :], (128, D)))
    shared["gffn_b"] = np.ascontiguousarray(np.broadcast_to(np.asarray(norm_ffn, f32)[0][None, :], (128, D)))
    shared["gfin_b"] = np.ascontiguousarray(np.broadcast_to(np.asarray(norm_final, f32)[None, :], (128, D)))
    shared["vgain_b"] = np.ascontiguousarray(np.broadcast_to(np.asarray(gmlp_v_gain, f32)[0].reshape(1, 2048), (128, 2048)))
    shared["bsb"] = np.ascontiguousarray(np.broadcast_to(np.asarray(b_spatial, f32)[0].reshape(1, 2048), (128, 2048)))
    shared["wsT"] = np.ascontiguousarray(np.asarray(w_spatial, f32)[0].transpose(2, 0, 1)).reshape(128, 2048)

    in_maps = []
    for c in range(8):
        b, r = c // 4, c % 4
        tiles = own_tiles(r)
        xb = x[b]
        m = dict(shared)
        m["x_all"] = xb
        m["x_own"] = np.ascontiguousarray(np.concatenate([xb[t * 128:(t + 1) * 128] for t in tiles], axis=0))
        nmk = np.zeros((NT, 128, 512), dtype=f32)
        for mi, t in enumerate(tiles):
            nb = 4 * (mi + 1)
            qpos = t * 128 + np.arange(128)[:, None]
            kpos = (nb - 4) * 128 + np.arange(512)[None, :]
            nmk[mi] = np.where(kpos <= qpos, 0.0, NEG).astype(f32)
        m["negmask"] = nmk
        in_maps.append(m)

    if "nc" not in _CACHE:
        _CACHE["nc"] = build_program()
    nc = _CACHE["nc"]
    ncores = int(os.environ.get("K_NCORES", "8"))
    if ncores < 8:
        res = run_bass_kernel_spmd(nc, in_maps[:ncores], core_ids=list(range(ncores)))
        _CACHE["res"] = res
        return None
    res = run_bass_kernel_spmd(nc, in_maps, core_ids=list(range(8)))
    outp = np.zeros((2, L, D), dtype=np.float32)
    for c in range(8):
        b, r = c // 4, c % 4
        o = res.results[c]["out"]
        for mi, t in enumerate(own_tiles(r)):
            outp[b, t * 128:(t + 1) * 128] = o[mi * 128:(mi + 1) * 128]
    if DEBUG:
        _CACHE["res"] = res
    return outp
```
